# Optimizing a Trainium2 kernel written in Bass

```python
import math
import jax, jax.numpy as jnp
from jax import lax
import numpy as np

D_MODEL = 1024
BATCH = 4
SEQ = 4096
DEPTH = 4
DEC_BATCH = 32
DEC_SEQ = 1
PAST_LEN = 8192
PAGE_SIZE = 128

N_A_LAYERS = DEPTH // 2
N_B_LAYERS = DEPTH - N_A_LAYERS
POOL_WINDOWS = (2, 4, 8, 16)
N_POOL_GROUPS = len(POOL_WINDOWS)
POOL_GROUP = D_MODEL // N_POOL_GROUPS
POOL_STATE = max(POOL_WINDOWS) - 1
BRANCHES = ((128, 1), (512, 4), (2048, 16))
N_BRANCH = len(BRANCHES)
HEADS = 8
HEAD_DIM = D_MODEL // HEADS
ATT_WIDTH = HEADS * HEAD_DIM
D_FF = 4 * D_MODEL
NUM_BUCKETS = 32
MAX_DISTANCE = 2048
Q_BLOCK = 128
EPS = 1e-6

kernel_name = "yoco_pool_dilated_attn_step"


def rmsnorm(x, g):
    xf = x.astype(jnp.float32)
    y = xf * lax.rsqrt(jnp.mean(xf * xf, axis=-1, keepdims=True) + EPS)
    return (y * g.astype(jnp.float32)).astype(x.dtype)


def t5_bucket(dist):
    max_exact = NUM_BUCKETS // 2
    df = jnp.maximum(dist, 1).astype(jnp.float32)
    large = max_exact + (jnp.log(df / max_exact) / math.log(MAX_DISTANCE / max_exact)
                         * (NUM_BUCKETS - max_exact)).astype(jnp.int32)
    large = jnp.minimum(large, NUM_BUCKETS - 1)
    return jnp.where(dist < max_exact, dist, large)


def branch_biases(rel_bias):
    out = []
    for g, (w, d) in enumerate(BRANCHES):
        dist = jnp.arange(w // d + 1, dtype=jnp.int32) * d
        out.append(rel_bias[t5_bucket(dist)][:, g * HEADS:(g + 1) * HEADS])
    return out


def pool_mix(h, buf, pos0, w_pool, scale):
    B, T, D = h.shape
    xp = jnp.concatenate([buf.astype(h.dtype), h], axis=1).astype(jnp.float32)
    cs = jnp.concatenate([jnp.zeros_like(xp[:, :1]), jnp.cumsum(xp, axis=1)], axis=1)
    pos = pos0 + jnp.arange(T)
    hi = cs[:, POOL_STATE + 1:]
    parts = []
    for g, w in enumerate(POOL_WINDOWS):
        sl = slice(g * POOL_GROUP, (g + 1) * POOL_GROUP)
        lo = cs[:, POOL_STATE + 1 - w:POOL_STATE + 1 - w + T, sl]
        cnt = jnp.minimum(pos + 1, w).astype(jnp.float32)[None, :, None]
        parts.append((hi[..., sl] - lo) / cnt - xp[:, POOL_STATE:, sl])
    pooled = jnp.stack(parts, axis=2).astype(h.dtype)
    y = jnp.einsum('btgc,gce->btge', pooled, w_pool).reshape(B, T, D) * scale
    new_buf = xp[:, -POOL_STATE:].astype(h.dtype)
    return y, new_buf


def branch_attend(q, k_ctx, v_ctx, q_idx, dil, n_keys, bias):
    idx = q_idx[:, None] - dil * jnp.arange(n_keys)[None, :]
    valid = idx >= 0
    idx = jnp.maximum(idx, 0)
    k_g = k_ctx[:, idx]
    v_g = v_ctx[:, idx]
    s = jnp.einsum('bqhd,bqkhd->bqhk', q, k_g).astype(jnp.float32) * (HEAD_DIM ** -0.5)
    s = s + bias.T.astype(jnp.float32)[None, None]
    s = jnp.where(valid[None, :, None, :], s, -jnp.inf)
    lse = jax.nn.logsumexp(s, axis=-1)
    p = jnp.exp(s - lse[..., None])
    o = jnp.einsum('bqhk,bqkhd->bqhd', p.astype(v_g.dtype), v_g)
    return o, lse


def dilated_mix(q3, ctxs, q_idxs, biases):
    outs, lses = [], []
    for g, (w, d) in enumerate(BRANCHES):
        o, l = branch_attend(q3[:, :, g], ctxs[g][0], ctxs[g][1], q_idxs[g], d, w // d + 1, biases[g])
        outs.append(o)
        lses.append(l)
    wts = jax.nn.softmax(jnp.stack(lses, axis=0), axis=0)
    return jnp.einsum('gbqh,gbqhd->bqhd', wts.astype(outs[0].dtype), jnp.stack(outs, axis=0))


def dilated_attention(q3, ctxs, offsets, biases):
    B, T = q3.shape[:2]
    if T >= Q_BLOCK and T % Q_BLOCK == 0:
        nb = T // Q_BLOCK
        qs = q3.reshape((B, nb, Q_BLOCK) + q3.shape[2:]).swapaxes(0, 1)

        def block(args):
            qb, start = args
            idxs = [off + start + jnp.arange(Q_BLOCK) for off in offsets]
            return dilated_mix(qb, ctxs, idxs, biases)

        out = lax.map(block, (qs, jnp.arange(nb) * Q_BLOCK))
        return out.swapaxes(0, 1).reshape(B, T, HEADS, HEAD_DIM)
    idxs = [off + jnp.arange(T) for off in offsets]
    return dilated_mix(q3, ctxs, idxs, biases)


def trunk(x, pool_state, kv_caches, pos0, norm_mix, pool_w, pool_scale, norm_mlp, mlp_in, mlp_out,
          norm_kv, w_kv, w_q, w_o, biases, norm_final):
    B, T, D = x.shape
    new_pool, new_kv = [], []
    ctxs, offsets = [], []
    for l in range(DEPTH):
        h = rmsnorm(x, norm_mix[l])
        if l < N_A_LAYERS:
            y, nbuf = pool_mix(h, pool_state[l], pos0, pool_w[l], pool_scale[l])
            new_pool.append(nbuf)
        else:
            if l == N_A_LAYERS:
                kv = (rmsnorm(x, norm_kv) @ w_kv).reshape(B, T, N_BRANCH, 2, HEADS, HEAD_DIM)
                for g, (w, d) in enumerate(BRANCHES):
                    cache = kv_caches[g].astype(kv.dtype)
                    full = jnp.concatenate([cache, kv[:, :, g]], axis=1)
                    ctxs.append((full[:, :, 0], full[:, :, 1]))
                    offsets.append(cache.shape[1])
                    new_kv.append(kv[:, max(T - w, 0):, g])
            lb = l - N_A_LAYERS
            q3 = (h @ w_q[lb]).reshape(B, T, N_BRANCH, HEADS, HEAD_DIM)
            o = dilated_attention(q3, ctxs, offsets, biases)
            y = o.reshape(B, T, ATT_WIDTH) @ w_o[lb]
        x = x + y
        h = rmsnorm(x, norm_mlp[l])
        x = x + jnp.square(jax.nn.relu(h @ mlp_in[l])) @ mlp_out[l]
    return rmsnorm(x, norm_final), jnp.stack(new_pool, axis=0), new_kv


def setup_inputs(seed: int = 0) -> dict:
    key = jax.random.key(seed)
    ks = jax.random.split(key, 24)
    f32 = jnp.float32
    nrm = lambda k, shape: jax.random.normal(k, shape, f32)
    win = [min(w, PAST_LEN) for (w, d) in BRANCHES]
    return {
        "x_prompt": nrm(ks[0], (BATCH, SEQ, D_MODEL)),
        "x_sample": nrm(ks[1], (DEC_BATCH, DEC_SEQ, D_MODEL)),
        "state_pool": nrm(ks[2], (N_A_LAYERS, DEC_BATCH, POOL_STATE, D_MODEL)),
        "cache_kv_w128": nrm(ks[3], (DEC_BATCH, win[0], 2, HEADS, HEAD_DIM)),
        "cache_kv_w512": nrm(ks[4], (DEC_BATCH, win[1], 2, HEADS, HEAD_DIM)),
        "cache_kv_w2048": nrm(ks[5], (DEC_BATCH, win[2], 2, HEADS, HEAD_DIM)),
        "norm_mix": 1.0 + 0.05 * nrm(ks[6], (DEPTH, D_MODEL)),
        "pool_w": nrm(ks[7], (N_A_LAYERS, N_POOL_GROUPS, POOL_GROUP, POOL_GROUP)) * POOL_GROUP ** -0.5,
        "pool_scale": 1.0 + 0.05 * nrm(ks[8], (N_A_LAYERS, D_MODEL)),
        "norm_mlp": 1.0 + 0.05 * nrm(ks[9], (DEPTH, D_MODEL)),
        "mlp_in": nrm(ks[10], (DEPTH, D_MODEL, D_FF)) * D_MODEL ** -0.5,
        "mlp_out": nrm(ks[11], (DEPTH, D_FF, D_MODEL)) * D_FF ** -0.5,
        "norm_kv": 1.0 + 0.05 * nrm(ks[12], (D_MODEL,)),
        "w_kv": nrm(ks[13], (D_MODEL, N_BRANCH * 2 * ATT_WIDTH)) * D_MODEL ** -0.5,
        "w_q": nrm(ks[14], (N_B_LAYERS, D_MODEL, N_BRANCH * ATT_WIDTH)) * D_MODEL ** -0.5,
        "w_o": nrm(ks[15], (N_B_LAYERS, ATT_WIDTH, D_MODEL)) * ATT_WIDTH ** -0.5,
        "rel_bias": 0.2 * nrm(ks[16], (NUM_BUCKETS, N_BRANCH * HEADS)),
        "norm_final": 1.0 + 0.05 * nrm(ks[17], (D_MODEL,)),
    }


def reference(x_prompt, x_sample, state_pool, cache_kv_w128, cache_kv_w512, cache_kv_w2048,
              norm_mix, pool_w, pool_scale, norm_mlp, mlp_in, mlp_out, norm_kv, w_kv, w_q, w_o,
              rel_bias, norm_final):
    biases = branch_biases(rel_bias)
    weights = (norm_mix, pool_w, pool_scale, norm_mlp, mlp_in, mlp_out, norm_kv, w_kv, w_q, w_o, biases, norm_final)
    Bp = x_prompt.shape[0]
    pool0 = jnp.zeros((N_A_LAYERS, Bp, POOL_STATE, D_MODEL), x_prompt.dtype)
    empty = [jnp.zeros((Bp, 0, 2, HEADS, HEAD_DIM), x_prompt.dtype) for _ in BRANCHES]
    y_prompt, pool_p, kv_p = trunk(x_prompt, pool0, empty, 0, *weights)
    y_sample, pool_s, kv_s = trunk(x_sample, state_pool, [cache_kv_w128, cache_kv_w512, cache_kv_w2048],
                                   PAST_LEN, *weights)
    return (y_prompt, y_sample, pool_p, pool_s, kv_p[0], kv_s[0], kv_p[1], kv_s[1], kv_p[2], kv_s[2])
```

```python
import math
from contextlib import ExitStack

import numpy as np
import concourse.bass as bass
import concourse.mybir as mybir
from concourse.bass_utils import run_bass_kernel_spmd

F32 = mybir.dt.float32
BF16 = mybir.dt.bfloat16
AF = mybir.ActivationFunctionType
ALU = mybir.AluOpType
AX = mybir.AxisListType

D = 1024
NT = 16
TS = 16
TH = 17
NS = 4
HID = 4096
EPS = 1e-6
BR = ((128, 1), (512, 4), (2048, 16))
CH = 16
C0 = CH + 128
CS = C0 + 2048
CW = CS + NS
SCALE = 128 ** -0.5


def TT(out, a, b, op):
    return lambda e: e.tensor_tensor(out, a, b, op)


def STT(out, a, s, b, op0, op1):
    return lambda e: e.scalar_tensor_tensor(out, a, s, b, op0, op1)


def TSC(out, a, s1, op0):
    return lambda e: e.tensor_scalar(out, a, s1, None, op0)


def TSC2(out, a, s1, s2, op0, op1):
    return lambda e: e.tensor_scalar(out, a, s1, s2, op0, op1)


def ACTI(out, in_, func, **kw):
    return lambda e: e.activation(out, in_, func, **kw)


def DMA(out, in_, **kw):
    return lambda e: e.dma_start(out=out, in_=in_, **kw)


def MM(items):
    def f(e):
        ins = None
        for (o, l, r, st, sp) in items:
            ins = e.matmul(o, l, r, start=st, stop=sp)
        return ins
    return f


def MMX(items):
    def f(e):
        ins = None
        for (o, l, r, st, sp) in items:
            ins = e.matmul(o, l, r, start=st, stop=sp, skip_group_check=True)
        return ins
    return f


def TRS(items, ident):
    def f(e):
        ins = None
        for (o, i) in items:
            ins = e.transpose(o, i, ident)
        return ins
    return f


def RECIP(out, in_):
    return lambda e: e.reciprocal(out, in_)


def MEMSET(ap, v):
    return lambda e: e.memset(ap, v)


def COPY(out, in_):
    return lambda e: e.tensor_copy(out, in_)


def REDUCE(out, in_, axis, op):
    return lambda e: e.tensor_reduce(out, in_, axis, op)


def CC(groups, src, dst):
    return lambda e: e.collective_compute("AllGather", ALU.bypass, replica_groups=groups, ins=[src], outs=[dst])


class Op:
    __slots__ = ("eng", "fn", "deps", "need", "is_dma", "semkey", "sig", "idx", "inc")


class Prog:
    ENGS = ("pe", "act", "dve", "pool", "sp")

    def __init__(self):
        self.ops = []
        self.lastw = {}
        self.readers = {}
        self.pending_barrier = {e: [] for e in self.ENGS}
        self.last_on = {}

    def add(self, eng, fn, reads=(), writes=(), dma=None, cc=False):
        op = Op()
        op.eng = eng
        op.fn = fn
        op.is_dma = dma is not None
        op.semkey = dma
        op.need = op.is_dma
        op.sig = None
        op.inc = 1 if (cc or dma is None) else 16
        deps = set()
        lw = self.lastw
        rd = self.readers
        for k in reads:
            w = lw.get(k)
            if w is not None:
                deps.add(w)
        for k in writes:
            w = lw.get(k)
            if w is not None:
                deps.add(w)
            r = rd.get(k)
            if r:
                deps.update(r)
        if eng == "pe" and not op.is_dma:
            deps = {d for d in deps if d.is_dma or d.eng != "pe"}
        pb = self.pending_barrier[eng]
        if pb:
            deps.update(pb)
            self.pending_barrier[eng] = []
        for d in deps:
            d.need = True
        op.deps = deps
        for k in reads:
            rd.setdefault(k, []).append(op)
        for k in writes:
            lw[k] = op
            rd[k] = []
        op.idx = len(self.ops)
        self.ops.append(op)
        self.last_on[(eng, op.is_dma, dma)] = op
        return op

    def barrier(self):
        lasts = list(self.last_on.values())
        for o in lasts:
            o.need = True
        for e in self.ENGS:
            self.pending_barrier[e] = list(lasts)
        self.lastw = {}
        self.readers = {}

    def emit(self, nc, es):
        SEM_ROT = 12000
        DMA_ROT = 700
        comp_count = {e: 0 for e in self.ENGS}
        sem_pool = {}

        def get_sem(name):
            s = sem_pool.get(name)
            if s is None:
                s = es.enter_context(nc.semaphore(name))
                sem_pool[name] = s
            return s

        totals = {}
        for op in self.ops:
            if op.is_dma:
                totals[op.semkey] = totals.get(op.semkey, 0) + 1
        dma_cnt = {}
        for op in self.ops:
            if op.is_dma:
                c = dma_cnt.get(op.semkey, 0) + 1
                dma_cnt[op.semkey] = c
                if op.semkey.startswith("@"):
                    assert totals[op.semkey] <= DMA_ROT
                    op.sig = (get_sem("dg_" + op.semkey[1:]), op.inc * totals[op.semkey])
                else:
                    gen = (c - 1) // DMA_ROT
                    op.sig = (get_sem("d_%s_%d" % (op.semkey, gen)), op.inc * (c - gen * DMA_ROT))
            elif op.need:
                c = comp_count[op.eng] + 1
                comp_count[op.eng] = c
                gen = (c - 1) // SEM_ROT
                op.sig = (get_sem("c_%s_%d" % (op.eng, gen)), c - gen * SEM_ROT)
        self.n_sems = len(sem_pool)
        final_dma = {}
        for op in self.ops:
            if op.is_dma:
                k = id(op.sig[0])
                if k not in final_dma or final_dma[k][1] < op.sig[1]:
                    final_dma[k] = op.sig
        per_eng = {e: [] for e in self.ENGS}
        for op in self.ops:
            per_eng[op.eng].append(op)
        self.n_waits = 0
        block = es.enter_context(nc.Block())

        def run(engname, e):
            waited = {}
            for op in per_eng[engname]:
                need = {}
                for d in op.deps:
                    s, v = d.sig
                    k = id(s)
                    if waited.get(k, 0) >= v:
                        continue
                    if k not in need or need[k][1] < v:
                        need[k] = (s, v)
                for k, (s, v) in need.items():
                    e.wait_ge(s, v)
                    waited[k] = v
                    self.n_waits += 1
                ins = op.fn(e)
                if op.sig is not None:
                    if op.is_dma and op.inc == 1:
                        ins.then_inc(op.sig[0])
                    else:
                        ins.then_inc(op.sig[0], op.inc)
            if engname == "sp":
                for k, (s, v) in final_dma.items():
                    if waited.get(k, 0) < v:
                        e.wait_ge(s, v)

        @block.tensor
        def _(e):
            run("pe", e)

        @block.scalar
        def _(e):
            run("act", e)

        @block.vector
        def _(e):
            run("dve", e)

        @block.gpsimd
        def _(e):
            run("pool", e)

        @block.sync
        def _(e):
            run("sp", e)


def t5_bucket_np(dist):
    dist = np.asarray(dist, np.int64)
    max_exact = 16
    df = np.maximum(dist, 1).astype(np.float32)
    large = max_exact + (np.log(df / np.float32(max_exact)) / np.float32(math.log(2048 / max_exact))
                         * np.float32(32 - max_exact)).astype(np.int32)
    large = np.minimum(large, 31)
    return np.where(dist < max_exact, dist, large).astype(np.int64)


def toeplitz_index():
    k = np.arange(128)[:, None]
    c = np.arange(256)[None, :]
    j = np.where(c < 128, 128 + c - k, c - 128 - k)
    valid = np.where(c < 128, k >= c, (c - 128) >= k)
    j = np.where(valid, j, 0)
    return j, valid.astype(np.float32)


def build_program(stop_after=None, debug=False):
    nc = bass.Bass("TRN2", target_bir_lowering=False)
    P = Prog()

    def din(name, shape, dt=F32):
        return nc.dram_tensor(name, list(shape), dt, kind="ExternalInput").ap()

    def dout(name, shape, dt=F32):
        return nc.dram_tensor(name, list(shape), dt, kind="ExternalOutput").ap()

    def dscr(name, shape, dt=BF16):
        return nc.dram_tensor(name, list(shape), dt).ap()

    xin = din("xin", [18, 128, D])
    state = din("state", [2, 60, D])
    cache = [din("cache0", [NS, 128, 2048]), din("cache1", [NS, 512, 2048]), din("cache2", [NS, 2048, 2048])]
    gvec = din("gvec", [12, D])
    pool_w = din("pool_w", [2, 4, 256, 256])
    mlp_in = din("mlp_in", [4, D, HID])
    mlp_out = din("mlp_out", [4, HID, D])
    w_kv = din("w_kv", [D, 6144])
    w_q = din("w_q", [2, D, 3072])
    w_o = din("w_o", [2, D, D])
    c_ident = din("c_ident", [128, 128])
    c_apool = din("c_apool", [2, 4, 2, 128, 128])
    c_selh = din("c_selh", [NS, 16])
    c_sel = din("c_sel", [60, 16])
    c_bias3 = din("c_bias3", [24, 128, 768])
    c_biasS = din("c_biasS", [3, 128, 8])
    c_bias0 = din("c_bias0", [3, 8])
    y_p = dout("y_p", [NT, 128, D])
    y_s = dout("y_s", [NS, D])
    pool_p = dout("pool_p", [2, 15, D])
    pool_s = dout("pool_s", [2, NS, 15, D])
    kvp = [dout("kvp0", [128, 2048]), dout("kvp1", [512, 2048]), dout("kvp2", [2048, 2048])]
    kvs = [dout("kvs%d" % g, [NS, 2048]) for g in range(3)]
    dbg = dout("dbg", [18, 128, D]) if debug else None
    kT_d = dscr("kT_d", [24, 128, 2048])
    v_d = dscr("v_d", [24, 128, 2048])
    exp_src0 = dscr("exp_src0", [1024, 256])
    exp_dst0 = dscr("exp_dst0", [2048, 256])
    exp_src1 = dscr("exp_src1", [1024, 1024])
    exp_dst1 = dscr("exp_dst1", [2048, 1024])
    exp_src2 = [dscr("exp_src2_%d" % i, [256, 4096]) for i in range(4)]
    exp_dst2 = [dscr("exp_dst2_%d" % i, [512, 4096]) for i in range(4)]
    ot_d = dscr("ot_d", [8, 128, 2048])

    es = ExitStack()
    ARENA_BYTES = 212000
    arena = es.enter_context(nc.sbuf_tensor("arena", [128, ARENA_BYTES // 2], BF16))
    pos = [0]

    def alloc(nbytes):
        off = (pos[0] + 63) // 64 * 64
        pos[0] = off + nbytes
        assert pos[0] <= ARENA_BYTES, ("arena overflow", pos[0])
        return off

    def vbf(off, n):
        return arena[:, off // 2: off // 2 + n]

    def vf32(off, n):
        return arena[:, off // 2: off // 2 + 2 * n].bitcast(F32)

    X = vf32(alloc(18 * D * 4), 18 * D).rearrange("p (t d) -> p t d", t=18)
    HT = vbf(alloc(8 * CW * 2), 8 * CW).rearrange("p (k c) -> p k c", k=8)
    WALL = vbf(alloc(4 * 8192), 4 * 4096)
    WSL = [WALL[:, i * 4096:(i + 1) * 4096] for i in range(4)]
    IDENT = vbf(alloc(256), 128)
    ONESB = vbf(alloc(256), 128)
    ONES32 = vf32(alloc(512), 128)
    GCOL = vf32(alloc(12 * 8 * 4), 96).rearrange("p (v k) -> p v k", v=12)
    SS = vf32(alloc(32 * 4), 32)
    RS = vf32(alloc(32 * 4), 32)
    FLAG = vf32(alloc(64), 2)
    TBL = vf32(alloc(4096), 1024)
    HTOKALL = vbf(alloc(4096), 2048)
    HTOK = [HTOKALL[:, 0:1024], HTOKALL[:, 1024:2048]]
    YALL = vf32(alloc(8192), 2048)
    YST = [YALL[:, 0:1024], YALL[:, 1024:2048]]
    PH0 = alloc(0)

    def phase_alloc():
        st = [PH0]

        def a(nbytes):
            off = (st[0] + 63) // 64 * 64
            st[0] = off + nbytes
            assert st[0] <= ARENA_BYTES, ("phase overflow", st[0] - PH0, ARENA_BYTES - PH0)
            return off
        return a

    PSALL = es.enter_context(nc.psum_tensor("psall", [128, 4096], F32))
    PS = [PSALL[:, i * 512:(i + 1) * 512] for i in range(8)]

    def psb(b):
        return PS[b][:, :].bitcast(BF16)

    P.add("pool", DMA(IDENT, c_ident), writes=[("ident",)], dma="ident")
    P.add("sp", DMA(GCOL, gvec.rearrange("v (k p) -> p v k", p=128), allow_slow_non_contiguous=True),
          writes=[("gcol",)], dma="@setup2")
    P.add("dve", MEMSET(ONESB, 1.0), writes=[("onesb",)])
    P.add("dve", MEMSET(ONES32, 1.0), writes=[("ones32",)])
    P.add("dve", MEMSET(HT[:, :, 0:CH], 0.0), writes=[("hT", "Z")])
    def load_x(tiles, extra_reads=()):
        for t in tiles:
            P.add("sp", DMA(X[:, t, :], xin[t]), reads=list(extra_reads), writes=[("x", t)], dma="xin%d" % t)

    load_x([TH, 0, 1])

    def tile_cols(t):
        if t == TH:
            return CH, 128
        if t == TS:
            return CS, NS
        return C0 + 128 * t, 128

    def hkey(t):
        return ("hT", t)

    tr_ctr = [0]

    def norm_a(t, gidx, h32_tbl=None, h32_slot=0):
        buf = tr_ctr[0] % 2
        tr_ctr[0] += 1
        htok = HTOK[buf]
        bank = 6 + buf
        c, n = tile_cols(t)
        P.add("act", ACTI(htok, X[:, t, :], AF.Square, accum_out=SS[:, t:t + 1]),
              reads=[("x", t)], writes=[("htok", buf), ("ss", t)])
        P.add("act", ACTI(RS[:, t:t + 1], SS[:, t:t + 1], AF.Sqrt, bias=EPS, scale=1.0 / D),
              reads=[("ss", t)], writes=[("rs", t)])
        P.add("dve", RECIP(RS[:, t:t + 1], RS[:, t:t + 1]), reads=[("rs", t)], writes=[("rs", t)])
        P.add("dve", TSC(htok, X[:, t, :], RS[:, t:t + 1], ALU.mult),
              reads=[("x", t), ("rs", t)], writes=[("htok", buf)])
        if h32_tbl is not None:
            P.add("dve", STT(YST[h32_slot], X[:, t, :], RS[:, t:t + 1], h32_tbl, ALU.mult, ALU.mult),
                  reads=[("x", t), ("rs", t), ("gtbl",)], writes=[("yst", h32_slot)])
        pv = psb(bank)
        P.add("pe", TRS([(pv[:, k * 128:(k + 1) * 128], htok[:, k * 128:(k + 1) * 128]) for k in range(8)], IDENT),
              reads=[("htok", buf), ("ident",)], writes=[("ps", bank)])

        def evac():
            P.add("dve", TT(HT[:, :, c:c + n], pv.rearrange("p (k n) -> p k n", k=8)[:, :, 0:n],
                            GCOL[:, gidx, :].unsqueeze(2).broadcast_to([128, 8, n]), ALU.mult),
                  reads=[("gcol",)], writes=[("ps", bank), hkey(t)])
        return evac

    def norm_parts(t, gidx):
        buf = tr_ctr[0] % 2
        tr_ctr[0] += 1
        htok = HTOK[buf]
        bank = 6 + buf
        c, n = tile_cols(t)
        pv = psb(bank)

        def a1():
            P.add("act", ACTI(htok, X[:, t, :], AF.Square, accum_out=SS[:, t:t + 1]),
                  reads=[("x", t)], writes=[("htok", buf), ("ss", t)])
            P.add("act", ACTI(RS[:, t:t + 1], SS[:, t:t + 1], AF.Sqrt, bias=EPS, scale=1.0 / D),
                  reads=[("ss", t)], writes=[("rs", t)])
            P.add("dve", RECIP(RS[:, t:t + 1], RS[:, t:t + 1]), reads=[("rs", t)], writes=[("rs", t)])
            P.add("dve", TSC(htok, X[:, t, :], RS[:, t:t + 1], ALU.mult),
                  reads=[("x", t), ("rs", t)], writes=[("htok", buf)])

        def a2():
            P.add("pe", TRS([(pv[:, k * 128:(k + 1) * 128], htok[:, k * 128:(k + 1) * 128]) for k in range(8)], IDENT),
                  reads=[("htok", buf), ("ident",)], writes=[("ps", bank)])

        def ev():
            P.add("dve", TT(HT[:, :, c:c + n], pv.rearrange("p (k n) -> p k n", k=8)[:, :, 0:n],
                            GCOL[:, gidx, :].unsqueeze(2).broadcast_to([128, 8, n]), ALU.mult),
                  reads=[("gcol",)], writes=[("ps", bank), hkey(t)])
        return a1, a2, ev

    class HookNorm:
        def __init__(self, gidx, tiles):
            self.gidx = gidx
            self.tiles = set(tiles)
            self.q = []

        def __call__(self, t):
            if t not in self.tiles:
                return
            a1, a2, ev = norm_parts(t, self.gidx)
            a1()
            self.q.append([a2, ev])
            self._advance(2)

        def _advance(self, keep):
            if len(self.q) >= 2 and self.q[-2][0] is not None:
                self.q[-2][0]()
                self.q[-2][0] = None
            while len(self.q) > keep:
                a2, ev = self.q.pop(0)
                if a2 is not None:
                    a2()
                ev()

        def flush(self):
            while self.q:
                a2, ev = self.q.pop(0)
                if a2 is not None:
                    a2()
                ev()

    def norm_tiles(tiles, gidx, hooks=None):
        pend = None
        for t in tiles:
            hk = hooks.get(t) if hooks else None
            ev = norm_a(t, gidx, *(hk[0] if hk else ()))
            if hk:
                hk[1]()
            if pend is not None:
                pend()
            pend = ev
        if pend is not None:
            pend()

    def norm_tile(t, gidx):
        norm_a(t, gidx)()

    def load_w(slot_ap, src_ap, keys, semkey):
        P.add("pool", DMA(slot_ap, src_ap), writes=keys, dma=semkey)

    def mlp_layer(l, batches, hook=None):
        pa = phase_alloc()
        HIDT = [vbf(pa(4096), 2048).rearrange("p (c n) -> p c n", c=4) for _ in range(2)]
        SQ = [vf32(pa(2048), 512) for _ in range(2)]
        NHB = 8
        steps = [(hb, bi) for hb in range(NHB) for bi in range(len(batches))]

        def wviews(hb):
            par = hb % 2
            win = WSL[2 * par].rearrange("p (k n) -> p k n", k=8)
            wout = WSL[2 * par + 1].rearrange("p (c n) -> p c n", c=4)
            return par, win, wout

        def issue_load(hb):
            par, win, wout = wviews(hb)
            load_w(win, mlp_in[l, :, hb * 512:(hb + 1) * 512].rearrange("(k p) n -> p k n", p=128),
                   [("wsl", 2 * par)], "w%d" % (2 * par))
            load_w(wout, mlp_out[l, hb * 512:(hb + 1) * 512, :].rearrange("(c p) n -> p c n", p=128),
                   [("wsl", 2 * par + 1)], "w%d" % (2 * par + 1))

        ctr = {"ps": 0, "out": 0}

        def emit_in(i):
            hb, bi = steps[i]
            par, win, wout = wviews(hb)
            tiles, c, n = batches[bi]
            hbuf = i % 2
            hid = HIDT[hbuf]
            for m in range(4):
                bank = ctr["ps"] % 2
                ctr["ps"] += 1
                P.add("pe", MM([(PS[bank][:, 0:n], win[:, k, m * 128:(m + 1) * 128], HT[:, k, c:c + n], k == 0, k == 7)
                                for k in range(8)]),
                      reads=[("wsl", 2 * par)] + [hkey(t) for t in tiles], writes=[("ps", bank)])
                sq = SQ[bank]
                P.add("act", ACTI(sq[:, 0:n], PS[bank][:, 0:n], AF.Square), reads=[("ps", bank)], writes=[("sq", bank)])
                P.add("dve", STT(hid[:, m, 0:n], PS[bank][:, 0:n], 0.0, sq[:, 0:n], ALU.is_gt, ALU.mult),
                      reads=[("ps", bank), ("sq", bank)], writes=[("hid", hbuf, m)])

        def emit_out(i):
            hb, bi = steps[i]
            par, win, wout = wviews(hb)
            tiles, c, n = batches[bi]
            hbuf = i % 2
            hid = HIDT[hbuf]
            for ti, t in enumerate(tiles):
                pb = 2 + 2 * (ctr["out"] % 2)
                ctr["out"] += 1
                nr = NS if t == TS else 128
                o = 0 if t == TS else ti * 128
                P.add("pe", MM([(PS[pb + half][0:nr, :], hid[:, hc, o:o + nr], wout[:, hc, half * 512:(half + 1) * 512],
                                 hc == 0, hc == 3) for hc in range(4) for half in range(2)]),
                      reads=[("wsl", 2 * par + 1)] + [("hid", hbuf, m) for m in range(4)],
                      writes=[("ps", pb), ("ps", pb + 1)])
                for half in range(2):
                    xs = X[0:nr, t, half * 512:(half + 1) * 512]
                    P.add("dve", TT(xs, xs, PS[pb + half][0:nr, :], ALU.add),
                          reads=[("ps", pb + half), ("x", t)], writes=[("x", t)])
                if hook is not None and hb == NHB - 1:
                    hook(t)

        issue_load(0)
        for i in range(len(steps)):
            hb, bi = steps[i]
            emit_in(i)
            if i >= 1:
                emit_out(i - 1)
            if bi == 0 and hb + 1 < NHB:
                issue_load(hb + 1)
        emit_out(len(steps) - 1)
        if hook is not None:
            hook.flush()

    def pool_layer(l):
        pa = phase_alloc()
        POOLED = [vbf(pa(8 * 128 * 2), 8 * 128).rearrange("p (k n) -> p k n", k=8) for _ in range(2)]
        AG = vbf(pa(8 * 128 * 2), 8 * 128).rearrange("p (g s n) -> p g s n", g=4, s=2)
        A0 = vbf(pa(8 * 128 * 2), 8 * 128).rearrange("p (g s n) -> p g s n", g=4, s=2)
        STT_ = vf32(pa(D * 4), D)
        SEL = vf32(pa(16 * 4), 16)
        SELH = vf32(pa(16 * 4), 16)
        WP = vbf(pa(8 * 256 * 2), 8 * 256).rearrange("p (g n) -> p g n", g=8)
        WPS = vbf(pa(8 * 256 * 2), 8 * 256).rearrange("p (g n) -> p g n", g=8)
        GTB = vf32(pa(4096), 1024)
        PS_S = vbf(pa(8 * NS * 2), 8 * NS).rearrange("p (k n) -> p k n", k=8)
        HT3 = [vbf(pa(2048), 1024) for _ in range(3)]
        gk = "@pl%d" % l
        P.add("sp", DMA(STT_[0:60, :], state[l]), writes=[("stt",)], dma=gk)
        P.add("sp", DMA(SEL[0:60, :], c_sel), writes=[("sel",)], dma=gk)
        P.add("sp", DMA(SELH[0:NS, :], c_selh), writes=[("selh",)], dma=gk)
        P.add("sp", DMA(TBL, gvec[10 + l].partition_broadcast(128)), writes=[("tbl",)], dma=gk)
        P.add("sp", DMA(GTB, gvec[l].partition_broadcast(128)), writes=[("gtbl",)], dma=gk)
        P.add("pool", DMA(WP, pool_w[l].rearrange("g (kk p) n -> p (g kk) n", p=128)), writes=[("wp",)], dma="wp")
        P.add("pool", DMA(AG, c_apool[0].rearrange("g s p n -> p g s n")), writes=[("ag",)], dma="ag")
        P.add("pool", DMA(A0, c_apool[1].rearrange("g s p n -> p g s n")), writes=[("a0",)], dma="a0")
        P.add("sp", DMA(pool_s[l, :, 0:14, :], state[l].rearrange("(s r) d -> s r d", r=15)[:, 1:15, :]),
              dma="out_psa%d" % l)
        if l == 0:
            load_x(list(range(2, NT)) + [TS], extra_reads=[("wp",), ("ag",), ("a0",), ("tbl",), ("gtbl",)])

        def fold_scale():
            for kk in range(2):
                P.add("dve", TT(WPS.rearrange("p (g k) n -> p g k n", k=2)[:, :, kk, :],
                                WP.rearrange("p (g k) n -> p g k n", k=2)[:, :, kk, :],
                                TBL.rearrange("p (g n) -> p g n", g=4), ALU.mult),
                      reads=[("wp",), ("tbl",)], writes=[("wps", kk)])
        tiles = [TH] + list(range(NT))
        mlp_hook = HookNorm(4 + l, ([TH] if l == 0 else []) + list(range(NT)) + [TS])
        pend = []
        pctr = [0]
        for i, t in enumerate(tiles):
            if i == 1:
                fold_scale()
            buf = i % 3
            htok = HT3[buf]
            hprev = HT3[(i - 1) % 3]
            P.add("act", ACTI(htok, X[:, t, :], AF.Square, accum_out=SS[:, t:t + 1]),
                  reads=[("x", t)], writes=[("htok3", buf), ("ss", t)])
            P.add("act", ACTI(RS[:, t:t + 1], SS[:, t:t + 1], AF.Sqrt, bias=EPS, scale=1.0 / D),
                  reads=[("ss", t)], writes=[("rs", t)])
            P.add("dve", RECIP(RS[:, t:t + 1], RS[:, t:t + 1]), reads=[("rs", t)], writes=[("rs", t)])
            P.add("dve", TSC(htok, X[:, t, :], RS[:, t:t + 1], ALU.mult),
                  reads=[("x", t), ("rs", t)], writes=[("htok3", buf)])
            if t == NT - 1:
                P.add("dve", STT(YST[0], X[:, t, :], RS[:, t:t + 1], GTB, ALU.mult, ALU.mult),
                      reads=[("x", t), ("rs", t), ("gtbl",)], writes=[("yst", 0)])
                P.add("sp", DMA(pool_p[l], YST[0][113:128, :]), reads=[("yst", 0)], dma="out_pp%d" % l)
            Am = A0 if t == 0 else AG
            pb = 2 + 2 * (i % 2)
            mm = []
            for c in range(8):
                g = c // 2
                o = PS[pb + c // 4][:, (c % 4) * 128:(c % 4) * 128 + 128]
                has_prev = (i > 0)
                mm.append((o, htok[:, c * 128:(c + 1) * 128], Am[:, g, 1, :], True, not has_prev))
                if has_prev:
                    mm.append((o, hprev[:, c * 128:(c + 1) * 128], Am[:, g, 0, :], False, True))
            P.add("pe", MM(mm), reads=[("htok3", buf), ("htok3", (i - 1) % 3), ("ag",), ("a0",)],
                  writes=[("ps", pb), ("ps", pb + 1)])

            def tail(t=t, pb=pb, i=i):
                pooled = POOLED[i % 2]
                for half in range(2):
                    P.add("dve", TT(pooled[:, 4 * half:4 * half + 4, :],
                                    PS[pb + half][:, :].rearrange("p (k n) -> p k n", k=4),
                                    GCOL[:, l, 4 * half:4 * half + 4].unsqueeze(2).broadcast_to([128, 4, 128]), ALU.mult),
                          reads=[("gcol",)], writes=[("ps", pb + half), ("pooled", i % 2, half)])
                ob = 0
                pctr[0] += 1
                P.add("pe", MM([(PS[ob + g // 2][:, (g % 2) * 256:(g % 2) * 256 + 256],
                                 pooled[:, 2 * g + kk, :], WPS[:, 2 * g + kk, :], kk == 0, kk == 1)
                                for g in range(4) for kk in range(2)]),
                      reads=[("pooled", i % 2, 0), ("pooled", i % 2, 1), ("wps", 0), ("wps", 1)],
                      writes=[("ps", ob), ("ps", ob + 1)])
                for half in range(2):
                    xs = X[:, t, half * 512:(half + 1) * 512]
                    P.add("dve", TT(xs, xs, PS[ob + half][:, :], ALU.add), reads=[("x", t)],
                          writes=[("ps", ob + half), ("x", t)])
                mlp_hook(t)
            pend.append(tail)
            if len(pend) > 1:
                pend.pop(0)()
        while pend:
            pend.pop(0)()
        t = TS
        P.add("act", ACTI(HT3[0], X[:, t, :], AF.Square, accum_out=SS[:, t:t + 1]),
              reads=[("x", t)], writes=[("htok3", 0), ("ss", t)])
        P.add("act", ACTI(RS[:, t:t + 1], SS[:, t:t + 1], AF.Sqrt, bias=EPS, scale=1.0 / D),
              reads=[("ss", t)], writes=[("rs", t)])
        P.add("dve", RECIP(RS[:, t:t + 1], RS[:, t:t + 1]), reads=[("rs", t)], writes=[("rs", t)])
        P.add("dve", STT(YST[1], X[:, t, :], RS[:, t:t + 1], GTB, ALU.mult, ALU.mult),
              reads=[("x", t), ("rs", t), ("gtbl",)], writes=[("yst", 1)])
        P.add("sp", DMA(pool_s[l, :, 14, :], YST[1][0:NS, :]), reads=[("yst", 1)], dma="out_psb%d" % l)
        mm = []
        for c in range(8):
            g = c // 2
            o = PS[4][:, c * NS:(c + 1) * NS]
            mm.append((o, STT_[0:60, c * 128:(c + 1) * 128], SEL[0:60, g * NS:(g + 1) * NS], True, False))
            mm.append((o, YST[1][0:NS, c * 128:(c + 1) * 128], SELH[0:NS, g * NS:(g + 1) * NS], False, True))
        P.add("pe", MM(mm), reads=[("stt",), ("sel",), ("selh",), ("yst", 1)], writes=[("ps", 4)])
        P.add("dve", COPY(PS_S, PS[4][:, 0:8 * NS].rearrange("p (k n) -> p k n", k=8)), reads=[],
              writes=[("ps", 4), ("pss",)])
        P.add("pe", MM([(PS[2 + g // 2][0:NS, (g % 2) * 256:(g % 2) * 256 + 256], PS_S[:, 2 * g + kk, :],
                         WPS[:, 2 * g + kk, :], kk == 0, kk == 1) for g in range(4) for kk in range(2)]),
              reads=[("pss",), ("wps", 0), ("wps", 1)], writes=[("ps", 2), ("ps", 3)])
        for half in range(2):
            xs = X[0:NS, TS, half * 512:(half + 1) * 512]
            P.add("dve", TT(xs, xs, PS[2 + half][0:NS, :], ALU.add), reads=[("x", TS)],
                  writes=[("ps", 2 + half), ("x", TS)])
        mlp_hook(TS)
        mlp_hook.flush()

    def mlp_phase(l, with_halo, hook=None):
        tiles = ([TH] if with_halo else []) + list(range(NT)) + [TS]
        if l >= 2:
            norm_tiles(tiles, 4 + l)
        batches = []
        if with_halo:
            batches.append(([TH], CH, 128))
        for b in range(4):
            batches.append((list(range(4 * b, 4 * b + 4)), C0 + 512 * b, 512))
        batches.append(([TS], CS, NS))
        mlp_layer(l, batches, hook)

    def dump_dbg():
        if dbg is not None:
            for t in range(18):
                P.add("sp", DMA(dbg[t], X[:, t, :]), reads=[("x", t)], dma="dbg")

    def colset(k, g, nb):
        d = BR[g][1]
        if d == 1:
            return HT[:, k, C0 + nb * 512:C0 + (nb + 1) * 512]
        if d == 4:
            return HT[:, k, C0 + nb:C0 + nb + 4 * 511 + 1:4]
        return HT[:, k, C0:C0 + 2048].rearrange("p (u r) -> p r u", r=16)[:, 4 * nb:4 * nb + 4, :]

    def chunkset(k, g, ci):
        d = BR[g][1]
        if d == 1:
            return HT[:, k, C0 + ci * 128:C0 + (ci + 1) * 128]
        if d == 4:
            r, jb = ci // 4, ci % 4
            s0 = C0 + 512 * jb + r
            return HT[:, k, s0:s0 + 4 * 127 + 1:4]
        s0 = C0 + ci
        return HT[:, k, s0:s0 + 16 * 127 + 1:16]

    def halo_chunk0(g):
        return (0, 1, 5)[g]

    GROUPS = [[0, 1], [2, 3], [4, 5], [6, 7]]

    def halo_src(g, h, kv_i):
        if g == 0:
            base = exp_dst0[h * 128:(h + 1) * 128, :]
        elif g == 1:
            base = exp_dst1[h * 128:(h + 1) * 128, :]
        else:
            base = exp_dst2[h // 2][(h % 2) * 128:(h % 2 + 1) * 128, :]
        return base.rearrange("p (c x) -> p c x", x=256)[:, :, kv_i * 128:(kv_i + 1) * 128]

    def kv_phase(skip_norm=False):
        pa = phase_alloc()
        KTS = [vbf(pa(4096), 2048) for _ in range(2)]
        VST = [vbf(pa(2048), 1024).rearrange("p (h x) -> p h x", h=8) for _ in range(2)]
        if not skip_norm:
            norm_tiles(list(range(NT)) + [TS], 8)
        ectr = {"kts": 0, "vst": 0, "ps": 0, "pt": 0, "yst": 0}

        def exp_view(g, h):
            if g == 0:
                t = exp_src0[h * 128:(h + 1) * 128, :]
            elif g == 1:
                t = exp_src1[h * 128:(h + 1) * 128, :]
            else:
                t = exp_src2[h // 2][(h % 2) * 128:(h % 2 + 1) * 128, :]
            return t.rearrange("p (c x) -> p c x", x=256)

        for g in (2, 1, 0):
            d = BR[g][1]
            R = d
            NB = 16 // R
            wK = WALL[:, 0:8192].rearrange("p (k n) -> p k n", k=8)
            wV = WALL[:, 8192:16384].rearrange("p (k n) -> p k n", k=8)
            load_w(wK, w_kv[:, g * 2048:g * 2048 + 1024].rearrange("(k p) n -> p k n", p=128),
                   [("wsl", 0), ("wsl", 1)], "w0")
            load_w(wV, w_kv[:, g * 2048 + 1024:g * 2048 + 2048].rearrange("(k p) n -> p k n", p=128),
                   [("wsl", 2), ("wsl", 3)], "w2")
            for h in range(8):
                kb = ectr["kts"] % 2
                ectr["kts"] += 1
                kts = KTS[kb]
                for nb in range(4):
                    bank = ectr["ps"] % 2
                    ectr["ps"] += 1
                    P.add("pe", MM([(PS[bank][:, :], wK[:, k, h * 128:(h + 1) * 128],
                                     HT[:, k, C0 + nb * 512:C0 + (nb + 1) * 512], k == 0, k == 7) for k in range(8)]),
                          reads=[("wsl", 0), ("wsl", 1)] + [hkey(t) for t in range(4 * nb, 4 * nb + 4)],
                          writes=[("ps", bank)])
                    uw = 512 // d
                    P.add("act", ACTI(kts.rearrange("p (r u) -> p r u", r=d)[:, :, nb * uw:(nb + 1) * uw],
                                      PS[bank][:, :].rearrange("p (u r) -> p r u", r=d), AF.Copy),
                          reads=[], writes=[("ps", bank), ("kts", kb, nb)])
                rk = [("kts", kb, nb) for nb in range(4)]
                P.add("sp", DMA(kT_d[g * 8 + h], kts), reads=rk, writes=[("kT_d", g * 8 + h)], dma="kts%d" % kb)
                src = kts.rearrange("p (r u) -> p r u", r=R)[:, :, (NB - 1) * 128:NB * 128]
                P.add("sp", DMA(exp_view(g, h)[:, :, 0:128], src), reads=rk, writes=[("exps", g, h, "k")],
                      dma="kts%d" % kb)
            for ci in range(16):
                pb = 2 + 2 * (ectr["pt"] % 2)
                ectr["pt"] += 1
                vb = ectr["vst"] % 2
                ectr["vst"] += 1
                P.add("pe", MM([(PS[pb + half][:, :], chunkset(k, g, ci), wV[:, k, half * 512:(half + 1) * 512],
                                 k == 0, k == 7) for half in range(2) for k in range(8)]),
                      reads=[("wsl", 2), ("wsl", 3)] + [hkey(t) for t in range(NT)],
                      writes=[("ps", pb), ("ps", pb + 1)])
                for half in range(2):
                    P.add("dve", COPY(VST[vb][:, 4 * half:4 * half + 4, :],
                                      PS[pb + half][:, :].rearrange("p (h x) -> p h x", h=4)),
                          reads=[], writes=[("ps", pb + half), ("vst", vb, half)])
                rk = [("vst", vb, 0), ("vst", vb, 1)]
                P.add("sp", DMA(v_d[g * 8:(g + 1) * 8].rearrange("h p (c x) -> p h c x", x=128)[:, :, ci, :], VST[vb]),
                      reads=rk, writes=[("v_d", g, ci)], dma="vst%d" % vb)
                r, jb = ci // NB, ci % NB
                if jb == NB - 1:
                    if g < 2:
                        srcv = (exp_src0 if g == 0 else exp_src1).rearrange("(h p) (c x) -> p h c x", p=128, x=256)
                        P.add("sp", DMA(srcv[:, :, r, 128:256], VST[vb]), reads=rk, writes=[("exps", g, "v", r)],
                              dma="vst%d" % vb)
                    else:
                        for i in range(4):
                            srcv = exp_src2[i].rearrange("(h p) (c x) -> p h c x", p=128, x=256)
                            P.add("sp", DMA(srcv[:, :, r, 128:256], VST[vb][:, 2 * i:2 * i + 2, :]), reads=rk,
                                  writes=[("exps", g, "v", r, i)], dma="vst%d" % vb)
            if g < 2:
                rk = [("exps", g, h, "k") for h in range(8)] + [("exps", g, "v", r) for r in range(R)]
                P.add("pool", CC(GROUPS, (exp_src0 if g == 0 else exp_src1).opt(), (exp_dst0 if g == 0 else exp_dst1).opt()),
                      reads=rk, writes=[("expd", g, h) for h in range(8)], dma="cc", cc=True)
            else:
                for i in range(4):
                    rk = [("exps", g, h, "k") for h in (2 * i, 2 * i + 1)] + [("exps", g, "v", r, i) for r in range(R)]
                    P.add("pool", CC(GROUPS, exp_src2[i].opt(), exp_dst2[i].opt()),
                          reads=rk, writes=[("expd", g, h) for h in (2 * i, 2 * i + 1)], dma="cc", cc=True)
            nt_out = (1, 4, 16)[g]
            for t in list(range(NT - nt_out, NT)) + [TS]:
                c, n = tile_cols(t)
                for kv_i, wsel in enumerate((wK, wV)):
                    pb = 2 + 2 * (ectr["pt"] % 2)
                    ectr["pt"] += 1
                    slot = ectr["yst"] % 2
                    ectr["yst"] += 1
                    P.add("pe", MM([(PS[pb + half][0:n, :], HT[:, k, c:c + n], wsel[:, k, half * 512:(half + 1) * 512],
                                     k == 0, k == 7) for half in range(2) for k in range(8)]),
                          reads=[("wsl", 2 * kv_i), ("wsl", 2 * kv_i + 1), hkey(t)],
                          writes=[("ps", pb), ("ps", pb + 1)])
                    for half in range(2):
                        P.add("act", ACTI(YST[slot][0:n, half * 512:(half + 1) * 512], PS[pb + half][0:n, :], AF.Copy),
                              reads=[], writes=[("ps", pb + half), ("yst", slot, half)])
                    if t == TS:
                        dst = kvs[g][:, kv_i * 1024:(kv_i + 1) * 1024]
                        wk = [("kvs", g)]
                    else:
                        r0 = (t - (NT - nt_out)) * 128
                        dst = kvp[g][r0:r0 + 128, kv_i * 1024:(kv_i + 1) * 1024]
                        wk = []
                    P.add("sp", DMA(dst, YST[slot][0:n, :]), reads=[("yst", slot, 0), ("yst", slot, 1)], writes=wk,
                          dma="out_kv%d" % slot)

    def attn_layer(l, skip_norm=False):
        lb = l - 2
        pa = phase_alloc()
        KTG, VVG = [], []
        for g in range(3):
            R = BR[g][1]
            NB = 16 // R
            n = R * (NB + 1) * 128
            KTG.append(vbf(pa(n * 2), n).rearrange("p (r c x) -> p r c x", r=R, x=128))
            VVG.append(vbf(pa(n * 2), n).rearrange("p (r c x) -> p r c x", r=R, x=128))
        OACC = vf32(pa(2 * 2048 * 4), 2 * 2048).rearrange("p (a n) -> p a n", a=2)
        QT = [YALL[:, 0:1024].bitcast(BF16), YALL[:, 1024:2048].bitcast(BF16)]
        WQ = [WALL[:, 8192 + i * 1024:8192 + (i + 1) * 1024].rearrange("p (k n) -> p k n", k=8) for i in range(2)]
        o3 = 8192 + 2048
        TBS = [WALL[:, o3 + i * 768:o3 + (i + 1) * 768] for i in range(2)]
        o3 += 2 * 768
        PPB = [WALL[:, o3 + i * 1024:o3 + (i + 1) * 1024] for i in range(3)]
        o3 += 3 * 1024
        QTS = WALL[:, o3:o3 + 96].rearrange("p (a s) -> p a s", s=NS)
        o3 += 96
        assert o3 <= 16384
        OT2 = [TBL.bitcast(BF16), TBL.bitcast(BF16)]
        DTMP = [HTOKALL.bitcast(F32)[:, 0:512], HTOKALL.bitcast(F32)[:, 512:1024]]
        WO = WALL[:, 0:8192].rearrange("p (h n) -> p h n", h=8)
        load_w(WO, w_o[lb].rearrange("(h p) n -> p h n", p=128), [("wsl", 0), ("wsl", 1)], "w0")
        ctr = {"unit": 0, "ps": 0, "pp": 0, "ud": 0, "sring": 0}
        items = [(h, g) for h in range(8) for g in range(3)]
        DLAG = 1
        deferred = []

        def pop_deferred(maxlen):
            while len(deferred) > maxlen:
                deferred.pop(0)()

        def issue_loads(idx):
            h, g = items[idx]
            gh = g * 8 + h
            R = BR[g][1]
            NB = 16 // R
            par = idx % 2
            P.add("pool", DMA(WQ[par], w_q[lb][:, g * 1024 + h * 128:g * 1024 + (h + 1) * 128]
                              .rearrange("(k p) n -> p k n", p=128)), writes=[("wq", par)], dma="wq%d" % par)
            P.add("pool", DMA(TBS[par], c_bias3[gh]), writes=[("tbs", par)], dma="tbs%d" % par)
            P.add("sp", DMA(KTG[g][:, :, 1:NB + 1, :], kT_d[gh].rearrange("p (r c x) -> p r c x", r=R, x=128)),
                  reads=[("kT_d", gh)], writes=[("kt", g, "own")], dma="ktk%d" % g)
            P.add("sp", DMA(KTG[g][:, :, 0, :], halo_src(g, h, 0)), reads=[("expd", g, h)],
                  writes=[("kt", g, "h")], dma="ktk%d" % g)
            P.add("sp", DMA(VVG[g][:, :, 1:NB + 1, :], v_d[gh].rearrange("p (r c x) -> p r c x", r=R, x=128)),
                  reads=[("v_d", g, ci) for ci in range(16)], writes=[("vv", g, "own")], dma="ktv%d" % g)
            P.add("sp", DMA(VVG[g][:, :, 0, :], halo_src(g, h, 1)), reads=[("expd", g, h)],
                  writes=[("vv", g, "h")], dma="ktv%d" % g)

        def qproj_part(idx, nb):
            h, g = items[idx]
            gh = g * 8 + h
            d = BR[g][1]
            par = idx % 2
            wq = WQ[par]
            qt = QT[par]
            qt3 = qt.rearrange("p (r u) -> p r u", r=d)
            bank = 0
            if nb < 4:
                P.add("pe", MM([(PS[bank][:, :], wq[:, k, :], HT[:, k, C0 + nb * 512:C0 + (nb + 1) * 512], k == 0, k == 7)
                                for k in range(8)]),
                      reads=[("wq", par)] + [hkey(t) for t in range(4 * nb, 4 * nb + 4)], writes=[("ps", bank)])
                uw = 512 // d
                P.add("act", ACTI(qt3[:, :, nb * uw:(nb + 1) * uw], PS[bank][:, :].rearrange("p (u r) -> p r u", r=d),
                                  AF.Copy, scale=SCALE), reads=[], writes=[("ps", bank), ("qt", par, nb)])
            else:
                P.add("pe", MM([(PS[bank][:, 0:NS], wq[:, k, :], HT[:, k, CS:CS + NS], k == 0, k == 7) for k in range(8)]),
                      reads=[("wq", par), hkey(TS)], writes=[("ps", bank)])
                P.add("act", ACTI(QTS[:, gh, :], PS[bank][:, 0:NS], AF.Copy, scale=SCALE),
                      reads=[], writes=[("ps", bank), ("qts", gh)])

        def qproj(idx):
            for nb in range(5):
                qproj_part(idx, nb)

        def make_pv(g, pv_fn, den_fn, oa, pi):
            def f():
                ui = ctr["ud"] % 2
                ctr["ud"] += 1
                ub, db = 4 + 2 * ui, 5 + 2 * ui
                U, DN = PS[ub], PS[db]
                UDv = PSALL[:, ub * 512:(ub + 2) * 512].rearrange("p (a n) -> p a n", a=2)
                P.add("pe", MMX(pv_fn(U) + den_fn(DN)),
                      reads=[("vv", g, "own"), ("vv", g, "h"), ("pp", pi, 0), ("pp", pi, 1), ("onesb",)],
                      writes=[("ps", ub), ("ps", db)])
                if g == 0:
                    P.add("dve", COPY(oa, UDv), reads=[], writes=[("ps", ub), ("ps", db), ("oacc", "u"), ("oacc", "d")])
                else:
                    uv = UDv if g == 1 else UDv.rearrange("p a (m i) -> p a m i", m=4)
                    P.add("dve", TT(oa, oa, uv, ALU.add), reads=[],
                          writes=[("ps", ub), ("ps", db), ("oacc", "u"), ("oacc", "d")])
            return f

        def make_finish(h):
            def f():
                den = OACC[:, 1, :]
                ot = OT2[h % 2]
                P.add("act", ACTI(den, den, AF.Ln), reads=[], writes=[("oacc", "d")])
                P.add("act", ACTI(den, den, AF.Exp, scale=-1.0), reads=[], writes=[("oacc", "d")])
                P.add("dve", TT(ot, OACC[:, 0, :], den, ALU.mult), reads=[], writes=[("ot", h % 2), ("oacc", "u"), ("oacc", "d")])
                P.add("sp", DMA(ot_d[h], ot), reads=[("ot", h % 2)], writes=[("ot_d", h)], dma="otd%d" % (h % 2))
            return f

        issue_loads(0)
        if not skip_norm:
            norm_tiles(list(range(NT)) + [TS], l)
        qproj(0)
        for idx, (h, g) in enumerate(items):
            if idx + 1 < len(items):
                issue_loads(idx + 1)
            d = BR[g][1]
            par = idx % 2
            qt = QT[par]
            tbs = TBS[par]
            K4 = KTG[g]
            V4 = VVG[g]
            tb4h = tbs[:, 0:512]
            tb4 = tbs[:, 256:768]
            nunits = 4 if g < 2 else 4
            for u in range(nunits):
                sa = 1 + ctr["sring"] % 3
                sbb = 1 + (ctr["sring"] + 1) % 3
                ctr["sring"] += 2
                ctr["unit"] += 1
                A, B = PS[sa], PS[sbb]
                qk = []
                pv = []
                if g < 2:
                    if g == 0:
                        r, j0 = 0, 4 * u
                        NBg = 16
                    else:
                        r, j0 = u, 0
                        NBg = 4
                    qc = lambda j, n=1: qt[:, (r * NBg + j) * 128:(r * NBg + j + n) * 128]
                    tba = tb4h if j0 == 0 else tb4
                    tbb = tb4
                    qk = [(A, IDENT, tba, True, False),
                          (A[:, 0:128], K4[:, r, j0, :], qc(j0), False, False),
                          (A[:, 128:384], K4[:, r, j0 + 1, :], qc(j0, 2), False, False),
                          (A[:, 384:512], K4[:, r, j0 + 2, :], qc(j0 + 1), False, True),
                          (B, IDENT, tbb, True, False),
                          (B[:, 0:128], K4[:, r, j0 + 2, :], qc(j0 + 2), False, False),
                          (B[:, 128:384], K4[:, r, j0 + 3, :], qc(j0 + 2, 2), False, False),
                          (B[:, 384:512], K4[:, r, j0 + 4, :], qc(j0 + 3), False, True)]
                    if g == 0:
                        oa = OACC[:, :, 512 * u:512 * (u + 1)]
                    else:
                        oa = OACC[:, :, r:r + 4 * 511 + 1:4]
                else:
                    r0 = 4 * u
                    tba = tb4h
                    tbb = tb4h
                    for m in range(4):
                        bank = A if m < 2 else B
                        c = (m % 2) * 256
                        q = qt[:, (r0 + m) * 128:(r0 + m + 1) * 128]
                        if m % 2 == 0:
                            qk.append((bank, IDENT, tb4h, True, False))
                        qk.append((bank[:, c:c + 128], K4[:, r0 + m, 0, :], q, False, False))
                        qk.append((bank[:, c + 128:c + 256], K4[:, r0 + m, 1, :], q, False, m % 2 == 1))
                    oa = OACC[:, :, r0:r0 + 16 * 127 + 4].rearrange("p a (i m) -> p a m i", m=16)[:, :, 0:4, :] \
                        if False else None
                P.add("pe", MMX(qk), reads=[("kt", g, "own"), ("kt", g, "h"), ("tbs", par), ("ident",)] +
                      [("qt", par, nb) for nb in range(4)], writes=[("ps", sa), ("ps", sbb)])
                pi = ctr["pp"] % 3
                ctr["pp"] += 1
                pp = PPB[pi]
                P.add("act", ACTI(pp[:, 0:512], A, AF.Exp), reads=[], writes=[("ps", sa), ("pp", pi, 0)])
                P.add("act", ACTI(pp[:, 512:1024], B, AF.Exp), reads=[], writes=[("ps", sbb), ("pp", pi, 1)])
                sl = lambda s_, n=1, pp=pp: pp[:, s_ * 128:(s_ + n) * 128]
                if g < 2:
                    def pv_fn(U, r=r, j0=j0, sl=sl, V4=V4):
                        return [(U[:, 0:128], V4[:, r, j0, :], sl(0), True, False),
                                (U[:, 0:256], V4[:, r, j0 + 1, :], sl(1, 2), False, False),
                                (U[:, 128:384], V4[:, r, j0 + 2, :], sl(3, 2), False, False),
                                (U[:, 256:512], V4[:, r, j0 + 3, :], sl(5, 2), False, False),
                                (U[:, 384:512], V4[:, r, j0 + 4, :], sl(7), False, True)]
                else:
                    def pv_fn(U, r0=r0, sl=sl, V4=V4):
                        out = []
                        for m in range(4):
                            out.append((U[:, m * 128:(m + 1) * 128], V4[:, r0 + m, 0, :], sl(2 * m), m == 0, False))
                            out.append((U[:, m * 128:(m + 1) * 128], V4[:, r0 + m, 1, :], sl(2 * m + 1), False, m == 3))
                        return out
                    oa = OACC[:, :, :].rearrange("p a (i r) -> p a r i", r=16)[:, :, r0:r0 + 4, :]
                ppv = pp.rearrange("p (b s x) -> p s b x", s=2, x=128)

                def den_fn(DN, ppv=ppv):
                    return [(DN[:, :], ONESB, ppv[:, 0, :, :], True, False), (DN[:, :], ONESB, ppv[:, 1, :, :], False, True)]
                pv, den = pv_fn, den_fn
                if idx + 1 < len(items):
                    qproj_part(idx + 1, u)
                    if u == 3:
                        qproj_part(idx + 1, 4)
                deferred.append(make_pv(g, pv, den, oa, pi))
                pop_deferred(DLAG)
            if g == 2:
                deferred.append(make_finish(h))
        pop_deferred(0)
        P.barrier()
        OTT = [YALL[:, 0:512].bitcast(BF16).rearrange("p (h c) -> p h c", h=8),
               YALL[:, 512:1024].bitcast(BF16).rearrange("p (h c) -> p h c", h=8)]

        def wo_load(t):
            ob = t % 2
            P.add("sp", DMA(OTT[ob], ot_d[:, :, t * 128:(t + 1) * 128].rearrange("h p c -> p h c")),
                  writes=[("ott", ob)], dma="ott%d" % ob)

        def wo_tile(t):
            ob = t % 2
            pb = 0
            P.add("pe", MM([(PS[pb + half][:, :], OTT[ob][:, hh, :], WO[:, hh, half * 512:(half + 1) * 512], hh == 0, hh == 7)
                            for half in range(2) for hh in range(8)]),
                  reads=[("ott", ob), ("wsl", 0), ("wsl", 1)], writes=[("ps", pb), ("ps", pb + 1)])
            for half in range(2):
                xs = X[:, t, half * 512:(half + 1) * 512]
                P.add("dve", TT(xs, xs, PS[pb + half][:, :], ALU.add), reads=[("x", t)],
                      writes=[("ps", pb + half), ("x", t)])
            if t + 2 < NT:
                wo_load(t + 2)
        pa2 = phase_alloc()
        KC = [vf32(pa2(4096), 1024) for _ in range(2)]
        KB = [vf32(pa2(4096), 1024) for _ in range(2)]
        PROD = [vf32(pa2(4096), 1024) for _ in range(2)]
        VC = [vbf(pa2(2048), 1024) for _ in range(2)]
        VB = [vbf(pa2(2048), 1024) for _ in range(2)]
        BS = vf32(pa2(3 * 8 * 4), 24).rearrange("p (g h) -> p g h", g=3)
        B0 = vf32(pa2(3 * 8 * 4), 24).rearrange("p (g h) -> p g h", g=3)
        SC = [vf32(pa2(32), 8) for _ in range(2)]
        SCB = [vf32(pa2(32), 8) for _ in range(2)]
        PA = [vbf(pa2(64), 8) for _ in range(2)]
        PB = [vbf(pa2(64), 8) for _ in range(2)]
        OSA = vf32(pa2(NS * 16 * 4), NS * 16).rearrange("p (s x) -> p s x", s=NS)
        RD = vf32(pa2(NS * 8 * 4), NS * 8).rearrange("p (s x) -> p s x", s=NS)
        OTS = vbf(pa2(8 * NS * 2), 8 * NS).rearrange("p (h s) -> p h s", h=8)
        P.add("sp", DMA(BS, c_biasS.rearrange("g p h -> p g h")), writes=[("bs",)], dma="@sb%d" % l)
        P.add("sp", DMA(B0[0:1, :, :], c_bias0.rearrange("(o g) h -> o g h", o=1)), writes=[("b0",)], dma="@sb%d" % l)
        sitems = [(s, g) for s in range(NS) for g in range(3)]

        def s_loads(it):
            s, g = sitems[it]
            b = it % 2
            d = BR[g][1]
            P.add("sp", DMA(KC[b], cache[g][s, 0:127 * d + 1:d, 0:1024]), writes=[("kc", b)], dma="sc_kc%d" % b)
            P.add("pool", DMA(VC[b], cache[g][s, 0:127 * d + 1:d, 1024:2048]), writes=[("vc", b)], dma="sc_vc%d" % b)
            P.add("sp", DMA(KB[b][0:1, :], kvs[g][s:s + 1, 0:1024]), reads=[("kvs", g)], writes=[("kb", b)],
                  dma="sc_kb%d" % b)
            P.add("pool", DMA(VB[b][0:1, :], kvs[g][s:s + 1, 1024:2048]), reads=[("kvs", g)], writes=[("vb", b)],
                  dma="sc_vb%d" % b)

        def s_scores(it):
            s, g = sitems[it]
            b = it % 2
            qb0 = 2 if b == 0 else 4
            P.add("pe", MM([(PS[qb0 + hh // 4][:, (hh % 4) * 128:(hh % 4) * 128 + 128],
                             QTS[:, g * 8 + hh, s:s + 1].to_broadcast([128, 128]), IDENT, True, True)
                            for hh in range(8)]),
                  reads=[("ident",)], writes=[("ps", qb0), ("ps", qb0 + 1)])
            for half in range(2):
                P.add("dve", TT(PROD[b][:, half * 512:(half + 1) * 512], KC[b][:, half * 512:(half + 1) * 512],
                                PS[qb0 + half][:, :], ALU.mult), reads=[("kc", b), ("ps", qb0 + half)],
                      writes=[("prod", b, half)])
            P.add("dve", REDUCE(SC[b], PROD[b].rearrange("p (h x) -> p h x", h=8), AX.X, ALU.add),
                  reads=[("prod", b, 0), ("prod", b, 1)], writes=[("sc", b)])
            for half in range(2):
                P.add("dve", TT(PROD[b][0:1, half * 512:(half + 1) * 512], KB[b][0:1, half * 512:(half + 1) * 512],
                                PS[qb0 + half][0:1, :], ALU.mult), reads=[("kb", b), ("sc", b)],
                      writes=[("prod", b, half), ("ps", qb0 + half)])
            P.add("dve", REDUCE(SCB[b][0:1, :], PROD[b][0:1, :].rearrange("p (h x) -> p h x", h=8), AX.X, ALU.add),
                  reads=[("prod", b, 0), ("prod", b, 1)], writes=[("scb", b)])
            P.add("dve", TT(SC[b], SC[b], BS[:, g, :], ALU.add), reads=[("bs",)], writes=[("sc", b)])
            P.add("dve", TT(SCB[b][0:1, :], SCB[b][0:1, :], B0[0:1, g, :], ALU.add), reads=[("b0",)], writes=[("scb", b)])
            P.add("act", ACTI(PA[b], SC[b], AF.Exp), reads=[("sc", b)], writes=[("pa", b)])
            P.add("act", ACTI(PB[b][0:1, :], SCB[b][0:1, :], AF.Exp), reads=[("scb", b)], writes=[("pb", b)])

        def s_pv(it):
            s, g = sitems[it]
            b = it % 2
            pvb = 6 + b
            items2 = []
            for hh in range(8):
                items2.append((PS[pvb][:, hh:hh + 1], VC[b][:, hh * 128:(hh + 1) * 128], PA[b][:, hh:hh + 1], True, False))
                items2.append((PS[pvb][:, hh:hh + 1], VB[b][0:1, hh * 128:(hh + 1) * 128], PB[b][0:1, hh:hh + 1],
                               False, True))
            items2.append((PS[pvb][:, 8:16], ONESB, PA[b], True, False))
            items2.append((PS[pvb][:, 8:16], ONESB[0:1, :], PB[b][0:1, :], False, True))
            P.add("pe", MM(items2), reads=[("vc", b), ("vb", b), ("pa", b), ("pb", b), ("onesb",)], writes=[("ps", pvb)])
            if g == 0:
                P.add("dve", COPY(OSA[:, s, :], PS[pvb][:, 0:16]), reads=[], writes=[("ps", pvb), ("osa", s)])
            else:
                P.add("dve", TT(OSA[:, s, :], OSA[:, s, :], PS[pvb][:, 0:16], ALU.add), reads=[],
                      writes=[("ps", pvb), ("osa", s)])

        wo_load(0)
        wo_load(1)
        s_loads(0)
        s_loads(1)
        s_scores(0)
        wo_next = [0]

        def wo_some(n):
            for _ in range(n):
                if wo_next[0] < NT:
                    wo_tile(wo_next[0])
                    wo_next[0] += 1

        for it in range(len(sitems)):
            wo_some(2 if it % 3 == 0 else 1)
            if it + 1 < len(sitems):
                s_scores(it + 1)
            s_pv(it)
            if it + 2 < len(sitems):
                s_loads(it + 2)
        wo_some(NT)
        P.add("dve", RECIP(RD, OSA[:, :, 8:16]), reads=[("osa", s) for s in range(NS)], writes=[("rd",)])
        P.add("dve", TT(OTS.rearrange("p h s -> p s h"), OSA[:, :, 0:8], RD, ALU.mult),
              reads=[("rd",)] + [("osa", s) for s in range(NS)], writes=[("ots",)])
        P.add("pe", MM([(PS[half][0:NS, :], OTS[:, hh, :], WO[:, hh, half * 512:(half + 1) * 512], hh == 0, hh == 7)
                        for half in range(2) for hh in range(8)]),
              reads=[("ots",), ("wsl", 0), ("wsl", 1)], writes=[("ps", 0), ("ps", 1)])
        for half in range(2):
            xs = X[0:NS, TS, half * 512:(half + 1) * 512]
            P.add("dve", TT(xs, xs, PS[half][0:NS, :], ALU.add), reads=[("x", TS)],
                  writes=[("ps", half), ("x", TS)])

    def final_norm():
        P.add("sp", DMA(TBL, gvec[9].partition_broadcast(128)), writes=[("tbl",)], dma="tblf")
        for t in list(range(NT)) + [TS]:
            slot = t % 2
            P.add("act", ACTI(YST[slot], X[:, t, :], AF.Square, accum_out=SS[:, t:t + 1]),
                  reads=[("x", t)], writes=[("yst", slot), ("ss", t)])
            P.add("act", ACTI(RS[:, t:t + 1], SS[:, t:t + 1], AF.Sqrt, bias=EPS, scale=1.0 / D),
                  reads=[("ss", t)], writes=[("rs", t)])
            P.add("dve", RECIP(RS[:, t:t + 1], RS[:, t:t + 1]), reads=[("rs", t)], writes=[("rs", t)])
            P.add("dve", STT(YST[slot], X[:, t, :], RS[:, t:t + 1], TBL, ALU.mult, ALU.mult),
                  reads=[("x", t), ("rs", t), ("tbl",)], writes=[("yst", slot)])
            if t == TS:
                P.add("sp", DMA(y_s, YST[slot][0:NS, :]), reads=[("yst", slot)], dma="out_y%d" % slot)
            else:
                P.add("sp", DMA(y_p[t], YST[slot]), reads=[("yst", slot)], dma="out_y%d" % slot)

    stages = ["pool0", "mlp0", "pool1", "mlp1", "kv", "attn2", "mlp2", "attn3", "mlp3"]
    last = stages.index(stop_after) if stop_after else len(stages) - 1

    def run_stage(i):
        name = stages[i]
        if name == "pool0":
            pool_layer(0)
        elif name == "mlp0":
            mlp_phase(0, True)
        elif name == "pool1":
            pool_layer(1)
        elif name == "mlp1":
            mlp_phase(1, False, HookNorm(8, list(range(NT)) + [TS]))
        elif name == "kv":
            kv_phase(skip_norm=True)
        elif name == "attn2":
            attn_layer(2)
        elif name == "mlp2":
            mlp_phase(2, False, HookNorm(3, list(range(NT)) + [TS]))
        elif name == "attn3":
            attn_layer(3, skip_norm=True)
        elif name == "mlp3":
            mlp_phase(3, False)

    for i in range(last + 1):
        run_stage(i)
        P.barrier()
    dump_dbg()
    final_norm()
    P.emit(nc, es)
    es.close()
    return nc, P


def make_in_maps(inputs):
    f = lambda a: np.ascontiguousarray(np.asarray(a, dtype=np.float32))
    x_prompt = f(inputs["x_prompt"]); x_sample = f(inputs["x_sample"]); state_pool = f(inputs["state_pool"])
    caches = [f(inputs["cache_kv_w128"]), f(inputs["cache_kv_w512"]), f(inputs["cache_kv_w2048"])]
    rel_bias = f(inputs["rel_bias"])
    gvec = np.concatenate([f(inputs["norm_mix"]), f(inputs["norm_mlp"]), f(inputs["norm_kv"])[None],
                           f(inputs["norm_final"])[None], f(inputs["pool_scale"])], 0)
    shared = {
        "gvec": gvec, "pool_w": f(inputs["pool_w"]), "mlp_in": f(inputs["mlp_in"]), "mlp_out": f(inputs["mlp_out"]),
        "w_kv": f(inputs["w_kv"]), "w_q": f(inputs["w_q"]), "w_o": f(inputs["w_o"]),
        "c_ident": np.eye(128, dtype=np.float32),
    }
    sel = np.zeros((NS, 15, 4, NS), np.float32)
    for g, w in enumerate((2, 4, 8, 16)):
        for s in range(NS):
            sel[s, 15 - (w - 1):, g, s] = 1.0 / w
    shared["c_sel"] = sel.reshape(60, 16)
    selh = np.zeros((NS, 4, NS), np.float32)
    for g, w in enumerate((2, 4, 8, 16)):
        for s_ in range(NS):
            selh[s_, g, s_] = 1.0 / w - 1.0
    shared["c_selh"] = selh.reshape(NS, 16)
    j, valid = toeplitz_index()
    bf = np.zeros((24, 128, 256), np.float32)
    for g, (w, d) in enumerate(BR):
        bk = t5_bucket_np(j * d)
        for h in range(8):
            bf[g * 8 + h] = rel_bias[bk, g * 8 + h]
    nx, sm = bf[:, :, 0:128], bf[:, :, 128:256]
    vn, vs = valid[:, 0:128] > 0, valid[:, 128:256] > 0
    NEG = np.float32(-30000.0)
    nxm = np.where(vn[None], nx, NEG)
    smm = np.where(vs[None], sm, NEG)
    dead = np.full_like(nxm, NEG)
    tabs = []
    for has_halo in (False, True):
        hn = nxm if has_halo else dead
        t01 = np.concatenate([hn, smm, nxm, smm, nxm, smm], 2)
        t2 = np.concatenate([hn, smm, hn, smm, dead, dead], 2)
        tabs.append(np.ascontiguousarray(np.concatenate([t01[0:16], t2[16:24]], 0)))
    bS = np.zeros((3, 128, 8), np.float32)
    b0 = np.zeros((3, 8), np.float32)
    for g, (w, d) in enumerate(BR):
        bk = t5_bucket_np((128 - np.arange(128)) * d)
        bS[g] = rel_bias[bk, g * 8:(g + 1) * 8]
        b0[g] = rel_bias[0, g * 8:(g + 1) * 8]
    shared["c_biasS"] = bS
    shared["c_bias0"] = b0
    in_maps = []
    for c in range(8):
        b, half = c // 2, c % 2
        xin = np.zeros((18, 128, D), np.float32)
        xin[0:NT] = x_prompt[b, half * 2048:(half + 1) * 2048].reshape(NT, 128, D)
        xin[TS, 0:NS] = x_sample[NS * c:NS * (c + 1), 0, :]
        if half == 1:
            xin[TH] = x_prompt[b, 1920:2048]
        ap = np.zeros((2, 4, 2, 128, 128), np.float32)
        tt_src = np.arange(128)[:, None]
        tt_dst = np.arange(128)[None, :]
        for g, w in enumerate((2, 4, 8, 16)):
            for first in range(2):
                if first == 1 and half == 0:
                    cnt = np.minimum(np.arange(128) + 1, w).astype(np.float32)[None, :]
                else:
                    cnt = np.full((1, 128), float(w), np.float32)
                dist_same = tt_dst - tt_src
                dist_prev = tt_dst + 128 - tt_src
                ap[first, g, 1] = ((dist_same >= 0) & (dist_same < w)) / cnt - (dist_same == 0)
                ap[first, g, 0] = ((dist_prev >= 0) & (dist_prev < w)) / cnt
        m = dict(shared)
        m["xin"] = xin
        m["state"] = np.ascontiguousarray(state_pool[:, NS * c:NS * (c + 1)].reshape(2, 60, D))
        for g in range(3):
            m["cache%d" % g] = np.ascontiguousarray(caches[g][NS * c:NS * (c + 1)].reshape(NS, -1, 2048))
        m["c_apool"] = ap
        m["c_bias3"] = tabs[half]
        in_maps.append(m)
    return in_maps


_CACHE = {}


def kernel(**inputs):
    if "nc" not in _CACHE:
        _CACHE["nc"] = build_program()[0]
    nc = _CACHE["nc"]
    in_maps = make_in_maps(inputs)
    res = run_bass_kernel_spmd(nc, in_maps, core_ids=list(range(8)))
    R = res.results
    y_prompt = np.zeros((4, 4096, D), np.float32)
    y_sample = np.zeros((32, 1, D), np.float32)
    pool_p = np.zeros((2, 4, 15, D), np.float32)
    pool_s = np.zeros((2, 32, 15, D), np.float32)
    kvp = [np.zeros((4, w, 2, 8, 128), np.float32) for (w, d) in BR]
    kvs = [np.zeros((32, 1, 2, 8, 128), np.float32) for _ in BR]
    for c in range(8):
        b, half = c // 2, c % 2
        r = R[c]
        y_prompt[b, half * 2048:(half + 1) * 2048] = r["y_p"].reshape(2048, D)
        y_sample[NS * c:NS * (c + 1), 0] = r["y_s"]
        pool_s[:, NS * c:NS * (c + 1)] = r["pool_s"]
        for g in range(3):
            kvs[g][NS * c:NS * (c + 1), 0] = r["kvs%d" % g].reshape(NS, 2, 8, 128)
        if half == 1:
            pool_p[:, b] = r["pool_p"]
            for g, (w, d) in enumerate(BR):
                kvp[g][b] = r["kvp%d" % g].reshape(w, 2, 8, 128)
    return (y_prompt, y_sample, pool_p, pool_s, kvp[0], kvs[0], kvp[1], kvs[1], kvp[2], kvs[2])
```

```python
import math
from contextlib import ExitStack

import numpy as np
import concourse.bass as bass
import concourse.mybir as mybir
from concourse.bass_utils import run_bass_kernel_spmd

F32 = mybir.dt.float32
BF16 = mybir.dt.bfloat16
AF = mybir.ActivationFunctionType
ALU = mybir.AluOpType
AX = mybir.AxisListType

D = 1024
NT = 16
TS = 16
TH = 17
NS = 4
HID = 4096
EPS = 1e-6
BR = ((128, 1), (512, 4), (2048, 16))
CH = 16
C0 = CH + 128
CS = C0 + 2048
CW = CS + NS
SCALE = 128 ** -0.5


def TT(out, a, b, op):
    return lambda e: e.tensor_tensor(out, a, b, op)


def STT(out, a, s, b, op0, op1):
    return lambda e: e.scalar_tensor_tensor(out, a, s, b, op0, op1)


def TSC(out, a, s1, op0):
    return lambda e: e.tensor_scalar(out, a, s1, None, op0)


def TSC2(out, a, s1, s2, op0, op1):
    return lambda e: e.tensor_scalar(out, a, s1, s2, op0, op1)


def ACTI(out, in_, func, **kw):
    return lambda e: e.activation(out, in_, func, **kw)


def DMA(out, in_, **kw):
    return lambda e: e.dma_start(out=out, in_=in_, **kw)


def MM(items):
    def f(e):
        ins = None
        for (o, l, r, st, sp) in items:
            ins = e.matmul(o, l, r, start=st, stop=sp)
        return ins
    return f


def MMX(items):
    def f(e):
        ins = None
        for (o, l, r, st, sp) in items:
            ins = e.matmul(o, l, r, start=st, stop=sp, skip_group_check=True)
        return ins
    return f


def TRS(items, ident):
    def f(e):
        ins = None
        for (o, i) in items:
            ins = e.transpose(o, i, ident)
        return ins
    return f


def RECIP(out, in_):
    return lambda e: e.reciprocal(out, in_)


def MEMSET(ap, v):
    return lambda e: e.memset(ap, v)


def COPY(out, in_):
    return lambda e: e.tensor_copy(out, in_)


def REDUCE(out, in_, axis, op):
    return lambda e: e.tensor_reduce(out, in_, axis, op)


def CC(groups, src, dst):
    return lambda e: e.collective_compute("AllGather", ALU.bypass, replica_groups=groups, ins=[src], outs=[dst])


class Op:
    __slots__ = ("eng", "fn", "deps", "need", "is_dma", "semkey", "sig", "idx", "inc")


class Prog:
    ENGS = ("pe", "act", "dve", "pool", "sp")

    def __init__(self):
        self.ops = []
        self.lastw = {}
        self.readers = {}
        self.pending_barrier = {e: [] for e in self.ENGS}
        self.last_on = {}

    def add(self, eng, fn, reads=(), writes=(), dma=None, cc=False):
        op = Op()
        op.eng = eng
        op.fn = fn
        op.is_dma = dma is not None
        op.semkey = dma
        op.need = op.is_dma
        op.sig = None
        op.inc = 1 if (cc or dma is None) else 16
        deps = set()
        lw = self.lastw
        rd = self.readers
        for k in reads:
            w = lw.get(k)
            if w is not None:
                deps.add(w)
        for k in writes:
            w = lw.get(k)
            if w is not None:
                deps.add(w)
            r = rd.get(k)
            if r:
                deps.update(r)
        if eng == "pe" and not op.is_dma:
            deps = {d for d in deps if d.is_dma or d.eng != "pe"}
        pb = self.pending_barrier[eng]
        if pb:
            deps.update(pb)
            self.pending_barrier[eng] = []
        for d in deps:
            d.need = True
        op.deps = deps
        for k in reads:
            rd.setdefault(k, []).append(op)
        for k in writes:
            lw[k] = op
            rd[k] = []
        op.idx = len(self.ops)
        self.ops.append(op)
        self.last_on[(eng, op.is_dma, dma)] = op
        return op

    def barrier(self):
        lasts = list(self.last_on.values())
        for o in lasts:
            o.need = True
        for e in self.ENGS:
            self.pending_barrier[e] = list(lasts)
        self.lastw = {}
        self.readers = {}

    def emit(self, nc, es):
        SEM_ROT = 12000
        DMA_ROT = 700
        comp_count = {e: 0 for e in self.ENGS}
        sem_pool = {}

        def get_sem(name):
            s = sem_pool.get(name)
            if s is None:
                s = es.enter_context(nc.semaphore(name))
                sem_pool[name] = s
            return s

        totals = {}
        for op in self.ops:
            if op.is_dma:
                totals[op.semkey] = totals.get(op.semkey, 0) + 1
        dma_cnt = {}
        for op in self.ops:
            if op.is_dma:
                c = dma_cnt.get(op.semkey, 0) + 1
                dma_cnt[op.semkey] = c
                if op.semkey.startswith("@"):
                    assert totals[op.semkey] <= DMA_ROT
                    op.sig = (get_sem("dg_" + op.semkey[1:]), op.inc * totals[op.semkey])
                else:
                    gen = (c - 1) // DMA_ROT
                    op.sig = (get_sem("d_%s_%d" % (op.semkey, gen)), op.inc * (c - gen * DMA_ROT))
            elif op.need:
                c = comp_count[op.eng] + 1
                comp_count[op.eng] = c
                gen = (c - 1) // SEM_ROT
                op.sig = (get_sem("c_%s_%d" % (op.eng, gen)), c - gen * SEM_ROT)
        self.n_sems = len(sem_pool)
        final_dma = {}
        for op in self.ops:
            if op.is_dma:
                k = id(op.sig[0])
                if k not in final_dma or final_dma[k][1] < op.sig[1]:
                    final_dma[k] = op.sig
        per_eng = {e: [] for e in self.ENGS}
        for op in self.ops:
            per_eng[op.eng].append(op)
        self.n_waits = 0
        block = es.enter_context(nc.Block())

        def run(engname, e):
            waited = {}
            for op in per_eng[engname]:
                need = {}
                for d in op.deps:
                    s, v = d.sig
                    k = id(s)
                    if waited.get(k, 0) >= v:
                        continue
                    if k not in need or need[k][1] < v:
                        need[k] = (s, v)
                for k, (s, v) in need.items():
                    e.wait_ge(s, v)
                    waited[k] = v
                    self.n_waits += 1
                ins = op.fn(e)
                if op.sig is not None:
                    if op.is_dma and op.inc == 1:
                        ins.then_inc(op.sig[0])
                    else:
                        ins.then_inc(op.sig[0], op.inc)
            if engname == "sp":
                for k, (s, v) in final_dma.items():
                    if waited.get(k, 0) < v:
                        e.wait_ge(s, v)

        @block.tensor
        def _(e):
            run("pe", e)

        @block.scalar
        def _(e):
            run("act", e)

        @block.vector
        def _(e):
            run("dve", e)

        @block.gpsimd
        def _(e):
            run("pool", e)

        @block.sync
        def _(e):
            run("sp", e)


def t5_bucket_np(dist):
    dist = np.asarray(dist, np.int64)
    max_exact = 16
    df = np.maximum(dist, 1).astype(np.float32)
    large = max_exact + (np.log(df / np.float32(max_exact)) / np.float32(math.log(2048 / max_exact))
                         * np.float32(32 - max_exact)).astype(np.int32)
    large = np.minimum(large, 31)
    return np.where(dist < max_exact, dist, large).astype(np.int64)


def toeplitz_index():
    k = np.arange(128)[:, None]
    c = np.arange(256)[None, :]
    j = np.where(c < 128, 128 + c - k, c - 128 - k)
    valid = np.where(c < 128, k >= c, (c - 128) >= k)
    j = np.where(valid, j, 0)
    return j, valid.astype(np.float32)


def build_program(stop_after=None, debug=False):
    nc = bass.Bass("TRN2", target_bir_lowering=False)
    P = Prog()

    def din(name, shape, dt=F32):
        return nc.dram_tensor(name, list(shape), dt, kind="ExternalInput").ap()

    def dout(name, shape, dt=F32):
        return nc.dram_tensor(name, list(shape), dt, kind="ExternalOutput").ap()

    def dscr(name, shape, dt=BF16):
        return nc.dram_tensor(name, list(shape), dt).ap()

    xin = din("xin", [18, 128, D])
    state = din("state", [2, 60, D])
    cache = [din("cache0", [NS, 128, 2048]), din("cache1", [NS, 512, 2048]), din("cache2", [NS, 2048, 2048])]
    gvec = din("gvec", [12, D])
    pool_w = din("pool_w", [2, 4, 256, 256])
    mlp_in = din("mlp_in", [4, D, HID])
    mlp_out = din("mlp_out", [4, HID, D])
    w_kv = din("w_kv", [D, 6144])
    w_q = din("w_q", [2, D, 3072])
    w_o = din("w_o", [2, D, D])
    c_ident = din("c_ident", [128, 128])
    c_apool = din("c_apool", [2, 4, 2, 128, 128])
    c_selh = din("c_selh", [NS, 16])
    c_sel = din("c_sel", [60, 16])
    c_bias3 = din("c_bias3", [24, 128, 768])
    c_biasS = din("c_biasS", [3, 128, 8])
    c_bias0 = din("c_bias0", [3, 8])
    y_p = dout("y_p", [NT, 128, D])
    y_s = dout("y_s", [NS, D])
    pool_p = dout("pool_p", [2, 15, D])
    pool_s = dout("pool_s", [2, NS, 15, D])
    kvp = [dout("kvp0", [128, 2048]), dout("kvp1", [512, 2048]), dout("kvp2", [2048, 2048])]
    kvs = [dout("kvs%d" % g, [NS, 2048]) for g in range(3)]
    dbg = dout("dbg", [18, 128, D]) if debug else None
    kT_d = dscr("kT_d", [24, 128, 2048])
    v_d = dscr("v_d", [24, 128, 2048])
    exp_src0 = dscr("exp_src0", [1024, 256])
    exp_dst0 = dscr("exp_dst0", [2048, 256])
    exp_src1 = dscr("exp_src1", [1024, 1024])
    exp_dst1 = dscr("exp_dst1", [2048, 1024])
    exp_src2 = [dscr("exp_src2_%d" % i, [256, 4096]) for i in range(4)]
    exp_dst2 = [dscr("exp_dst2_%d" % i, [512, 4096]) for i in range(4)]
    ot_d = dscr("ot_d", [8, 128, 2048])

    es = ExitStack()
    ARENA_BYTES = 212000
    arena = es.enter_context(nc.sbuf_tensor("arena", [128, ARENA_BYTES // 2], BF16))
    pos = [0]

    def alloc(nbytes):
        off = (pos[0] + 63) // 64 * 64
        pos[0] = off + nbytes
        assert pos[0] <= ARENA_BYTES, ("arena overflow", pos[0])
        return off

    def vbf(off, n):
        return arena[:, off // 2: off // 2 + n]

    def vf32(off, n):
        return arena[:, off // 2: off // 2 + 2 * n].bitcast(F32)

    X = vf32(alloc(18 * D * 4), 18 * D).rearrange("p (t d) -> p t d", t=18)
    HT = vbf(alloc(8 * CW * 2), 8 * CW).rearrange("p (k c) -> p k c", k=8)
    WALL = vbf(alloc(4 * 8192), 4 * 4096)
    WSL = [WALL[:, i * 4096:(i + 1) * 4096] for i in range(4)]
    IDENT = vbf(alloc(256), 128)
    ONESB = vbf(alloc(256), 128)
    ONES32 = vf32(alloc(512), 128)
    GCOL = vf32(alloc(12 * 8 * 4), 96).rearrange("p (v k) -> p v k", v=12)
    SS = vf32(alloc(32 * 4), 32)
    RS = vf32(alloc(32 * 4), 32)
    FLAG = vf32(alloc(64), 2)
    TBL = vf32(alloc(4096), 1024)
    HTOKALL = vbf(alloc(4096), 2048)
    HTOK = [HTOKALL[:, 0:1024], HTOKALL[:, 1024:2048]]
    YALL = vf32(alloc(8192), 2048)
    YST = [YALL[:, 0:1024], YALL[:, 1024:2048]]
    PH0 = alloc(0)

    def phase_alloc():
        st = [PH0]

        def a(nbytes):
            off = (st[0] + 63) // 64 * 64
            st[0] = off + nbytes
            assert st[0] <= ARENA_BYTES, ("phase overflow", st[0] - PH0, ARENA_BYTES - PH0)
            return off
        return a

    PSALL = es.enter_context(nc.psum_tensor("psall", [128, 4096], F32))
    PS = [PSALL[:, i * 512:(i + 1) * 512] for i in range(8)]

    def psb(b):
        return PS[b][:, :].bitcast(BF16)

    P.add("pool", DMA(IDENT, c_ident), writes=[("ident",)], dma="ident")
    P.add("sp", DMA(GCOL, gvec.rearrange("v (k p) -> p v k", p=128), allow_slow_non_contiguous=True),
          writes=[("gcol",)], dma="@setup2")
    P.add("dve", MEMSET(ONESB, 1.0), writes=[("onesb",)])
    P.add("dve", MEMSET(ONES32, 1.0), writes=[("ones32",)])
    P.add("dve", MEMSET(HT[:, :, 0:CH], 0.0), writes=[("hT", "Z")])
    def load_x(tiles, extra_reads=()):
        for t in tiles:
            P.add("sp", DMA(X[:, t, :], xin[t]), reads=list(extra_reads), writes=[("x", t)], dma="xin%d" % t)

    load_x([TH, 0, 1])

    def tile_cols(t):
        if t == TH:
            return CH, 128
        if t == TS:
            return CS, NS
        return C0 + 128 * t, 128

    def hkey(t):
        return ("hT", t)

    tr_ctr = [0]

    def norm_a(t, gidx, h32_tbl=None, h32_slot=0):
        buf = tr_ctr[0] % 2
        tr_ctr[0] += 1
        htok = HTOK[buf]
        bank = 6 + buf
        c, n = tile_cols(t)
        P.add("act", ACTI(htok, X[:, t, :], AF.Square, accum_out=SS[:, t:t + 1]),
              reads=[("x", t)], writes=[("htok", buf), ("ss", t)])
        P.add("act", ACTI(RS[:, t:t + 1], SS[:, t:t + 1], AF.Sqrt, bias=EPS, scale=1.0 / D),
              reads=[("ss", t)], writes=[("rs", t)])
        P.add("dve", RECIP(RS[:, t:t + 1], RS[:, t:t + 1]), reads=[("rs", t)], writes=[("rs", t)])
        P.add("dve", TSC(htok, X[:, t, :], RS[:, t:t + 1], ALU.mult),
              reads=[("x", t), ("rs", t)], writes=[("htok", buf)])
        if h32_tbl is not None:
            P.add("dve", STT(YST[h32_slot], X[:, t, :], RS[:, t:t + 1], h32_tbl, ALU.mult, ALU.mult),
                  reads=[("x", t), ("rs", t), ("gtbl",)], writes=[("yst", h32_slot)])
        pv = psb(bank)
        P.add("pe", TRS([(pv[:, k * 128:(k + 1) * 128], htok[:, k * 128:(k + 1) * 128]) for k in range(8)], IDENT),
              reads=[("htok", buf), ("ident",)], writes=[("ps", bank)])

        def evac():
            P.add("dve", TT(HT[:, :, c:c + n], pv.rearrange("p (k n) -> p k n", k=8)[:, :, 0:n],
                            GCOL[:, gidx, :].unsqueeze(2).broadcast_to([128, 8, n]), ALU.mult),
                  reads=[("gcol",)], writes=[("ps", bank), hkey(t)])
        return evac

    def norm_tiles(tiles, gidx, hooks=None):
        pend = None
        for t in tiles:
            hk = hooks.get(t) if hooks else None
            ev = norm_a(t, gidx, *(hk[0] if hk else ()))
            if hk:
                hk[1]()
            if pend is not None:
                pend()
            pend = ev
        if pend is not None:
            pend()

    def norm_tile(t, gidx):
        norm_a(t, gidx)()

    def load_w(slot_ap, src_ap, keys, semkey):
        P.add("pool", DMA(slot_ap, src_ap), writes=keys, dma=semkey)

    def mlp_layer(l, batches):
        pa = phase_alloc()
        HIDT = [vbf(pa(4096), 2048).rearrange("p (c n) -> p c n", c=4) for _ in range(2)]
        SQ = [vf32(pa(2048), 512) for _ in range(2)]
        NHB = 8
        steps = [(hb, bi) for hb in range(NHB) for bi in range(len(batches))]

        def wviews(hb):
            par = hb % 2
            win = WSL[2 * par].rearrange("p (k n) -> p k n", k=8)
            wout = WSL[2 * par + 1].rearrange("p (c n) -> p c n", c=4)
            return par, win, wout

        def issue_load(hb):
            par, win, wout = wviews(hb)
            load_w(win, mlp_in[l, :, hb * 512:(hb + 1) * 512].rearrange("(k p) n -> p k n", p=128),
                   [("wsl", 2 * par)], "w%d" % (2 * par))
            load_w(wout, mlp_out[l, hb * 512:(hb + 1) * 512, :].rearrange("(c p) n -> p c n", p=128),
                   [("wsl", 2 * par + 1)], "w%d" % (2 * par + 1))

        ctr = {"ps": 0, "out": 0}

        def emit_in(i):
            hb, bi = steps[i]
            par, win, wout = wviews(hb)
            tiles, c, n = batches[bi]
            hbuf = i % 2
            hid = HIDT[hbuf]
            for m in range(4):
                bank = ctr["ps"] % 2
                ctr["ps"] += 1
                P.add("pe", MM([(PS[bank][:, 0:n], win[:, k, m * 128:(m + 1) * 128], HT[:, k, c:c + n], k == 0, k == 7)
                                for k in range(8)]),
                      reads=[("wsl", 2 * par)] + [hkey(t) for t in tiles], writes=[("ps", bank)])
                sq = SQ[bank]
                P.add("act", ACTI(sq[:, 0:n], PS[bank][:, 0:n], AF.Square), reads=[("ps", bank)], writes=[("sq", bank)])
                P.add("dve", STT(hid[:, m, 0:n], PS[bank][:, 0:n], 0.0, sq[:, 0:n], ALU.is_gt, ALU.mult),
                      reads=[("ps", bank), ("sq", bank)], writes=[("hid", hbuf, m)])

        def emit_out(i):
            hb, bi = steps[i]
            par, win, wout = wviews(hb)
            tiles, c, n = batches[bi]
            hbuf = i % 2
            hid = HIDT[hbuf]
            for ti, t in enumerate(tiles):
                pb = 2 + 2 * (ctr["out"] % 2)
                ctr["out"] += 1
                nr = NS if t == TS else 128
                o = 0 if t == TS else ti * 128
                P.add("pe", MM([(PS[pb + half][0:nr, :], hid[:, hc, o:o + nr], wout[:, hc, half * 512:(half + 1) * 512],
                                 hc == 0, hc == 3) for hc in range(4) for half in range(2)]),
                      reads=[("wsl", 2 * par + 1)] + [("hid", hbuf, m) for m in range(4)],
                      writes=[("ps", pb), ("ps", pb + 1)])
                for half in range(2):
                    xs = X[0:nr, t, half * 512:(half + 1) * 512]
                    P.add("dve", TT(xs, xs, PS[pb + half][0:nr, :], ALU.add),
                          reads=[("ps", pb + half), ("x", t)], writes=[("x", t)])

        issue_load(0)
        for i in range(len(steps)):
            hb, bi = steps[i]
            emit_in(i)
            if i >= 1:
                emit_out(i - 1)
            if bi == 0 and hb + 1 < NHB:
                issue_load(hb + 1)
        emit_out(len(steps) - 1)

    def pool_layer(l):
        pa = phase_alloc()
        POOLED = [vbf(pa(8 * 128 * 2), 8 * 128).rearrange("p (k n) -> p k n", k=8) for _ in range(2)]
        AG = vbf(pa(8 * 128 * 2), 8 * 128).rearrange("p (g s n) -> p g s n", g=4, s=2)
        A0 = vbf(pa(8 * 128 * 2), 8 * 128).rearrange("p (g s n) -> p g s n", g=4, s=2)
        STT_ = vf32(pa(D * 4), D)
        SEL = vf32(pa(16 * 4), 16)
        SELH = vf32(pa(16 * 4), 16)
        WP = vbf(pa(8 * 256 * 2), 8 * 256).rearrange("p (g n) -> p g n", g=8)
        WPS = vbf(pa(8 * 256 * 2), 8 * 256).rearrange("p (g n) -> p g n", g=8)
        GTB = vf32(pa(4096), 1024)
        PS_S = vbf(pa(8 * NS * 2), 8 * NS).rearrange("p (k n) -> p k n", k=8)
        HT3 = [vbf(pa(2048), 1024) for _ in range(3)]
        gk = "@pl%d" % l
        P.add("sp", DMA(STT_[0:60, :], state[l]), writes=[("stt",)], dma=gk)
        P.add("sp", DMA(SEL[0:60, :], c_sel), writes=[("sel",)], dma=gk)
        P.add("sp", DMA(SELH[0:NS, :], c_selh), writes=[("selh",)], dma=gk)
        P.add("sp", DMA(TBL, gvec[10 + l].partition_broadcast(128)), writes=[("tbl",)], dma=gk)
        P.add("sp", DMA(GTB, gvec[l].partition_broadcast(128)), writes=[("gtbl",)], dma=gk)
        P.add("pool", DMA(WP, pool_w[l].rearrange("g (kk p) n -> p (g kk) n", p=128)), writes=[("wp",)], dma="wp")
        P.add("pool", DMA(AG, c_apool[0].rearrange("g s p n -> p g s n")), writes=[("ag",)], dma="ag")
        P.add("pool", DMA(A0, c_apool[1].rearrange("g s p n -> p g s n")), writes=[("a0",)], dma="a0")
        P.add("sp", DMA(pool_s[l, :, 0:14, :], state[l].rearrange("(s r) d -> s r d", r=15)[:, 1:15, :]),
              dma="out_psa%d" % l)
        if l == 0:
            load_x(list(range(2, NT)) + [TS], extra_reads=[("wp",), ("ag",), ("a0",), ("tbl",), ("gtbl",)])

        def fold_scale():
            for kk in range(2):
                P.add("dve", TT(WPS.rearrange("p (g k) n -> p g k n", k=2)[:, :, kk, :],
                                WP.rearrange("p (g k) n -> p g k n", k=2)[:, :, kk, :],
                                TBL.rearrange("p (g n) -> p g n", g=4), ALU.mult),
                      reads=[("wp",), ("tbl",)], writes=[("wps", kk)])
        tiles = [TH] + list(range(NT))
        pend = []
        pctr = [0]
        for i, t in enumerate(tiles):
            if i == 1:
                fold_scale()
            buf = i % 3
            htok = HT3[buf]
            hprev = HT3[(i - 1) % 3]
            P.add("act", ACTI(htok, X[:, t, :], AF.Square, accum_out=SS[:, t:t + 1]),
                  reads=[("x", t)], writes=[("htok", buf), ("ss", t)])
            P.add("act", ACTI(RS[:, t:t + 1], SS[:, t:t + 1], AF.Sqrt, bias=EPS, scale=1.0 / D),
                  reads=[("ss", t)], writes=[("rs", t)])
            P.add("dve", RECIP(RS[:, t:t + 1], RS[:, t:t + 1]), reads=[("rs", t)], writes=[("rs", t)])
            P.add("dve", TSC(htok, X[:, t, :], RS[:, t:t + 1], ALU.mult),
                  reads=[("x", t), ("rs", t)], writes=[("htok", buf)])
            if t == NT - 1:
                P.add("dve", STT(YST[0], X[:, t, :], RS[:, t:t + 1], GTB, ALU.mult, ALU.mult),
                      reads=[("x", t), ("rs", t), ("gtbl",)], writes=[("yst", 0)])
                P.add("sp", DMA(pool_p[l], YST[0][113:128, :]), reads=[("yst", 0)], dma="out_pp%d" % l)
            Am = A0 if t == 0 else AG
            pb = 2 + 2 * (i % 2)
            mm = []
            for c in range(8):
                g = c // 2
                o = PS[pb + c // 4][:, (c % 4) * 128:(c % 4) * 128 + 128]
                has_prev = (i > 0)
                mm.append((o, htok[:, c * 128:(c + 1) * 128], Am[:, g, 1, :], True, not has_prev))
                if has_prev:
                    mm.append((o, hprev[:, c * 128:(c + 1) * 128], Am[:, g, 0, :], False, True))
            P.add("pe", MM(mm), reads=[("htok", buf), ("htok", (i - 1) % 3), ("ag",), ("a0",)],
                  writes=[("ps", pb), ("ps", pb + 1)])

            def tail(t=t, pb=pb, i=i):
                pooled = POOLED[i % 2]
                for half in range(2):
                    P.add("dve", TT(pooled[:, 4 * half:4 * half + 4, :],
                                    PS[pb + half][:, :].rearrange("p (k n) -> p k n", k=4),
                                    GCOL[:, l, 4 * half:4 * half + 4].unsqueeze(2).broadcast_to([128, 4, 128]), ALU.mult),
                          reads=[("gcol",)], writes=[("ps", pb + half), ("pooled", i % 2, half)])
                ob = 6 if pctr[0] % 2 == 0 else 0
                pctr[0] += 1
                P.add("pe", MM([(PS[ob + g // 2][:, (g % 2) * 256:(g % 2) * 256 + 256],
                                 pooled[:, 2 * g + kk, :], WPS[:, 2 * g + kk, :], kk == 0, kk == 1)
                                for g in range(4) for kk in range(2)]),
                      reads=[("pooled", i % 2, 0), ("pooled", i % 2, 1), ("wps", 0), ("wps", 1)],
                      writes=[("ps", ob), ("ps", ob + 1)])
                for half in range(2):
                    xs = X[:, t, half * 512:(half + 1) * 512]
                    P.add("dve", TT(xs, xs, PS[ob + half][:, :], ALU.add), reads=[("x", t)],
                          writes=[("ps", ob + half), ("x", t)])
            pend.append(tail)
            if len(pend) > 1:
                pend.pop(0)()
        while pend:
            pend.pop(0)()
        t = TS
        P.add("act", ACTI(HTOK[0], X[:, t, :], AF.Square, accum_out=SS[:, t:t + 1]),
              reads=[("x", t)], writes=[("htok", 0), ("ss", t)])
        P.add("act", ACTI(RS[:, t:t + 1], SS[:, t:t + 1], AF.Sqrt, bias=EPS, scale=1.0 / D),
              reads=[("ss", t)], writes=[("rs", t)])
        P.add("dve", RECIP(RS[:, t:t + 1], RS[:, t:t + 1]), reads=[("rs", t)], writes=[("rs", t)])
        P.add("dve", STT(YST[1], X[:, t, :], RS[:, t:t + 1], GTB, ALU.mult, ALU.mult),
              reads=[("x", t), ("rs", t), ("gtbl",)], writes=[("yst", 1)])
        P.add("sp", DMA(pool_s[l, :, 14, :], YST[1][0:NS, :]), reads=[("yst", 1)], dma="out_psb%d" % l)
        mm = []
        for c in range(8):
            g = c // 2
            o = PS[4][:, c * NS:(c + 1) * NS]
            mm.append((o, STT_[0:60, c * 128:(c + 1) * 128], SEL[0:60, g * NS:(g + 1) * NS], True, False))
            mm.append((o, YST[1][0:NS, c * 128:(c + 1) * 128], SELH[0:NS, g * NS:(g + 1) * NS], False, True))
        P.add("pe", MM(mm), reads=[("stt",), ("sel",), ("selh",), ("yst", 1)], writes=[("ps", 4)])
        P.add("dve", COPY(PS_S, PS[4][:, 0:8 * NS].rearrange("p (k n) -> p k n", k=8)), reads=[],
              writes=[("ps", 4), ("pss",)])
        P.add("pe", MM([(PS[2 + g // 2][0:NS, (g % 2) * 256:(g % 2) * 256 + 256], PS_S[:, 2 * g + kk, :],
                         WPS[:, 2 * g + kk, :], kk == 0, kk == 1) for g in range(4) for kk in range(2)]),
              reads=[("pss",), ("wps", 0), ("wps", 1)], writes=[("ps", 2), ("ps", 3)])
        for half in range(2):
            xs = X[0:NS, TS, half * 512:(half + 1) * 512]
            P.add("dve", TT(xs, xs, PS[2 + half][0:NS, :], ALU.add), reads=[("x", TS)],
                  writes=[("ps", 2 + half), ("x", TS)])

    def mlp_phase(l, with_halo):
        tiles = ([TH] if with_halo else []) + list(range(NT)) + [TS]
        norm_tiles(tiles, 4 + l)
        batches = []
        if with_halo:
            batches.append(([TH], CH, 128))
        for b in range(4):
            batches.append((list(range(4 * b, 4 * b + 4)), C0 + 512 * b, 512))
        batches.append(([TS], CS, NS))
        mlp_layer(l, batches)

    def dump_dbg():
        if dbg is not None:
            for t in range(18):
                P.add("sp", DMA(dbg[t], X[:, t, :]), reads=[("x", t)], dma="dbg")

    def colset(k, g, nb):
        d = BR[g][1]
        if d == 1:
            return HT[:, k, C0 + nb * 512:C0 + (nb + 1) * 512]
        if d == 4:
            return HT[:, k, C0 + nb:C0 + nb + 4 * 511 + 1:4]
        return HT[:, k, C0:C0 + 2048].rearrange("p (u r) -> p r u", r=16)[:, 4 * nb:4 * nb + 4, :]

    def chunkset(k, g, ci):
        d = BR[g][1]
        if d == 1:
            return HT[:, k, C0 + ci * 128:C0 + (ci + 1) * 128]
        if d == 4:
            r, jb = ci // 4, ci % 4
            s0 = C0 + 512 * jb + r
            return HT[:, k, s0:s0 + 4 * 127 + 1:4]
        s0 = C0 + ci
        return HT[:, k, s0:s0 + 16 * 127 + 1:16]

    def halo_chunk0(g):
        return (0, 1, 5)[g]

    GROUPS = [[0, 1], [2, 3], [4, 5], [6, 7]]

    def halo_src(g, h, kv_i):
        if g == 0:
            base = exp_dst0[h * 128:(h + 1) * 128, :]
        elif g == 1:
            base = exp_dst1[h * 128:(h + 1) * 128, :]
        else:
            base = exp_dst2[h // 2][(h % 2) * 128:(h % 2 + 1) * 128, :]
        return base.rearrange("p (c x) -> p c x", x=256)[:, :, kv_i * 128:(kv_i + 1) * 128]

    def kv_phase():
        pa = phase_alloc()
        KTS = [vbf(pa(4096), 2048) for _ in range(2)]
        VST = [vbf(pa(2048), 1024).rearrange("p (h x) -> p h x", h=8) for _ in range(2)]
        norm_tiles(list(range(NT)) + [TS], 8)
        ectr = {"kts": 0, "vst": 0, "ps": 0, "pt": 0, "yst": 0}

        def exp_view(g, h):
            if g == 0:
                t = exp_src0[h * 128:(h + 1) * 128, :]
            elif g == 1:
                t = exp_src1[h * 128:(h + 1) * 128, :]
            else:
                t = exp_src2[h // 2][(h % 2) * 128:(h % 2 + 1) * 128, :]
            return t.rearrange("p (c x) -> p c x", x=256)

        for g in (2, 1, 0):
            d = BR[g][1]
            R = d
            NB = 16 // R
            wK = WALL[:, 0:8192].rearrange("p (k n) -> p k n", k=8)
            wV = WALL[:, 8192:16384].rearrange("p (k n) -> p k n", k=8)
            load_w(wK, w_kv[:, g * 2048:g * 2048 + 1024].rearrange("(k p) n -> p k n", p=128),
                   [("wsl", 0), ("wsl", 1)], "w0")
            load_w(wV, w_kv[:, g * 2048 + 1024:g * 2048 + 2048].rearrange("(k p) n -> p k n", p=128),
                   [("wsl", 2), ("wsl", 3)], "w2")
            for h in range(8):
                kb = ectr["kts"] % 2
                ectr["kts"] += 1
                kts = KTS[kb]
                for nb in range(4):
                    bank = ectr["ps"] % 2
                    ectr["ps"] += 1
                    P.add("pe", MM([(PS[bank][:, :], wK[:, k, h * 128:(h + 1) * 128],
                                     HT[:, k, C0 + nb * 512:C0 + (nb + 1) * 512], k == 0, k == 7) for k in range(8)]),
                          reads=[("wsl", 0), ("wsl", 1)] + [hkey(t) for t in range(4 * nb, 4 * nb + 4)],
                          writes=[("ps", bank)])
                    uw = 512 // d
                    P.add("act", ACTI(kts.rearrange("p (r u) -> p r u", r=d)[:, :, nb * uw:(nb + 1) * uw],
                                      PS[bank][:, :].rearrange("p (u r) -> p r u", r=d), AF.Copy),
                          reads=[], writes=[("ps", bank), ("kts", kb, nb)])
                rk = [("kts", kb, nb) for nb in range(4)]
                P.add("sp", DMA(kT_d[g * 8 + h], kts), reads=rk, writes=[("kT_d", g * 8 + h)], dma="kts%d" % kb)
                src = kts.rearrange("p (r u) -> p r u", r=R)[:, :, (NB - 1) * 128:NB * 128]
                P.add("sp", DMA(exp_view(g, h)[:, :, 0:128], src), reads=rk, writes=[("exps", g, h, "k")],
                      dma="kte%d" % kb)
            for ci in range(16):
                pb = 2 + 2 * (ectr["pt"] % 2)
                ectr["pt"] += 1
                vb = ectr["vst"] % 2
                ectr["vst"] += 1
                P.add("pe", MM([(PS[pb + half][:, :], chunkset(k, g, ci), wV[:, k, half * 512:(half + 1) * 512],
                                 k == 0, k == 7) for half in range(2) for k in range(8)]),
                      reads=[("wsl", 2), ("wsl", 3)] + [hkey(t) for t in range(NT)],
                      writes=[("ps", pb), ("ps", pb + 1)])
                for half in range(2):
                    P.add("dve", COPY(VST[vb][:, 4 * half:4 * half + 4, :],
                                      PS[pb + half][:, :].rearrange("p (h x) -> p h x", h=4)),
                          reads=[], writes=[("ps", pb + half), ("vst", vb, half)])
                rk = [("vst", vb, 0), ("vst", vb, 1)]
                P.add("sp", DMA(v_d[g * 8:(g + 1) * 8].rearrange("h p (c x) -> p h c x", x=128)[:, :, ci, :], VST[vb]),
                      reads=rk, writes=[("v_d", g, ci)], dma="vst%d" % vb)
                r, jb = ci // NB, ci % NB
                if jb == NB - 1:
                    if g < 2:
                        srcv = (exp_src0 if g == 0 else exp_src1).rearrange("(h p) (c x) -> p h c x", p=128, x=256)
                        P.add("sp", DMA(srcv[:, :, r, 128:256], VST[vb]), reads=rk, writes=[("exps", g, "v", r)],
                              dma="vse%d" % vb)
                    else:
                        for i in range(4):
                            srcv = exp_src2[i].rearrange("(h p) (c x) -> p h c x", p=128, x=256)
                            P.add("sp", DMA(srcv[:, :, r, 128:256], VST[vb][:, 2 * i:2 * i + 2, :]), reads=rk,
                                  writes=[("exps", g, "v", r, i)], dma="vse%d_%d" % (vb, i))
            if g < 2:
                rk = [("exps", g, h, "k") for h in range(8)] + [("exps", g, "v", r) for r in range(R)]
                P.add("pool", CC(GROUPS, (exp_src0 if g == 0 else exp_src1).opt(), (exp_dst0 if g == 0 else exp_dst1).opt()),
                      reads=rk, writes=[("expd", g, h) for h in range(8)], dma="cc%d" % g, cc=True)
            else:
                for i in range(4):
                    rk = [("exps", g, h, "k") for h in (2 * i, 2 * i + 1)] + [("exps", g, "v", r, i) for r in range(R)]
                    P.add("pool", CC(GROUPS, exp_src2[i].opt(), exp_dst2[i].opt()),
                          reads=rk, writes=[("expd", g, h) for h in (2 * i, 2 * i + 1)], dma="cc2_%d" % i, cc=True)
            nt_out = (1, 4, 16)[g]
            for t in list(range(NT - nt_out, NT)) + [TS]:
                c, n = tile_cols(t)
                for kv_i, wsel in enumerate((wK, wV)):
                    pb = 2 + 2 * (ectr["pt"] % 2)
                    ectr["pt"] += 1
                    slot = ectr["yst"] % 2
                    ectr["yst"] += 1
                    P.add("pe", MM([(PS[pb + half][0:n, :], HT[:, k, c:c + n], wsel[:, k, half * 512:(half + 1) * 512],
                                     k == 0, k == 7) for half in range(2) for k in range(8)]),
                          reads=[("wsl", 2 * kv_i), ("wsl", 2 * kv_i + 1), hkey(t)],
                          writes=[("ps", pb), ("ps", pb + 1)])
                    for half in range(2):
                        P.add("act", ACTI(YST[slot][0:n, half * 512:(half + 1) * 512], PS[pb + half][0:n, :], AF.Copy),
                              reads=[], writes=[("ps", pb + half), ("yst", slot, half)])
                    if t == TS:
                        dst = kvs[g][:, kv_i * 1024:(kv_i + 1) * 1024]
                        wk = [("kvs", g)]
                    else:
                        r0 = (t - (NT - nt_out)) * 128
                        dst = kvp[g][r0:r0 + 128, kv_i * 1024:(kv_i + 1) * 1024]
                        wk = []
                    P.add("sp", DMA(dst, YST[slot][0:n, :]), reads=[("yst", slot, 0), ("yst", slot, 1)], writes=wk,
                          dma="out_kv%d" % slot)

    def attn_layer(l):
        lb = l - 2
        pa = phase_alloc()
        KTG, VVG = [], []
        for g in range(3):
            R = BR[g][1]
            NB = 16 // R
            n = R * (NB + 1) * 128
            KTG.append(vbf(pa(n * 2), n).rearrange("p (r c x) -> p r c x", r=R, x=128))
            VVG.append(vbf(pa(n * 2), n).rearrange("p (r c x) -> p r c x", r=R, x=128))
        OACC = vf32(pa(2 * 2048 * 4), 2 * 2048).rearrange("p (a n) -> p a n", a=2)
        QT = [YALL[:, 0:1024].bitcast(BF16), YALL[:, 1024:2048].bitcast(BF16)]
        WQ = [WALL[:, 8192 + i * 1024:8192 + (i + 1) * 1024].rearrange("p (k n) -> p k n", k=8) for i in range(2)]
        o3 = 8192 + 2048
        TBS = [WALL[:, o3 + i * 768:o3 + (i + 1) * 768] for i in range(2)]
        o3 += 2 * 768
        PPB = [WALL[:, o3 + i * 1024:o3 + (i + 1) * 1024] for i in range(3)]
        o3 += 3 * 1024
        QTS = WALL[:, o3:o3 + 96].rearrange("p (a s) -> p a s", s=NS)
        o3 += 96
        assert o3 <= 16384
        OT2 = [TBL.bitcast(BF16), TBL.bitcast(BF16)]
        DTMP = [HTOKALL.bitcast(F32)[:, 0:512], HTOKALL.bitcast(F32)[:, 512:1024]]
        WO = WALL[:, 0:8192].rearrange("p (h n) -> p h n", h=8)
        load_w(WO, w_o[lb].rearrange("(h p) n -> p h n", p=128), [("wsl", 0), ("wsl", 1)], "w0")
        ctr = {"unit": 0, "ps": 0, "pp": 0, "ud": 0, "sring": 0}
        items = [(h, g) for h in range(8) for g in range(3)]
        DLAG = 1
        deferred = []

        def pop_deferred(maxlen):
            while len(deferred) > maxlen:
                deferred.pop(0)()

        def issue_loads(idx):
            h, g = items[idx]
            gh = g * 8 + h
            R = BR[g][1]
            NB = 16 // R
            par = idx % 2
            P.add("pool", DMA(WQ[par], w_q[lb][:, g * 1024 + h * 128:g * 1024 + (h + 1) * 128]
                              .rearrange("(k p) n -> p k n", p=128)), writes=[("wq", par)], dma="wq%d" % par)
            P.add("pool", DMA(TBS[par], c_bias3[gh]), writes=[("tbs", par)], dma="tbs%d" % par)
            P.add("sp", DMA(KTG[g][:, :, 1:NB + 1, :], kT_d[gh].rearrange("p (r c x) -> p r c x", r=R, x=128)),
                  reads=[("kT_d", gh)], writes=[("kt", g, "own")], dma="ktk%d" % g)
            P.add("sp", DMA(KTG[g][:, :, 0, :], halo_src(g, h, 0)), reads=[("expd", g, h)],
                  writes=[("kt", g, "h")], dma="ktk%d" % g)
            P.add("sp", DMA(VVG[g][:, :, 1:NB + 1, :], v_d[gh].rearrange("p (r c x) -> p r c x", r=R, x=128)),
                  reads=[("v_d", g, ci) for ci in range(16)], writes=[("vv", g, "own")], dma="ktv%d" % g)
            P.add("sp", DMA(VVG[g][:, :, 0, :], halo_src(g, h, 1)), reads=[("expd", g, h)],
                  writes=[("vv", g, "h")], dma="ktv%d" % g)

        def qproj_part(idx, nb):
            h, g = items[idx]
            gh = g * 8 + h
            d = BR[g][1]
            par = idx % 2
            wq = WQ[par]
            qt = QT[par]
            qt3 = qt.rearrange("p (r u) -> p r u", r=d)
            bank = 0
            if nb < 4:
                P.add("pe", MM([(PS[bank][:, :], wq[:, k, :], HT[:, k, C0 + nb * 512:C0 + (nb + 1) * 512], k == 0, k == 7)
                                for k in range(8)]),
                      reads=[("wq", par)] + [hkey(t) for t in range(4 * nb, 4 * nb + 4)], writes=[("ps", bank)])
                uw = 512 // d
                P.add("act", ACTI(qt3[:, :, nb * uw:(nb + 1) * uw], PS[bank][:, :].rearrange("p (u r) -> p r u", r=d),
                                  AF.Copy, scale=SCALE), reads=[], writes=[("ps", bank), ("qt", par, nb)])
            else:
                P.add("pe", MM([(PS[bank][:, 0:NS], wq[:, k, :], HT[:, k, CS:CS + NS], k == 0, k == 7) for k in range(8)]),
                      reads=[("wq", par), hkey(TS)], writes=[("ps", bank)])
                P.add("act", ACTI(QTS[:, gh, :], PS[bank][:, 0:NS], AF.Copy, scale=SCALE),
                      reads=[], writes=[("ps", bank), ("qts", gh)])

        def qproj(idx):
            for nb in range(5):
                qproj_part(idx, nb)

        def make_pv(g, pv_fn, den_fn, oa, pi):
            def f():
                ui = ctr["ud"] % 2
                ctr["ud"] += 1
                ub, db = 4 + 2 * ui, 5 + 2 * ui
                U, DN = PS[ub], PS[db]
                UDv = PSALL[:, ub * 512:(ub + 2) * 512].rearrange("p (a n) -> p a n", a=2)
                P.add("pe", MMX(pv_fn(U) + den_fn(DN)),
                      reads=[("vv", g, "own"), ("vv", g, "h"), ("pp", pi, 0), ("pp", pi, 1), ("onesb",)],
                      writes=[("ps", ub), ("ps", db)])
                if g == 0:
                    P.add("dve", COPY(oa, UDv), reads=[], writes=[("ps", ub), ("ps", db), ("oacc", "u"), ("oacc", "d")])
                else:
                    uv = UDv if g == 1 else UDv.rearrange("p a (m i) -> p a m i", m=4)
                    P.add("dve", TT(oa, oa, uv, ALU.add), reads=[],
                          writes=[("ps", ub), ("ps", db), ("oacc", "u"), ("oacc", "d")])
            return f

        def make_finish(h):
            def f():
                den = OACC[:, 1, :]
                ot = OT2[h % 2]
                P.add("act", ACTI(den, den, AF.Ln), reads=[], writes=[("oacc", "d")])
                P.add("act", ACTI(den, den, AF.Exp, scale=-1.0), reads=[], writes=[("oacc", "d")])
                P.add("dve", TT(ot, OACC[:, 0, :], den, ALU.mult), reads=[], writes=[("ot", h % 2), ("oacc", "u"), ("oacc", "d")])
                P.add("sp", DMA(ot_d[h], ot), reads=[("ot", h % 2)], writes=[("ot_d", h)], dma="otd%d" % (h % 2))
            return f

        issue_loads(0)
        norm_tiles(list(range(NT)) + [TS], l)
        qproj(0)
        for idx, (h, g) in enumerate(items):
            if idx + 1 < len(items):
                issue_loads(idx + 1)
            d = BR[g][1]
            par = idx % 2
            qt = QT[par]
            tbs = TBS[par]
            K4 = KTG[g]
            V4 = VVG[g]
            tb4h = tbs[:, 0:512]
            tb4 = tbs[:, 256:768]
            nunits = 4 if g < 2 else 4
            for u in range(nunits):
                sa = 1 + ctr["sring"] % 3
                sbb = 1 + (ctr["sring"] + 1) % 3
                ctr["sring"] += 2
                ctr["unit"] += 1
                A, B = PS[sa], PS[sbb]
                qk = []
                pv = []
                if g < 2:
                    if g == 0:
                        r, j0 = 0, 4 * u
                        NBg = 16
                    else:
                        r, j0 = u, 0
                        NBg = 4
                    qc = lambda j, n=1: qt[:, (r * NBg + j) * 128:(r * NBg + j + n) * 128]
                    tba = tb4h if j0 == 0 else tb4
                    tbb = tb4
                    qk = [(A, IDENT, tba, True, False),
                          (A[:, 0:128], K4[:, r, j0, :], qc(j0), False, False),
                          (A[:, 128:384], K4[:, r, j0 + 1, :], qc(j0, 2), False, False),
                          (A[:, 384:512], K4[:, r, j0 + 2, :], qc(j0 + 1), False, True),
                          (B, IDENT, tbb, True, False),
                          (B[:, 0:128], K4[:, r, j0 + 2, :], qc(j0 + 2), False, False),
                          (B[:, 128:384], K4[:, r, j0 + 3, :], qc(j0 + 2, 2), False, False),
                          (B[:, 384:512], K4[:, r, j0 + 4, :], qc(j0 + 3), False, True)]
                    if g == 0:
                        oa = OACC[:, :, 512 * u:512 * (u + 1)]
                    else:
                        oa = OACC[:, :, r:r + 4 * 511 + 1:4]
                else:
                    r0 = 4 * u
                    tba = tb4h
                    tbb = tb4h
                    for m in range(4):
                        bank = A if m < 2 else B
                        c = (m % 2) * 256
                        q = qt[:, (r0 + m) * 128:(r0 + m + 1) * 128]
                        if m % 2 == 0:
                            qk.append((bank, IDENT, tb4h, True, False))
                        qk.append((bank[:, c:c + 128], K4[:, r0 + m, 0, :], q, False, False))
                        qk.append((bank[:, c + 128:c + 256], K4[:, r0 + m, 1, :], q, False, m % 2 == 1))
                    oa = OACC[:, :, r0:r0 + 16 * 127 + 4].rearrange("p a (i m) -> p a m i", m=16)[:, :, 0:4, :] \
                        if False else None
                P.add("pe", MMX(qk), reads=[("kt", g, "own"), ("kt", g, "h"), ("tbs", par), ("ident",)] +
                      [("qt", par, nb) for nb in range(4)], writes=[("ps", sa), ("ps", sbb)])
                pi = ctr["pp"] % 3
                ctr["pp"] += 1
                pp = PPB[pi]
                P.add("act", ACTI(pp[:, 0:512], A, AF.Exp), reads=[], writes=[("ps", sa), ("pp", pi, 0)])
                P.add("act", ACTI(pp[:, 512:1024], B, AF.Exp), reads=[], writes=[("ps", sbb), ("pp", pi, 1)])
                sl = lambda s_, n=1, pp=pp: pp[:, s_ * 128:(s_ + n) * 128]
                if g < 2:
                    def pv_fn(U, r=r, j0=j0, sl=sl, V4=V4):
                        return [(U[:, 0:128], V4[:, r, j0, :], sl(0), True, False),
                                (U[:, 0:256], V4[:, r, j0 + 1, :], sl(1, 2), False, False),
                                (U[:, 128:384], V4[:, r, j0 + 2, :], sl(3, 2), False, False),
                                (U[:, 256:512], V4[:, r, j0 + 3, :], sl(5, 2), False, False),
                                (U[:, 384:512], V4[:, r, j0 + 4, :], sl(7), False, True)]
                else:
                    def pv_fn(U, r0=r0, sl=sl, V4=V4):
                        out = []
                        for m in range(4):
                            out.append((U[:, m * 128:(m + 1) * 128], V4[:, r0 + m, 0, :], sl(2 * m), m == 0, False))
                            out.append((U[:, m * 128:(m + 1) * 128], V4[:, r0 + m, 1, :], sl(2 * m + 1), False, m == 3))
                        return out
                    oa = OACC[:, :, :].rearrange("p a (i r) -> p a r i", r=16)[:, :, r0:r0 + 4, :]
                ppv = pp.rearrange("p (b s x) -> p s b x", s=2, x=128)

                def den_fn(DN, ppv=ppv):
                    return [(DN[:, :], ONESB, ppv[:, 0, :, :], True, False), (DN[:, :], ONESB, ppv[:, 1, :, :], False, True)]
                pv, den = pv_fn, den_fn
                if idx + 1 < len(items):
                    qproj_part(idx + 1, u)
                    if u == 3:
                        qproj_part(idx + 1, 4)
                deferred.append(make_pv(g, pv, den, oa, pi))
                pop_deferred(DLAG)
            if g == 2:
                deferred.append(make_finish(h))
        pop_deferred(0)
        P.barrier()
        OTT = [YALL[:, 0:512].bitcast(BF16).rearrange("p (h c) -> p h c", h=8),
               YALL[:, 512:1024].bitcast(BF16).rearrange("p (h c) -> p h c", h=8)]

        def wo_load(t):
            ob = t % 2
            P.add("sp", DMA(OTT[ob], ot_d[:, :, t * 128:(t + 1) * 128].rearrange("h p c -> p h c")),
                  writes=[("ott", ob)], dma="ott%d" % ob)

        def wo_tile(t):
            ob = t % 2
            pb = 0
            P.add("pe", MM([(PS[pb + half][:, :], OTT[ob][:, hh, :], WO[:, hh, half * 512:(half + 1) * 512], hh == 0, hh == 7)
                            for half in range(2) for hh in range(8)]),
                  reads=[("ott", ob), ("wsl", 0), ("wsl", 1)], writes=[("ps", pb), ("ps", pb + 1)])
            for half in range(2):
                xs = X[:, t, half * 512:(half + 1) * 512]
                P.add("dve", TT(xs, xs, PS[pb + half][:, :], ALU.add), reads=[("x", t)],
                      writes=[("ps", pb + half), ("x", t)])
            if t + 2 < NT:
                wo_load(t + 2)
        pa2 = phase_alloc()
        KC = [vf32(pa2(4096), 1024) for _ in range(2)]
        KB = [vf32(pa2(4096), 1024) for _ in range(2)]
        PROD = [vf32(pa2(4096), 1024) for _ in range(2)]
        VC = [vbf(pa2(2048), 1024) for _ in range(2)]
        VB = [vbf(pa2(2048), 1024) for _ in range(2)]
        BS = vf32(pa2(3 * 8 * 4), 24).rearrange("p (g h) -> p g h", g=3)
        B0 = vf32(pa2(3 * 8 * 4), 24).rearrange("p (g h) -> p g h", g=3)
        SC = [vf32(pa2(32), 8) for _ in range(2)]
        SCB = [vf32(pa2(32), 8) for _ in range(2)]
        PA = [vbf(pa2(64), 8) for _ in range(2)]
        PB = [vbf(pa2(64), 8) for _ in range(2)]
        OSA = vf32(pa2(NS * 16 * 4), NS * 16).rearrange("p (s x) -> p s x", s=NS)
        RD = vf32(pa2(NS * 8 * 4), NS * 8).rearrange("p (s x) -> p s x", s=NS)
        OTS = vbf(pa2(8 * NS * 2), 8 * NS).rearrange("p (h s) -> p h s", h=8)
        P.add("sp", DMA(BS, c_biasS.rearrange("g p h -> p g h")), writes=[("bs",)], dma="@sb%d" % l)
        P.add("sp", DMA(B0[0:1, :, :], c_bias0.rearrange("(o g) h -> o g h", o=1)), writes=[("b0",)], dma="@sb%d" % l)
        sitems = [(s, g) for s in range(NS) for g in range(3)]

        def s_loads(it):
            s, g = sitems[it]
            b = it % 2
            d = BR[g][1]
            P.add("sp", DMA(KC[b], cache[g][s, 0:127 * d + 1:d, 0:1024]), writes=[("kc", b)], dma="sc_kc%d" % b)
            P.add("pool", DMA(VC[b], cache[g][s, 0:127 * d + 1:d, 1024:2048]), writes=[("vc", b)], dma="sc_vc%d" % b)
            P.add("sp", DMA(KB[b][0:1, :], kvs[g][s:s + 1, 0:1024]), reads=[("kvs", g)], writes=[("kb", b)],
                  dma="sc_kb%d" % b)
            P.add("pool", DMA(VB[b][0:1, :], kvs[g][s:s + 1, 1024:2048]), reads=[("kvs", g)], writes=[("vb", b)],
                  dma="sc_vb%d" % b)

        def s_scores(it):
            s, g = sitems[it]
            b = it % 2
            qb0 = 2 if b == 0 else 4
            P.add("pe", MM([(PS[qb0 + hh // 4][:, (hh % 4) * 128:(hh % 4) * 128 + 128],
                             QTS[:, g * 8 + hh, s:s + 1].to_broadcast([128, 128]), IDENT, True, True)
                            for hh in range(8)]),
                  reads=[("ident",)], writes=[("ps", qb0), ("ps", qb0 + 1)])
            for half in range(2):
                P.add("dve", TT(PROD[b][:, half * 512:(half + 1) * 512], KC[b][:, half * 512:(half + 1) * 512],
                                PS[qb0 + half][:, :], ALU.mult), reads=[("kc", b), ("ps", qb0 + half)],
                      writes=[("prod", b, half)])
            P.add("dve", REDUCE(SC[b], PROD[b].rearrange("p (h x) -> p h x", h=8), AX.X, ALU.add),
                  reads=[("prod", b, 0), ("prod", b, 1)], writes=[("sc", b)])
            for half in range(2):
                P.add("dve", TT(PROD[b][0:1, half * 512:(half + 1) * 512], KB[b][0:1, half * 512:(half + 1) * 512],
                                PS[qb0 + half][0:1, :], ALU.mult), reads=[("kb", b), ("sc", b)],
                      writes=[("prod", b, half), ("ps", qb0 + half)])
            P.add("dve", REDUCE(SCB[b][0:1, :], PROD[b][0:1, :].rearrange("p (h x) -> p h x", h=8), AX.X, ALU.add),
                  reads=[("prod", b, 0), ("prod", b, 1)], writes=[("scb", b)])
            P.add("dve", TT(SC[b], SC[b], BS[:, g, :], ALU.add), reads=[("bs",)], writes=[("sc", b)])
            P.add("dve", TT(SCB[b][0:1, :], SCB[b][0:1, :], B0[0:1, g, :], ALU.add), reads=[("b0",)], writes=[("scb", b)])
            P.add("act", ACTI(PA[b], SC[b], AF.Exp), reads=[("sc", b)], writes=[("pa", b)])
            P.add("act", ACTI(PB[b][0:1, :], SCB[b][0:1, :], AF.Exp), reads=[("scb", b)], writes=[("pb", b)])

        def s_pv(it):
            s, g = sitems[it]
            b = it % 2
            pvb = 6 + b
            items2 = []
            for hh in range(8):
                items2.append((PS[pvb][:, hh:hh + 1], VC[b][:, hh * 128:(hh + 1) * 128], PA[b][:, hh:hh + 1], True, False))
                items2.append((PS[pvb][:, hh:hh + 1], VB[b][0:1, hh * 128:(hh + 1) * 128], PB[b][0:1, hh:hh + 1],
                               False, True))
            items2.append((PS[pvb][:, 8:16], ONESB, PA[b], True, False))
            items2.append((PS[pvb][:, 8:16], ONESB[0:1, :], PB[b][0:1, :], False, True))
            P.add("pe", MM(items2), reads=[("vc", b), ("vb", b), ("pa", b), ("pb", b), ("onesb",)], writes=[("ps", pvb)])
            if g == 0:
                P.add("dve", COPY(OSA[:, s, :], PS[pvb][:, 0:16]), reads=[], writes=[("ps", pvb), ("osa", s)])
            else:
                P.add("dve", TT(OSA[:, s, :], OSA[:, s, :], PS[pvb][:, 0:16], ALU.add), reads=[],
                      writes=[("ps", pvb), ("osa", s)])

        wo_load(0)
        wo_load(1)
        s_loads(0)
        s_loads(1)
        s_scores(0)
        wo_next = [0]

        def wo_some(n):
            for _ in range(n):
                if wo_next[0] < NT:
                    wo_tile(wo_next[0])
                    wo_next[0] += 1

        for it in range(len(sitems)):
            wo_some(2 if it % 3 == 0 else 1)
            if it + 1 < len(sitems):
                s_scores(it + 1)
            s_pv(it)
            if it + 2 < len(sitems):
                s_loads(it + 2)
        wo_some(NT)
        P.add("dve", RECIP(RD, OSA[:, :, 8:16]), reads=[("osa", s) for s in range(NS)], writes=[("rd",)])
        P.add("dve", TT(OTS.rearrange("p h s -> p s h"), OSA[:, :, 0:8], RD, ALU.mult),
              reads=[("rd",)] + [("osa", s) for s in range(NS)], writes=[("ots",)])
        P.add("pe", MM([(PS[half][0:NS, :], OTS[:, hh, :], WO[:, hh, half * 512:(half + 1) * 512], hh == 0, hh == 7)
                        for half in range(2) for hh in range(8)]),
              reads=[("ots",), ("wsl", 0), ("wsl", 1)], writes=[("ps", 0), ("ps", 1)])
        for half in range(2):
            xs = X[0:NS, TS, half * 512:(half + 1) * 512]
            P.add("dve", TT(xs, xs, PS[half][0:NS, :], ALU.add), reads=[("x", TS)],
                  writes=[("ps", half), ("x", TS)])

    def final_norm():
        P.add("sp", DMA(TBL, gvec[9].partition_broadcast(128)), writes=[("tbl",)], dma="tblf")
        for t in list(range(NT)) + [TS]:
            slot = t % 2
            P.add("act", ACTI(YST[slot], X[:, t, :], AF.Square, accum_out=SS[:, t:t + 1]),
                  reads=[("x", t)], writes=[("yst", slot), ("ss", t)])
            P.add("act", ACTI(RS[:, t:t + 1], SS[:, t:t + 1], AF.Sqrt, bias=EPS, scale=1.0 / D),
                  reads=[("ss", t)], writes=[("rs", t)])
            P.add("dve", RECIP(RS[:, t:t + 1], RS[:, t:t + 1]), reads=[("rs", t)], writes=[("rs", t)])
            P.add("dve", STT(YST[slot], X[:, t, :], RS[:, t:t + 1], TBL, ALU.mult, ALU.mult),
                  reads=[("x", t), ("rs", t), ("tbl",)], writes=[("yst", slot)])
            if t == TS:
                P.add("sp", DMA(y_s, YST[slot][0:NS, :]), reads=[("yst", slot)], dma="out_y%d" % slot)
            else:
                P.add("sp", DMA(y_p[t], YST[slot]), reads=[("yst", slot)], dma="out_y%d" % slot)

    stages = ["pool0", "mlp0", "pool1", "mlp1", "kv", "attn2", "mlp2", "attn3", "mlp3"]
    last = stages.index(stop_after) if stop_after else len(stages) - 1

    def run_stage(i):
        name = stages[i]
        if name == "pool0":
            pool_layer(0)
        elif name == "mlp0":
            mlp_phase(0, True)
        elif name == "pool1":
            pool_layer(1)
        elif name == "mlp1":
            mlp_phase(1, False)
        elif name == "kv":
            kv_phase()
        elif name == "attn2":
            attn_layer(2)
        elif name == "mlp2":
            mlp_phase(2, False)
        elif name == "attn3":
            attn_layer(3)
        elif name == "mlp3":
            mlp_phase(3, False)

    for i in range(last + 1):
        run_stage(i)
        P.barrier()
    dump_dbg()
    final_norm()
    P.emit(nc, es)
    es.close()
    return nc, P


def make_in_maps(inputs):
    f = lambda a: np.ascontiguousarray(np.asarray(a, dtype=np.float32))
    x_prompt = f(inputs["x_prompt"]); x_sample = f(inputs["x_sample"]); state_pool = f(inputs["state_pool"])
    caches = [f(inputs["cache_kv_w128"]), f(inputs["cache_kv_w512"]), f(inputs["cache_kv_w2048"])]
    rel_bias = f(inputs["rel_bias"])
    gvec = np.concatenate([f(inputs["norm_mix"]), f(inputs["norm_mlp"]), f(inputs["norm_kv"])[None],
                           f(inputs["norm_final"])[None], f(inputs["pool_scale"])], 0)
    shared = {
        "gvec": gvec, "pool_w": f(inputs["pool_w"]), "mlp_in": f(inputs["mlp_in"]), "mlp_out": f(inputs["mlp_out"]),
        "w_kv": f(inputs["w_kv"]), "w_q": f(inputs["w_q"]), "w_o": f(inputs["w_o"]),
        "c_ident": np.eye(128, dtype=np.float32),
    }
    sel = np.zeros((NS, 15, 4, NS), np.float32)
    for g, w in enumerate((2, 4, 8, 16)):
        for s in range(NS):
            sel[s, 15 - (w - 1):, g, s] = 1.0 / w
    shared["c_sel"] = sel.reshape(60, 16)
    selh = np.zeros((NS, 4, NS), np.float32)
    for g, w in enumerate((2, 4, 8, 16)):
        for s_ in range(NS):
            selh[s_, g, s_] = 1.0 / w - 1.0
    shared["c_selh"] = selh.reshape(NS, 16)
    j, valid = toeplitz_index()
    bf = np.zeros((24, 128, 256), np.float32)
    for g, (w, d) in enumerate(BR):
        bk = t5_bucket_np(j * d)
        for h in range(8):
            bf[g * 8 + h] = rel_bias[bk, g * 8 + h]
    nx, sm = bf[:, :, 0:128], bf[:, :, 128:256]
    vn, vs = valid[:, 0:128] > 0, valid[:, 128:256] > 0
    NEG = np.float32(-30000.0)
    nxm = np.where(vn[None], nx, NEG)
    smm = np.where(vs[None], sm, NEG)
    dead = np.full_like(nxm, NEG)
    tabs = []
    for has_halo in (False, True):
        hn = nxm if has_halo else dead
        t01 = np.concatenate([hn, smm, nxm, smm, nxm, smm], 2)
        t2 = np.concatenate([hn, smm, hn, smm, dead, dead], 2)
        tabs.append(np.ascontiguousarray(np.concatenate([t01[0:16], t2[16:24]], 0)))
    bS = np.zeros((3, 128, 8), np.float32)
    b0 = np.zeros((3, 8), np.float32)
    for g, (w, d) in enumerate(BR):
        bk = t5_bucket_np((128 - np.arange(128)) * d)
        bS[g] = rel_bias[bk, g * 8:(g + 1) * 8]
        b0[g] = rel_bias[0, g * 8:(g + 1) * 8]
    shared["c_biasS"] = bS
    shared["c_bias0"] = b0
    in_maps = []
    for c in range(8):
        b, half = c // 2, c % 2
        xin = np.zeros((18, 128, D), np.float32)
        xin[0:NT] = x_prompt[b, half * 2048:(half + 1) * 2048].reshape(NT, 128, D)
        xin[TS, 0:NS] = x_sample[NS * c:NS * (c + 1), 0, :]
        if half == 1:
            xin[TH] = x_prompt[b, 1920:2048]
        ap = np.zeros((2, 4, 2, 128, 128), np.float32)
        tt_src = np.arange(128)[:, None]
        tt_dst = np.arange(128)[None, :]
        for g, w in enumerate((2, 4, 8, 16)):
            for first in range(2):
                if first == 1 and half == 0:
                    cnt = np.minimum(np.arange(128) + 1, w).astype(np.float32)[None, :]
                else:
                    cnt = np.full((1, 128), float(w), np.float32)
                dist_same = tt_dst - tt_src
                dist_prev = tt_dst + 128 - tt_src
                ap[first, g, 1] = ((dist_same >= 0) & (dist_same < w)) / cnt - (dist_same == 0)
                ap[first, g, 0] = ((dist_prev >= 0) & (dist_prev < w)) / cnt
        m = dict(shared)
        m["xin"] = xin
        m["state"] = np.ascontiguousarray(state_pool[:, NS * c:NS * (c + 1)].reshape(2, 60, D))
        for g in range(3):
            m["cache%d" % g] = np.ascontiguousarray(caches[g][NS * c:NS * (c + 1)].reshape(NS, -1, 2048))
        m["c_apool"] = ap
        m["c_bias3"] = tabs[half]
        in_maps.append(m)
    return in_maps


_CACHE = {}


def kernel(**inputs):
    if "nc" not in _CACHE:
        _CACHE["nc"] = build_program()[0]
    nc = _CACHE["nc"]
    in_maps = make_in_maps(inputs)
    res = run_bass_kernel_spmd(nc, in_maps, core_ids=list(range(8)))
    R = res.results
    y_prompt = np.zeros((4, 4096, D), np.float32)
    y_sample = np.zeros((32, 1, D), np.float32)
    pool_p = np.zeros((2, 4, 15, D), np.float32)
    pool_s = np.zeros((2, 32, 15, D), np.float32)
    kvp = [np.zeros((4, w, 2, 8, 128), np.float32) for (w, d) in BR]
    kvs = [np.zeros((32, 1, 2, 8, 128), np.float32) for _ in BR]
    for c in range(8):
        b, half = c // 2, c % 2
        r = R[c]
        y_prompt[b, half * 2048:(half + 1) * 2048] = r["y_p"].reshape(2048, D)
        y_sample[NS * c:NS * (c + 1), 0] = r["y_s"]
        pool_s[:, NS * c:NS * (c + 1)] = r["pool_s"]
        for g in range(3):
            kvs[g][NS * c:NS * (c + 1), 0] = r["kvs%d" % g].reshape(NS, 2, 8, 128)
        if half == 1:
            pool_p[:, b] = r["pool_p"]
            for g, (w, d) in enumerate(BR):
                kvp[g][b] = r["kvp%d" % g].reshape(w, 2, 8, 128)
    return (y_prompt, y_sample, pool_p, pool_s, kvp[0], kvs[0], kvp[1], kvs[1], kvp[2], kvs[2])
```

```python
import math
from contextlib import ExitStack

import numpy as np
import concourse.bass as bass
import concourse.mybir as mybir
from concourse.bass_utils import run_bass_kernel_spmd

F32 = mybir.dt.float32
BF16 = mybir.dt.bfloat16
AF = mybir.ActivationFunctionType
ALU = mybir.AluOpType
AX = mybir.AxisListType

D = 1024
NT = 16
TS = 16
TH = 17
NS = 4
HID = 4096
EPS = 1e-6
BR = ((128, 1), (512, 4), (2048, 16))
CH = 16
C0 = CH + 128
CS = C0 + 2048
CW = CS + NS
SCALE = 128 ** -0.5


def TT(out, a, b, op):
    return lambda e: e.tensor_tensor(out, a, b, op)


def STT(out, a, s, b, op0, op1):
    return lambda e: e.scalar_tensor_tensor(out, a, s, b, op0, op1)


def TSC(out, a, s1, op0):
    return lambda e: e.tensor_scalar(out, a, s1, None, op0)


def TSC2(out, a, s1, s2, op0, op1):
    return lambda e: e.tensor_scalar(out, a, s1, s2, op0, op1)


def ACTI(out, in_, func, **kw):
    return lambda e: e.activation(out, in_, func, **kw)


def DMA(out, in_, **kw):
    return lambda e: e.dma_start(out=out, in_=in_, **kw)


def MM(items):
    def f(e):
        ins = None
        for (o, l, r, st, sp) in items:
            ins = e.matmul(o, l, r, start=st, stop=sp)
        return ins
    return f


def MMX(items):
    def f(e):
        ins = None
        for (o, l, r, st, sp) in items:
            ins = e.matmul(o, l, r, start=st, stop=sp, skip_group_check=True)
        return ins
    return f


def TRS(items, ident):
    def f(e):
        ins = None
        for (o, i) in items:
            ins = e.transpose(o, i, ident)
        return ins
    return f


def RECIP(out, in_):
    return lambda e: e.reciprocal(out, in_)


def MEMSET(ap, v):
    return lambda e: e.memset(ap, v)


def COPY(out, in_):
    return lambda e: e.tensor_copy(out, in_)


def REDUCE(out, in_, axis, op):
    return lambda e: e.tensor_reduce(out, in_, axis, op)


def CC(groups, src, dst):
    return lambda e: e.collective_compute("AllGather", ALU.bypass, replica_groups=groups, ins=[src], outs=[dst])


class Op:
    __slots__ = ("eng", "fn", "deps", "need", "is_dma", "semkey", "sig", "idx", "inc")


class Prog:
    ENGS = ("pe", "act", "dve", "pool", "sp")

    def __init__(self):
        self.ops = []
        self.lastw = {}
        self.readers = {}
        self.pending_barrier = {e: [] for e in self.ENGS}
        self.last_on = {}

    def add(self, eng, fn, reads=(), writes=(), dma=None, cc=False):
        op = Op()
        op.eng = eng
        op.fn = fn
        op.is_dma = dma is not None
        op.semkey = dma
        op.need = op.is_dma
        op.sig = None
        op.inc = 1 if (cc or dma is None) else 16
        deps = set()
        lw = self.lastw
        rd = self.readers
        for k in reads:
            w = lw.get(k)
            if w is not None:
                deps.add(w)
        for k in writes:
            w = lw.get(k)
            if w is not None:
                deps.add(w)
            r = rd.get(k)
            if r:
                deps.update(r)
        if eng == "pe" and not op.is_dma:
            deps = {d for d in deps if d.is_dma or d.eng != "pe"}
        pb = self.pending_barrier[eng]
        if pb:
            deps.update(pb)
            self.pending_barrier[eng] = []
        for d in deps:
            d.need = True
        op.deps = deps
        for k in reads:
            rd.setdefault(k, []).append(op)
        for k in writes:
            lw[k] = op
            rd[k] = []
        op.idx = len(self.ops)
        self.ops.append(op)
        self.last_on[(eng, op.is_dma, dma)] = op
        return op

    def barrier(self):
        lasts = list(self.last_on.values())
        for o in lasts:
            o.need = True
        for e in self.ENGS:
            self.pending_barrier[e] = list(lasts)
        self.lastw = {}
        self.readers = {}

    def emit(self, nc, es):
        SEM_ROT = 12000
        DMA_ROT = 700
        comp_count = {e: 0 for e in self.ENGS}
        sem_pool = {}

        def get_sem(name):
            s = sem_pool.get(name)
            if s is None:
                s = es.enter_context(nc.semaphore(name))
                sem_pool[name] = s
            return s

        totals = {}
        for op in self.ops:
            if op.is_dma:
                totals[op.semkey] = totals.get(op.semkey, 0) + 1
        dma_cnt = {}
        for op in self.ops:
            if op.is_dma:
                c = dma_cnt.get(op.semkey, 0) + 1
                dma_cnt[op.semkey] = c
                if op.semkey.startswith("@"):
                    assert totals[op.semkey] <= DMA_ROT
                    op.sig = (get_sem("dg_" + op.semkey[1:]), op.inc * totals[op.semkey])
                else:
                    gen = (c - 1) // DMA_ROT
                    op.sig = (get_sem("d_%s_%d" % (op.semkey, gen)), op.inc * (c - gen * DMA_ROT))
            elif op.need:
                c = comp_count[op.eng] + 1
                comp_count[op.eng] = c
                gen = (c - 1) // SEM_ROT
                op.sig = (get_sem("c_%s_%d" % (op.eng, gen)), c - gen * SEM_ROT)
        self.n_sems = len(sem_pool)
        final_dma = {}
        for op in self.ops:
            if op.is_dma:
                k = id(op.sig[0])
                if k not in final_dma or final_dma[k][1] < op.sig[1]:
                    final_dma[k] = op.sig
        per_eng = {e: [] for e in self.ENGS}
        for op in self.ops:
            per_eng[op.eng].append(op)
        self.n_waits = 0
        block = es.enter_context(nc.Block())

        def run(engname, e):
            waited = {}
            for op in per_eng[engname]:
                need = {}
                for d in op.deps:
                    s, v = d.sig
                    k = id(s)
                    if waited.get(k, 0) >= v:
                        continue
                    if k not in need or need[k][1] < v:
                        need[k] = (s, v)
                for k, (s, v) in need.items():
                    e.wait_ge(s, v)
                    waited[k] = v
                    self.n_waits += 1
                ins = op.fn(e)
                if op.sig is not None:
                    if op.is_dma and op.inc == 1:
                        ins.then_inc(op.sig[0])
                    else:
                        ins.then_inc(op.sig[0], op.inc)
            if engname == "sp":
                for k, (s, v) in final_dma.items():
                    if waited.get(k, 0) < v:
                        e.wait_ge(s, v)

        @block.tensor
        def _(e):
            run("pe", e)

        @block.scalar
        def _(e):
            run("act", e)

        @block.vector
        def _(e):
            run("dve", e)

        @block.gpsimd
        def _(e):
            run("pool", e)

        @block.sync
        def _(e):
            run("sp", e)


def t5_bucket_np(dist):
    dist = np.asarray(dist, np.int64)
    max_exact = 16
    df = np.maximum(dist, 1).astype(np.float32)
    large = max_exact + (np.log(df / np.float32(max_exact)) / np.float32(math.log(2048 / max_exact))
                         * np.float32(32 - max_exact)).astype(np.int32)
    large = np.minimum(large, 31)
    return np.where(dist < max_exact, dist, large).astype(np.int64)


def toeplitz_index():
    k = np.arange(128)[:, None]
    c = np.arange(256)[None, :]
    j = np.where(c < 128, 128 + c - k, c - 128 - k)
    valid = np.where(c < 128, k >= c, (c - 128) >= k)
    j = np.where(valid, j, 0)
    return j, valid.astype(np.float32)


def build_program(stop_after=None, debug=False):
    nc = bass.Bass("TRN2", target_bir_lowering=False)
    P = Prog()

    def din(name, shape, dt=F32):
        return nc.dram_tensor(name, list(shape), dt, kind="ExternalInput").ap()

    def dout(name, shape, dt=F32):
        return nc.dram_tensor(name, list(shape), dt, kind="ExternalOutput").ap()

    def dscr(name, shape, dt=BF16):
        return nc.dram_tensor(name, list(shape), dt).ap()

    xin = din("xin", [18, 128, D])
    state = din("state", [2, 60, D])
    cache = [din("cache0", [NS, 128, 2048]), din("cache1", [NS, 512, 2048]), din("cache2", [NS, 2048, 2048])]
    gvec = din("gvec", [12, D])
    pool_w = din("pool_w", [2, 4, 256, 256])
    mlp_in = din("mlp_in", [4, D, HID])
    mlp_out = din("mlp_out", [4, HID, D])
    w_kv = din("w_kv", [D, 6144])
    w_q = din("w_q", [2, D, 3072])
    w_o = din("w_o", [2, D, D])
    c_ident = din("c_ident", [128, 128])
    c_apool = din("c_apool", [2, 4, 2, 128, 128])
    c_selh = din("c_selh", [NS, 16])
    c_sel = din("c_sel", [60, 16])
    c_bias3 = din("c_bias3", [24, 128, 768])
    c_biasS = din("c_biasS", [3, 128, 8])
    c_bias0 = din("c_bias0", [3, 8])
    y_p = dout("y_p", [NT, 128, D])
    y_s = dout("y_s", [NS, D])
    pool_p = dout("pool_p", [2, 15, D])
    pool_s = dout("pool_s", [2, NS, 15, D])
    kvp = [dout("kvp0", [128, 2048]), dout("kvp1", [512, 2048]), dout("kvp2", [2048, 2048])]
    kvs = [dout("kvs%d" % g, [NS, 2048]) for g in range(3)]
    dbg = dout("dbg", [18, 128, D]) if debug else None
    kT_d = dscr("kT_d", [24, 128, 2048])
    v_d = dscr("v_d", [24, 128, 2048])
    exp_src0 = dscr("exp_src0", [1024, 256])
    exp_dst0 = dscr("exp_dst0", [2048, 256])
    exp_src1 = dscr("exp_src1", [1024, 1024])
    exp_dst1 = dscr("exp_dst1", [2048, 1024])
    exp_src2 = [dscr("exp_src2_%d" % i, [256, 4096]) for i in range(4)]
    exp_dst2 = [dscr("exp_dst2_%d" % i, [512, 4096]) for i in range(4)]
    ot_d = dscr("ot_d", [8, 128, 2048])

    es = ExitStack()
    ARENA_BYTES = 212000
    arena = es.enter_context(nc.sbuf_tensor("arena", [128, ARENA_BYTES // 2], BF16))
    pos = [0]

    def alloc(nbytes):
        off = (pos[0] + 63) // 64 * 64
        pos[0] = off + nbytes
        assert pos[0] <= ARENA_BYTES, ("arena overflow", pos[0])
        return off

    def vbf(off, n):
        return arena[:, off // 2: off // 2 + n]

    def vf32(off, n):
        return arena[:, off // 2: off // 2 + 2 * n].bitcast(F32)

    X = vf32(alloc(18 * D * 4), 18 * D).rearrange("p (t d) -> p t d", t=18)
    HT = vbf(alloc(8 * CW * 2), 8 * CW).rearrange("p (k c) -> p k c", k=8)
    WALL = vbf(alloc(4 * 8192), 4 * 4096)
    WSL = [WALL[:, i * 4096:(i + 1) * 4096] for i in range(4)]
    IDENT = vbf(alloc(256), 128)
    ONESB = vbf(alloc(256), 128)
    ONES32 = vf32(alloc(512), 128)
    GCOL = vf32(alloc(12 * 8 * 4), 96).rearrange("p (v k) -> p v k", v=12)
    SS = vf32(alloc(32 * 4), 32)
    RS = vf32(alloc(32 * 4), 32)
    FLAG = vf32(alloc(64), 2)
    TBL = vf32(alloc(4096), 1024)
    HTOKALL = vbf(alloc(4096), 2048)
    HTOK = [HTOKALL[:, 0:1024], HTOKALL[:, 1024:2048]]
    YALL = vf32(alloc(8192), 2048)
    YST = [YALL[:, 0:1024], YALL[:, 1024:2048]]
    PH0 = alloc(0)

    def phase_alloc():
        st = [PH0]

        def a(nbytes):
            off = (st[0] + 63) // 64 * 64
            st[0] = off + nbytes
            assert st[0] <= ARENA_BYTES, ("phase overflow", st[0] - PH0, ARENA_BYTES - PH0)
            return off
        return a

    PSALL = es.enter_context(nc.psum_tensor("psall", [128, 4096], F32))
    PS = [PSALL[:, i * 512:(i + 1) * 512] for i in range(8)]

    def psb(b):
        return PS[b][:, :].bitcast(BF16)

    P.add("pool", DMA(IDENT, c_ident), writes=[("ident",)], dma="ident")
    P.add("sp", DMA(GCOL, gvec.rearrange("v (k p) -> p v k", p=128), allow_slow_non_contiguous=True),
          writes=[("gcol",)], dma="@setup2")
    P.add("dve", MEMSET(ONESB, 1.0), writes=[("onesb",)])
    P.add("dve", MEMSET(ONES32, 1.0), writes=[("ones32",)])
    P.add("dve", MEMSET(HT[:, :, 0:CH], 0.0), writes=[("hT", "Z")])
    def load_x(tiles, extra_reads=()):
        for t in tiles:
            P.add("sp", DMA(X[:, t, :], xin[t]), reads=list(extra_reads), writes=[("x", t)], dma="xin%d" % t)

    load_x([TH, 0, 1])

    def tile_cols(t):
        if t == TH:
            return CH, 128
        if t == TS:
            return CS, NS
        return C0 + 128 * t, 128

    def hkey(t):
        return ("hT", t)

    tr_ctr = [0]

    def norm_a(t, gidx, h32_tbl=None, h32_slot=0):
        buf = tr_ctr[0] % 2
        tr_ctr[0] += 1
        htok = HTOK[buf]
        bank = 6 + buf
        c, n = tile_cols(t)
        P.add("act", ACTI(htok, X[:, t, :], AF.Square, accum_out=SS[:, t:t + 1]),
              reads=[("x", t)], writes=[("htok", buf), ("ss", t)])
        P.add("act", ACTI(RS[:, t:t + 1], SS[:, t:t + 1], AF.Sqrt, bias=EPS, scale=1.0 / D),
              reads=[("ss", t)], writes=[("rs", t)])
        P.add("dve", RECIP(RS[:, t:t + 1], RS[:, t:t + 1]), reads=[("rs", t)], writes=[("rs", t)])
        P.add("dve", TSC(htok, X[:, t, :], RS[:, t:t + 1], ALU.mult),
              reads=[("x", t), ("rs", t)], writes=[("htok", buf)])
        if h32_tbl is not None:
            P.add("dve", STT(YST[h32_slot], X[:, t, :], RS[:, t:t + 1], h32_tbl, ALU.mult, ALU.mult),
                  reads=[("x", t), ("rs", t), ("gtbl",)], writes=[("yst", h32_slot)])
        pv = psb(bank)
        P.add("pe", TRS([(pv[:, k * 128:(k + 1) * 128], htok[:, k * 128:(k + 1) * 128]) for k in range(8)], IDENT),
              reads=[("htok", buf), ("ident",)], writes=[("ps", bank)])

        def evac():
            P.add("dve", TT(HT[:, :, c:c + n], pv.rearrange("p (k n) -> p k n", k=8)[:, :, 0:n],
                            GCOL[:, gidx, :].unsqueeze(2).broadcast_to([128, 8, n]), ALU.mult),
                  reads=[("gcol",)], writes=[("ps", bank), hkey(t)])
        return evac

    def norm_tiles(tiles, gidx, hooks=None):
        pend = None
        for t in tiles:
            hk = hooks.get(t) if hooks else None
            ev = norm_a(t, gidx, *(hk[0] if hk else ()))
            if hk:
                hk[1]()
            if pend is not None:
                pend()
            pend = ev
        if pend is not None:
            pend()

    def norm_tile(t, gidx):
        norm_a(t, gidx)()

    def load_w(slot_ap, src_ap, keys, semkey):
        P.add("pool", DMA(slot_ap, src_ap), writes=keys, dma=semkey)

    def mlp_layer(l, batches):
        pa = phase_alloc()
        HIDT = [vbf(pa(4096), 2048).rearrange("p (c n) -> p c n", c=4) for _ in range(2)]
        SQ = [vf32(pa(2048), 512) for _ in range(2)]
        NHB = 8
        steps = [(hb, bi) for hb in range(NHB) for bi in range(len(batches))]

        def wviews(hb):
            par = hb % 2
            win = WSL[2 * par].rearrange("p (k n) -> p k n", k=8)
            wout = WSL[2 * par + 1].rearrange("p (c n) -> p c n", c=4)
            return par, win, wout

        def issue_load(hb):
            par, win, wout = wviews(hb)
            load_w(win, mlp_in[l, :, hb * 512:(hb + 1) * 512].rearrange("(k p) n -> p k n", p=128),
                   [("wsl", 2 * par)], "w%d" % (2 * par))
            load_w(wout, mlp_out[l, hb * 512:(hb + 1) * 512, :].rearrange("(c p) n -> p c n", p=128),
                   [("wsl", 2 * par + 1)], "w%d" % (2 * par + 1))

        ctr = {"ps": 0, "out": 0}

        def emit_in(i):
            hb, bi = steps[i]
            par, win, wout = wviews(hb)
            tiles, c, n = batches[bi]
            hbuf = i % 2
            hid = HIDT[hbuf]
            for m in range(4):
                bank = ctr["ps"] % 2
                ctr["ps"] += 1
                P.add("pe", MM([(PS[bank][:, 0:n], win[:, k, m * 128:(m + 1) * 128], HT[:, k, c:c + n], k == 0, k == 7)
                                for k in range(8)]),
                      reads=[("wsl", 2 * par)] + [hkey(t) for t in tiles], writes=[("ps", bank)])
                sq = SQ[bank]
                P.add("act", ACTI(sq[:, 0:n], PS[bank][:, 0:n], AF.Square), reads=[("ps", bank)], writes=[("sq", bank)])
                P.add("dve", STT(hid[:, m, 0:n], PS[bank][:, 0:n], 0.0, sq[:, 0:n], ALU.is_gt, ALU.mult),
                      reads=[("ps", bank), ("sq", bank)], writes=[("hid", hbuf, m)])

        def emit_out(i):
            hb, bi = steps[i]
            par, win, wout = wviews(hb)
            tiles, c, n = batches[bi]
            hbuf = i % 2
            hid = HIDT[hbuf]
            for ti, t in enumerate(tiles):
                pb = 2 + 2 * (ctr["out"] % 2)
                ctr["out"] += 1
                nr = NS if t == TS else 128
                o = 0 if t == TS else ti * 128
                P.add("pe", MM([(PS[pb + half][0:nr, :], hid[:, hc, o:o + nr], wout[:, hc, half * 512:(half + 1) * 512],
                                 hc == 0, hc == 3) for hc in range(4) for half in range(2)]),
                      reads=[("wsl", 2 * par + 1)] + [("hid", hbuf, m) for m in range(4)],
                      writes=[("ps", pb), ("ps", pb + 1)])
                for half in range(2):
                    xs = X[0:nr, t, half * 512:(half + 1) * 512]
                    P.add("dve", TT(xs, xs, PS[pb + half][0:nr, :], ALU.add),
                          reads=[("ps", pb + half), ("x", t)], writes=[("x", t)])

        issue_load(0)
        for i in range(len(steps)):
            hb, bi = steps[i]
            emit_in(i)
            if i >= 1:
                emit_out(i - 1)
            if bi == 0 and hb + 1 < NHB:
                issue_load(hb + 1)
        emit_out(len(steps) - 1)

    def pool_layer(l):
        pa = phase_alloc()
        POOLED = [vbf(pa(8 * 128 * 2), 8 * 128).rearrange("p (k n) -> p k n", k=8) for _ in range(2)]
        AG = vbf(pa(8 * 128 * 2), 8 * 128).rearrange("p (g s n) -> p g s n", g=4, s=2)
        A0 = vbf(pa(8 * 128 * 2), 8 * 128).rearrange("p (g s n) -> p g s n", g=4, s=2)
        STT_ = vf32(pa(D * 4), D)
        SEL = vf32(pa(16 * 4), 16)
        SELH = vf32(pa(16 * 4), 16)
        WP = vbf(pa(8 * 256 * 2), 8 * 256).rearrange("p (g n) -> p g n", g=8)
        WPS = vbf(pa(8 * 256 * 2), 8 * 256).rearrange("p (g n) -> p g n", g=8)
        GTB = vf32(pa(4096), 1024)
        PS_S = vbf(pa(8 * NS * 2), 8 * NS).rearrange("p (k n) -> p k n", k=8)
        HT3 = [vbf(pa(2048), 1024) for _ in range(3)]
        gk = "@pl%d" % l
        P.add("sp", DMA(STT_[0:60, :], state[l]), writes=[("stt",)], dma=gk)
        P.add("sp", DMA(SEL[0:60, :], c_sel), writes=[("sel",)], dma=gk)
        P.add("sp", DMA(SELH[0:NS, :], c_selh), writes=[("selh",)], dma=gk)
        P.add("sp", DMA(TBL, gvec[10 + l].partition_broadcast(128)), writes=[("tbl",)], dma=gk)
        P.add("sp", DMA(GTB, gvec[l].partition_broadcast(128)), writes=[("gtbl",)], dma=gk)
        P.add("pool", DMA(WP, pool_w[l].rearrange("g (kk p) n -> p (g kk) n", p=128)), writes=[("wp",)], dma="wp")
        P.add("pool", DMA(AG, c_apool[0].rearrange("g s p n -> p g s n")), writes=[("ag",)], dma="ag")
        P.add("pool", DMA(A0, c_apool[1].rearrange("g s p n -> p g s n")), writes=[("a0",)], dma="a0")
        P.add("sp", DMA(pool_s[l, :, 0:14, :], state[l].rearrange("(s r) d -> s r d", r=15)[:, 1:15, :]),
              dma="out_psa%d" % l)
        if l == 0:
            load_x(list(range(2, NT)) + [TS], extra_reads=[("wp",), ("ag",), ("a0",), ("tbl",), ("gtbl",)])

        def fold_scale():
            for kk in range(2):
                P.add("dve", TT(WPS.rearrange("p (g k) n -> p g k n", k=2)[:, :, kk, :],
                                WP.rearrange("p (g k) n -> p g k n", k=2)[:, :, kk, :],
                                TBL.rearrange("p (g n) -> p g n", g=4), ALU.mult),
                      reads=[("wp",), ("tbl",)], writes=[("wps", kk)])
        tiles = [TH] + list(range(NT))
        pend = []
        pctr = [0]
        for i, t in enumerate(tiles):
            if i == 1:
                fold_scale()
            buf = i % 3
            htok = HT3[buf]
            hprev = HT3[(i - 1) % 3]
            P.add("act", ACTI(htok, X[:, t, :], AF.Square, accum_out=SS[:, t:t + 1]),
                  reads=[("x", t)], writes=[("htok", buf), ("ss", t)])
            P.add("act", ACTI(RS[:, t:t + 1], SS[:, t:t + 1], AF.Sqrt, bias=EPS, scale=1.0 / D),
                  reads=[("ss", t)], writes=[("rs", t)])
            P.add("dve", RECIP(RS[:, t:t + 1], RS[:, t:t + 1]), reads=[("rs", t)], writes=[("rs", t)])
            P.add("dve", TSC(htok, X[:, t, :], RS[:, t:t + 1], ALU.mult),
                  reads=[("x", t), ("rs", t)], writes=[("htok", buf)])
            if t == NT - 1:
                P.add("dve", STT(YST[0], X[:, t, :], RS[:, t:t + 1], GTB, ALU.mult, ALU.mult),
                      reads=[("x", t), ("rs", t), ("gtbl",)], writes=[("yst", 0)])
                P.add("sp", DMA(pool_p[l], YST[0][113:128, :]), reads=[("yst", 0)], dma="out_pp%d" % l)
            Am = A0 if t == 0 else AG
            pb = 2 + 2 * (i % 2)
            mm = []
            for c in range(8):
                g = c // 2
                o = PS[pb + c // 4][:, (c % 4) * 128:(c % 4) * 128 + 128]
                has_prev = (i > 0)
                mm.append((o, htok[:, c * 128:(c + 1) * 128], Am[:, g, 1, :], True, not has_prev))
                if has_prev:
                    mm.append((o, hprev[:, c * 128:(c + 1) * 128], Am[:, g, 0, :], False, True))
            P.add("pe", MM(mm), reads=[("htok", buf), ("htok", (i - 1) % 3), ("ag",), ("a0",)],
                  writes=[("ps", pb), ("ps", pb + 1)])

            def tail(t=t, pb=pb, i=i):
                pooled = POOLED[i % 2]
                for half in range(2):
                    P.add("dve", TT(pooled[:, 4 * half:4 * half + 4, :],
                                    PS[pb + half][:, :].rearrange("p (k n) -> p k n", k=4),
                                    GCOL[:, l, 4 * half:4 * half + 4].unsqueeze(2).broadcast_to([128, 4, 128]), ALU.mult),
                          reads=[("gcol",)], writes=[("ps", pb + half), ("pooled", i % 2, half)])
                ob = 6 if pctr[0] % 2 == 0 else 0
                pctr[0] += 1
                P.add("pe", MM([(PS[ob + g // 2][:, (g % 2) * 256:(g % 2) * 256 + 256],
                                 pooled[:, 2 * g + kk, :], WPS[:, 2 * g + kk, :], kk == 0, kk == 1)
                                for g in range(4) for kk in range(2)]),
                      reads=[("pooled", i % 2, 0), ("pooled", i % 2, 1), ("wps", 0), ("wps", 1)],
                      writes=[("ps", ob), ("ps", ob + 1)])
                for half in range(2):
                    xs = X[:, t, half * 512:(half + 1) * 512]
                    P.add("dve", TT(xs, xs, PS[ob + half][:, :], ALU.add), reads=[("x", t)],
                          writes=[("ps", ob + half), ("x", t)])
            pend.append(tail)
            if len(pend) > 1:
                pend.pop(0)()
        while pend:
            pend.pop(0)()
        t = TS
        P.add("act", ACTI(HTOK[0], X[:, t, :], AF.Square, accum_out=SS[:, t:t + 1]),
              reads=[("x", t)], writes=[("htok", 0), ("ss", t)])
        P.add("act", ACTI(RS[:, t:t + 1], SS[:, t:t + 1], AF.Sqrt, bias=EPS, scale=1.0 / D),
              reads=[("ss", t)], writes=[("rs", t)])
        P.add("dve", RECIP(RS[:, t:t + 1], RS[:, t:t + 1]), reads=[("rs", t)], writes=[("rs", t)])
        P.add("dve", STT(YST[1], X[:, t, :], RS[:, t:t + 1], GTB, ALU.mult, ALU.mult),
              reads=[("x", t), ("rs", t), ("gtbl",)], writes=[("yst", 1)])
        P.add("sp", DMA(pool_s[l, :, 14, :], YST[1][0:NS, :]), reads=[("yst", 1)], dma="out_psb%d" % l)
        mm = []
        for c in range(8):
            g = c // 2
            o = PS[4][:, c * NS:(c + 1) * NS]
            mm.append((o, STT_[0:60, c * 128:(c + 1) * 128], SEL[0:60, g * NS:(g + 1) * NS], True, False))
            mm.append((o, YST[1][0:NS, c * 128:(c + 1) * 128], SELH[0:NS, g * NS:(g + 1) * NS], False, True))
        P.add("pe", MM(mm), reads=[("stt",), ("sel",), ("selh",), ("yst", 1)], writes=[("ps", 4)])
        P.add("dve", COPY(PS_S, PS[4][:, 0:8 * NS].rearrange("p (k n) -> p k n", k=8)), reads=[],
              writes=[("ps", 4), ("pss",)])
        P.add("pe", MM([(PS[2 + g // 2][0:NS, (g % 2) * 256:(g % 2) * 256 + 256], PS_S[:, 2 * g + kk, :],
                         WPS[:, 2 * g + kk, :], kk == 0, kk == 1) for g in range(4) for kk in range(2)]),
              reads=[("pss",), ("wps", 0), ("wps", 1)], writes=[("ps", 2), ("ps", 3)])
        for half in range(2):
            xs = X[0:NS, TS, half * 512:(half + 1) * 512]
            P.add("dve", TT(xs, xs, PS[2 + half][0:NS, :], ALU.add), reads=[("x", TS)],
                  writes=[("ps", 2 + half), ("x", TS)])

    def mlp_phase(l, with_halo):
        tiles = ([TH] if with_halo else []) + list(range(NT)) + [TS]
        norm_tiles(tiles, 4 + l)
        batches = []
        if with_halo:
            batches.append(([TH], CH, 128))
        for b in range(4):
            batches.append((list(range(4 * b, 4 * b + 4)), C0 + 512 * b, 512))
        batches.append(([TS], CS, NS))
        mlp_layer(l, batches)

    def dump_dbg():
        if dbg is not None:
            for t in range(18):
                P.add("sp", DMA(dbg[t], X[:, t, :]), reads=[("x", t)], dma="dbg")

    def colset(k, g, nb):
        d = BR[g][1]
        if d == 1:
            return HT[:, k, C0 + nb * 512:C0 + (nb + 1) * 512]
        if d == 4:
            return HT[:, k, C0 + nb:C0 + nb + 4 * 511 + 1:4]
        return HT[:, k, C0:C0 + 2048].rearrange("p (u r) -> p r u", r=16)[:, 4 * nb:4 * nb + 4, :]

    def chunkset(k, g, ci):
        d = BR[g][1]
        if d == 1:
            return HT[:, k, C0 + ci * 128:C0 + (ci + 1) * 128]
        if d == 4:
            r, jb = ci // 4, ci % 4
            s0 = C0 + 512 * jb + r
            return HT[:, k, s0:s0 + 4 * 127 + 1:4]
        s0 = C0 + ci
        return HT[:, k, s0:s0 + 16 * 127 + 1:16]

    def halo_chunk0(g):
        return (0, 1, 5)[g]

    GROUPS = [[0, 1], [2, 3], [4, 5], [6, 7]]

    def halo_src(g, h, kv_i):
        if g == 0:
            base = exp_dst0[h * 128:(h + 1) * 128, :]
        elif g == 1:
            base = exp_dst1[h * 128:(h + 1) * 128, :]
        else:
            base = exp_dst2[h // 2][(h % 2) * 128:(h % 2 + 1) * 128, :]
        return base.rearrange("p (c x) -> p c x", x=256)[:, :, kv_i * 128:(kv_i + 1) * 128]

    def kv_phase():
        pa = phase_alloc()
        KTS = [vbf(pa(4096), 2048) for _ in range(2)]
        VST = [vbf(pa(2048), 1024).rearrange("p (h x) -> p h x", h=8) for _ in range(2)]
        norm_tiles(list(range(NT)) + [TS], 8)
        ectr = {"kts": 0, "vst": 0, "ps": 0, "pt": 0, "yst": 0}

        def exp_view(g, h):
            if g == 0:
                t = exp_src0[h * 128:(h + 1) * 128, :]
            elif g == 1:
                t = exp_src1[h * 128:(h + 1) * 128, :]
            else:
                t = exp_src2[h // 2][(h % 2) * 128:(h % 2 + 1) * 128, :]
            return t.rearrange("p (c x) -> p c x", x=256)

        for g in (2, 1, 0):
            d = BR[g][1]
            R = d
            NB = 16 // R
            wK = WALL[:, 0:8192].rearrange("p (k n) -> p k n", k=8)
            wV = WALL[:, 8192:16384].rearrange("p (k n) -> p k n", k=8)
            load_w(wK, w_kv[:, g * 2048:g * 2048 + 1024].rearrange("(k p) n -> p k n", p=128),
                   [("wsl", 0), ("wsl", 1)], "w0")
            load_w(wV, w_kv[:, g * 2048 + 1024:g * 2048 + 2048].rearrange("(k p) n -> p k n", p=128),
                   [("wsl", 2), ("wsl", 3)], "w2")
            for h in range(8):
                kb = ectr["kts"] % 2
                ectr["kts"] += 1
                kts = KTS[kb]
                for nb in range(4):
                    bank = ectr["ps"] % 2
                    ectr["ps"] += 1
                    P.add("pe", MM([(PS[bank][:, :], wK[:, k, h * 128:(h + 1) * 128],
                                     HT[:, k, C0 + nb * 512:C0 + (nb + 1) * 512], k == 0, k == 7) for k in range(8)]),
                          reads=[("wsl", 0), ("wsl", 1)] + [hkey(t) for t in range(4 * nb, 4 * nb + 4)],
                          writes=[("ps", bank)])
                    uw = 512 // d
                    P.add("act", ACTI(kts.rearrange("p (r u) -> p r u", r=d)[:, :, nb * uw:(nb + 1) * uw],
                                      PS[bank][:, :].rearrange("p (u r) -> p r u", r=d), AF.Copy),
                          reads=[], writes=[("ps", bank), ("kts", kb, nb)])
                rk = [("kts", kb, nb) for nb in range(4)]
                P.add("sp", DMA(kT_d[g * 8 + h], kts), reads=rk, writes=[("kT_d", g * 8 + h)], dma="kts%d" % kb)
                src = kts.rearrange("p (r u) -> p r u", r=R)[:, :, (NB - 1) * 128:NB * 128]
                P.add("sp", DMA(exp_view(g, h)[:, :, 0:128], src), reads=rk, writes=[("exps", g, h, "k")],
                      dma="kte%d" % kb)
            for ci in range(16):
                pb = 2 + 2 * (ectr["pt"] % 2)
                ectr["pt"] += 1
                vb = ectr["vst"] % 2
                ectr["vst"] += 1
                P.add("pe", MM([(PS[pb + half][:, :], chunkset(k, g, ci), wV[:, k, half * 512:(half + 1) * 512],
                                 k == 0, k == 7) for half in range(2) for k in range(8)]),
                      reads=[("wsl", 2), ("wsl", 3)] + [hkey(t) for t in range(NT)],
                      writes=[("ps", pb), ("ps", pb + 1)])
                for half in range(2):
                    P.add("dve", COPY(VST[vb][:, 4 * half:4 * half + 4, :],
                                      PS[pb + half][:, :].rearrange("p (h x) -> p h x", h=4)),
                          reads=[], writes=[("ps", pb + half), ("vst", vb, half)])
                rk = [("vst", vb, 0), ("vst", vb, 1)]
                P.add("sp", DMA(v_d[g * 8:(g + 1) * 8].rearrange("h p (c x) -> p h c x", x=128)[:, :, ci, :], VST[vb]),
                      reads=rk, writes=[("v_d", g, ci)], dma="vst%d" % vb)
                r, jb = ci // NB, ci % NB
                if jb == NB - 1:
                    if g < 2:
                        srcv = (exp_src0 if g == 0 else exp_src1).rearrange("(h p) (c x) -> p h c x", p=128, x=256)
                        P.add("sp", DMA(srcv[:, :, r, 128:256], VST[vb]), reads=rk, writes=[("exps", g, "v", r)],
                              dma="vse%d" % vb)
                    else:
                        for i in range(4):
                            srcv = exp_src2[i].rearrange("(h p) (c x) -> p h c x", p=128, x=256)
                            P.add("sp", DMA(srcv[:, :, r, 128:256], VST[vb][:, 2 * i:2 * i + 2, :]), reads=rk,
                                  writes=[("exps", g, "v", r, i)], dma="vse%d_%d" % (vb, i))
            if g < 2:
                rk = [("exps", g, h, "k") for h in range(8)] + [("exps", g, "v", r) for r in range(R)]
                P.add("pool", CC(GROUPS, (exp_src0 if g == 0 else exp_src1).opt(), (exp_dst0 if g == 0 else exp_dst1).opt()),
                      reads=rk, writes=[("expd", g, h) for h in range(8)], dma="cc%d" % g, cc=True)
            else:
                for i in range(4):
                    rk = [("exps", g, h, "k") for h in (2 * i, 2 * i + 1)] + [("exps", g, "v", r, i) for r in range(R)]
                    P.add("pool", CC(GROUPS, exp_src2[i].opt(), exp_dst2[i].opt()),
                          reads=rk, writes=[("expd", g, h) for h in (2 * i, 2 * i + 1)], dma="cc2_%d" % i, cc=True)
            nt_out = (1, 4, 16)[g]
            for t in list(range(NT - nt_out, NT)) + [TS]:
                c, n = tile_cols(t)
                for kv_i, wsel in enumerate((wK, wV)):
                    pb = 2 + 2 * (ectr["pt"] % 2)
                    ectr["pt"] += 1
                    slot = ectr["yst"] % 2
                    ectr["yst"] += 1
                    P.add("pe", MM([(PS[pb + half][0:n, :], HT[:, k, c:c + n], wsel[:, k, half * 512:(half + 1) * 512],
                                     k == 0, k == 7) for half in range(2) for k in range(8)]),
                          reads=[("wsl", 2 * kv_i), ("wsl", 2 * kv_i + 1), hkey(t)],
                          writes=[("ps", pb), ("ps", pb + 1)])
                    for half in range(2):
                        P.add("act", ACTI(YST[slot][0:n, half * 512:(half + 1) * 512], PS[pb + half][0:n, :], AF.Copy),
                              reads=[], writes=[("ps", pb + half), ("yst", slot, half)])
                    if t == TS:
                        dst = kvs[g][:, kv_i * 1024:(kv_i + 1) * 1024]
                        wk = [("kvs", g)]
                    else:
                        r0 = (t - (NT - nt_out)) * 128
                        dst = kvp[g][r0:r0 + 128, kv_i * 1024:(kv_i + 1) * 1024]
                        wk = []
                    P.add("sp", DMA(dst, YST[slot][0:n, :]), reads=[("yst", slot, 0), ("yst", slot, 1)], writes=wk,
                          dma="out_kv%d" % slot)

    def attn_layer(l):
        lb = l - 2
        pa = phase_alloc()
        KTG, VVG = [], []
        for g in range(3):
            R = BR[g][1]
            NB = 16 // R
            n = R * (NB + 1) * 128
            KTG.append(vbf(pa(n * 2), n).rearrange("p (r c x) -> p r c x", r=R, x=128))
            VVG.append(vbf(pa(n * 2), n).rearrange("p (r c x) -> p r c x", r=R, x=128))
        OACC = vf32(pa(2 * 2048 * 4), 2 * 2048).rearrange("p (a n) -> p a n", a=2)
        QT = [YALL[:, 0:1024].bitcast(BF16), YALL[:, 1024:2048].bitcast(BF16)]
        WQ = [WALL[:, 8192 + i * 1024:8192 + (i + 1) * 1024].rearrange("p (k n) -> p k n", k=8) for i in range(2)]
        o3 = 8192 + 2048
        TBS = [WALL[:, o3 + i * 768:o3 + (i + 1) * 768] for i in range(2)]
        o3 += 2 * 768
        PPB = [WALL[:, o3 + i * 1024:o3 + (i + 1) * 1024] for i in range(3)]
        o3 += 3 * 1024
        QTS = WALL[:, o3:o3 + 96].rearrange("p (a s) -> p a s", s=NS)
        o3 += 96
        assert o3 <= 16384
        OT2 = [TBL.bitcast(BF16), TBL.bitcast(BF16)]
        DTMP = [HTOKALL.bitcast(F32)[:, 0:512], HTOKALL.bitcast(F32)[:, 512:1024]]
        WO = WALL[:, 0:8192].rearrange("p (h n) -> p h n", h=8)
        load_w(WO, w_o[lb].rearrange("(h p) n -> p h n", p=128), [("wsl", 0), ("wsl", 1)], "w0")
        ctr = {"unit": 0, "ps": 0, "pp": 0, "ud": 0, "sring": 0}
        items = [(h, g) for h in range(8) for g in range(3)]
        DLAG = 1
        deferred = []

        def pop_deferred(maxlen):
            while len(deferred) > maxlen:
                deferred.pop(0)()

        def issue_loads(idx):
            h, g = items[idx]
            gh = g * 8 + h
            R = BR[g][1]
            NB = 16 // R
            par = idx % 2
            P.add("pool", DMA(WQ[par], w_q[lb][:, g * 1024 + h * 128:g * 1024 + (h + 1) * 128]
                              .rearrange("(k p) n -> p k n", p=128)), writes=[("wq", par)], dma="wq%d" % par)
            P.add("pool", DMA(TBS[par], c_bias3[gh]), writes=[("tbs", par)], dma="tbs%d" % par)
            P.add("sp", DMA(KTG[g][:, :, 1:NB + 1, :], kT_d[gh].rearrange("p (r c x) -> p r c x", r=R, x=128)),
                  reads=[("kT_d", gh)], writes=[("kt", g, "own")], dma="ktk%d" % g)
            P.add("sp", DMA(KTG[g][:, :, 0, :], halo_src(g, h, 0)), reads=[("expd", g, h)],
                  writes=[("kt", g, "h")], dma="ktk%d" % g)
            P.add("sp", DMA(VVG[g][:, :, 1:NB + 1, :], v_d[gh].rearrange("p (r c x) -> p r c x", r=R, x=128)),
                  reads=[("v_d", g, ci) for ci in range(16)], writes=[("vv", g, "own")], dma="ktv%d" % g)
            P.add("sp", DMA(VVG[g][:, :, 0, :], halo_src(g, h, 1)), reads=[("expd", g, h)],
                  writes=[("vv", g, "h")], dma="ktv%d" % g)

        def qproj_part(idx, nb):
            h, g = items[idx]
            gh = g * 8 + h
            d = BR[g][1]
            par = idx % 2
            wq = WQ[par]
            qt = QT[par]
            qt3 = qt.rearrange("p (r u) -> p r u", r=d)
            bank = 0
            if nb < 4:
                P.add("pe", MM([(PS[bank][:, :], wq[:, k, :], HT[:, k, C0 + nb * 512:C0 + (nb + 1) * 512], k == 0, k == 7)
                                for k in range(8)]),
                      reads=[("wq", par)] + [hkey(t) for t in range(4 * nb, 4 * nb + 4)], writes=[("ps", bank)])
                uw = 512 // d
                P.add("act", ACTI(qt3[:, :, nb * uw:(nb + 1) * uw], PS[bank][:, :].rearrange("p (u r) -> p r u", r=d),
                                  AF.Copy, scale=SCALE), reads=[], writes=[("ps", bank), ("qt", par, nb)])
            else:
                P.add("pe", MM([(PS[bank][:, 0:NS], wq[:, k, :], HT[:, k, CS:CS + NS], k == 0, k == 7) for k in range(8)]),
                      reads=[("wq", par), hkey(TS)], writes=[("ps", bank)])
                P.add("act", ACTI(QTS[:, gh, :], PS[bank][:, 0:NS], AF.Copy, scale=SCALE),
                      reads=[], writes=[("ps", bank), ("qts", gh)])

        def qproj(idx):
            for nb in range(5):
                qproj_part(idx, nb)

        def make_pv(g, pv_fn, den_fn, oa, pi):
            def f():
                ui = ctr["ud"] % 2
                ctr["ud"] += 1
                ub, db = 4 + 2 * ui, 5 + 2 * ui
                U, DN = PS[ub], PS[db]
                UDv = PSALL[:, ub * 512:(ub + 2) * 512].rearrange("p (a n) -> p a n", a=2)
                P.add("pe", MMX(pv_fn(U) + den_fn(DN)),
                      reads=[("vv", g, "own"), ("vv", g, "h"), ("pp", pi, 0), ("pp", pi, 1), ("onesb",)],
                      writes=[("ps", ub), ("ps", db)])
                if g == 0:
                    P.add("dve", COPY(oa, UDv), reads=[], writes=[("ps", ub), ("ps", db), ("oacc", "u"), ("oacc", "d")])
                else:
                    uv = UDv if g == 1 else UDv.rearrange("p a (m i) -> p a m i", m=4)
                    P.add("dve", TT(oa, oa, uv, ALU.add), reads=[],
                          writes=[("ps", ub), ("ps", db), ("oacc", "u"), ("oacc", "d")])
            return f

        def make_finish(h):
            def f():
                den = OACC[:, 1, :]
                ot = OT2[h % 2]
                P.add("act", ACTI(den, den, AF.Ln), reads=[], writes=[("oacc", "d")])
                P.add("act", ACTI(den, den, AF.Exp, scale=-1.0), reads=[], writes=[("oacc", "d")])
                P.add("dve", TT(ot, OACC[:, 0, :], den, ALU.mult), reads=[], writes=[("ot", 0), ("oacc", "u"), ("oacc", "d")])
                P.add("sp", DMA(ot_d[h], ot), reads=[("ot", 0)], writes=[("ot_d", h)], dma="otd0")
            return f

        issue_loads(0)
        norm_tiles(list(range(NT)) + [TS], l)
        qproj(0)
        for idx, (h, g) in enumerate(items):
            if idx + 1 < len(items):
                issue_loads(idx + 1)
            d = BR[g][1]
            par = idx % 2
            qt = QT[par]
            tbs = TBS[par]
            K4 = KTG[g]
            V4 = VVG[g]
            tb4h = tbs[:, 0:512]
            tb4 = tbs[:, 256:768]
            nunits = 4 if g < 2 else 4
            for u in range(nunits):
                sa = 1 + ctr["sring"] % 3
                sbb = 1 + (ctr["sring"] + 1) % 3
                ctr["sring"] += 2
                ctr["unit"] += 1
                A, B = PS[sa], PS[sbb]
                qk = []
                pv = []
                if g < 2:
                    if g == 0:
                        r, j0 = 0, 4 * u
                        NBg = 16
                    else:
                        r, j0 = u, 0
                        NBg = 4
                    qc = lambda j, n=1: qt[:, (r * NBg + j) * 128:(r * NBg + j + n) * 128]
                    tba = tb4h if j0 == 0 else tb4
                    tbb = tb4
                    qk = [(A, IDENT, tba, True, False),
                          (A[:, 0:128], K4[:, r, j0, :], qc(j0), False, False),
                          (A[:, 128:384], K4[:, r, j0 + 1, :], qc(j0, 2), False, False),
                          (A[:, 384:512], K4[:, r, j0 + 2, :], qc(j0 + 1), False, True),
                          (B, IDENT, tbb, True, False),
                          (B[:, 0:128], K4[:, r, j0 + 2, :], qc(j0 + 2), False, False),
                          (B[:, 128:384], K4[:, r, j0 + 3, :], qc(j0 + 2, 2), False, False),
                          (B[:, 384:512], K4[:, r, j0 + 4, :], qc(j0 + 3), False, True)]
                    if g == 0:
                        oa = OACC[:, :, 512 * u:512 * (u + 1)]
                    else:
                        oa = OACC[:, :, r:r + 4 * 511 + 1:4]
                else:
                    r0 = 4 * u
                    tba = tb4h
                    tbb = tb4h
                    for m in range(4):
                        bank = A if m < 2 else B
                        c = (m % 2) * 256
                        q = qt[:, (r0 + m) * 128:(r0 + m + 1) * 128]
                        if m % 2 == 0:
                            qk.append((bank, IDENT, tb4h, True, False))
                        qk.append((bank[:, c:c + 128], K4[:, r0 + m, 0, :], q, False, False))
                        qk.append((bank[:, c + 128:c + 256], K4[:, r0 + m, 1, :], q, False, m % 2 == 1))
                    oa = OACC[:, :, r0:r0 + 16 * 127 + 4].rearrange("p a (i m) -> p a m i", m=16)[:, :, 0:4, :] \
                        if False else None
                P.add("pe", MMX(qk), reads=[("kt", g, "own"), ("kt", g, "h"), ("tbs", par), ("ident",)] +
                      [("qt", par, nb) for nb in range(4)], writes=[("ps", sa), ("ps", sbb)])
                pi = ctr["pp"] % 3
                ctr["pp"] += 1
                pp = PPB[pi]
                P.add("act", ACTI(pp[:, 0:512], A, AF.Exp), reads=[], writes=[("ps", sa), ("pp", pi, 0)])
                P.add("act", ACTI(pp[:, 512:1024], B, AF.Exp), reads=[], writes=[("ps", sbb), ("pp", pi, 1)])
                sl = lambda s_, n=1, pp=pp: pp[:, s_ * 128:(s_ + n) * 128]
                if g < 2:
                    def pv_fn(U, r=r, j0=j0, sl=sl, V4=V4):
                        return [(U[:, 0:128], V4[:, r, j0, :], sl(0), True, False),
                                (U[:, 0:256], V4[:, r, j0 + 1, :], sl(1, 2), False, False),
                                (U[:, 128:384], V4[:, r, j0 + 2, :], sl(3, 2), False, False),
                                (U[:, 256:512], V4[:, r, j0 + 3, :], sl(5, 2), False, False),
                                (U[:, 384:512], V4[:, r, j0 + 4, :], sl(7), False, True)]
                else:
                    def pv_fn(U, r0=r0, sl=sl, V4=V4):
                        out = []
                        for m in range(4):
                            out.append((U[:, m * 128:(m + 1) * 128], V4[:, r0 + m, 0, :], sl(2 * m), m == 0, False))
                            out.append((U[:, m * 128:(m + 1) * 128], V4[:, r0 + m, 1, :], sl(2 * m + 1), False, m == 3))
                        return out
                    oa = OACC[:, :, :].rearrange("p a (i r) -> p a r i", r=16)[:, :, r0:r0 + 4, :]
                ppv = pp.rearrange("p (b s x) -> p s b x", s=2, x=128)

                def den_fn(DN, ppv=ppv):
                    return [(DN[:, :], ONESB, ppv[:, 0, :, :], True, False), (DN[:, :], ONESB, ppv[:, 1, :, :], False, True)]
                pv, den = pv_fn, den_fn
                if idx + 1 < len(items):
                    qproj_part(idx + 1, u)
                    if u == 3:
                        qproj_part(idx + 1, 4)
                deferred.append(make_pv(g, pv, den, oa, pi))
                pop_deferred(DLAG)
            if g == 2:
                deferred.append(make_finish(h))
        pop_deferred(0)
        P.barrier()
        OTT = [YALL[:, 0:512].bitcast(BF16).rearrange("p (h c) -> p h c", h=8),
               YALL[:, 512:1024].bitcast(BF16).rearrange("p (h c) -> p h c", h=8)]

        def wo_load(t):
            ob = t % 2
            P.add("sp", DMA(OTT[ob], ot_d[:, :, t * 128:(t + 1) * 128].rearrange("h p c -> p h c")),
                  writes=[("ott", ob)], dma="ott%d" % ob)

        def wo_tile(t):
            ob = t % 2
            pb = 0
            P.add("pe", MM([(PS[pb + half][:, :], OTT[ob][:, hh, :], WO[:, hh, half * 512:(half + 1) * 512], hh == 0, hh == 7)
                            for half in range(2) for hh in range(8)]),
                  reads=[("ott", ob), ("wsl", 0), ("wsl", 1)], writes=[("ps", pb), ("ps", pb + 1)])
            for half in range(2):
                xs = X[:, t, half * 512:(half + 1) * 512]
                P.add("dve", TT(xs, xs, PS[pb + half][:, :], ALU.add), reads=[("x", t)],
                      writes=[("ps", pb + half), ("x", t)])
            if t + 2 < NT:
                wo_load(t + 2)
        pa2 = phase_alloc()
        KC = [vf32(pa2(4096), 1024) for _ in range(2)]
        KB = [vf32(pa2(4096), 1024) for _ in range(2)]
        PROD = [vf32(pa2(4096), 1024) for _ in range(2)]
        VC = [vbf(pa2(2048), 1024) for _ in range(2)]
        VB = [vbf(pa2(2048), 1024) for _ in range(2)]
        BS = vf32(pa2(3 * 8 * 4), 24).rearrange("p (g h) -> p g h", g=3)
        B0 = vf32(pa2(3 * 8 * 4), 24).rearrange("p (g h) -> p g h", g=3)
        SC = [vf32(pa2(32), 8) for _ in range(2)]
        SCB = [vf32(pa2(32), 8) for _ in range(2)]
        PA = [vbf(pa2(64), 8) for _ in range(2)]
        PB = [vbf(pa2(64), 8) for _ in range(2)]
        OSA = vf32(pa2(NS * 16 * 4), NS * 16).rearrange("p (s x) -> p s x", s=NS)
        RD = vf32(pa2(NS * 8 * 4), NS * 8).rearrange("p (s x) -> p s x", s=NS)
        OTS = vbf(pa2(8 * NS * 2), 8 * NS).rearrange("p (h s) -> p h s", h=8)
        P.add("sp", DMA(BS, c_biasS.rearrange("g p h -> p g h")), writes=[("bs",)], dma="@sb%d" % l)
        P.add("sp", DMA(B0[0:1, :, :], c_bias0.rearrange("(o g) h -> o g h", o=1)), writes=[("b0",)], dma="@sb%d" % l)
        sitems = [(s, g) for s in range(NS) for g in range(3)]

        def s_loads(it):
            s, g = sitems[it]
            b = it % 2
            d = BR[g][1]
            P.add("sp", DMA(KC[b], cache[g][s, 0:127 * d + 1:d, 0:1024]), writes=[("kc", b)], dma="sc_kc%d" % b)
            P.add("pool", DMA(VC[b], cache[g][s, 0:127 * d + 1:d, 1024:2048]), writes=[("vc", b)], dma="sc_vc%d" % b)
            P.add("sp", DMA(KB[b][0:1, :], kvs[g][s:s + 1, 0:1024]), reads=[("kvs", g)], writes=[("kb", b)],
                  dma="sc_kb%d" % b)
            P.add("pool", DMA(VB[b][0:1, :], kvs[g][s:s + 1, 1024:2048]), reads=[("kvs", g)], writes=[("vb", b)],
                  dma="sc_vb%d" % b)

        def s_scores(it):
            s, g = sitems[it]
            b = it % 2
            qb0 = 2 if b == 0 else 4
            P.add("pe", MM([(PS[qb0 + hh // 4][:, (hh % 4) * 128:(hh % 4) * 128 + 128],
                             QTS[:, g * 8 + hh, s:s + 1].to_broadcast([128, 128]), IDENT, True, True)
                            for hh in range(8)]),
                  reads=[("ident",)], writes=[("ps", qb0), ("ps", qb0 + 1)])
            for half in range(2):
                P.add("dve", TT(PROD[b][:, half * 512:(half + 1) * 512], KC[b][:, half * 512:(half + 1) * 512],
                                PS[qb0 + half][:, :], ALU.mult), reads=[("kc", b), ("ps", qb0 + half)],
                      writes=[("prod", b, half)])
            P.add("dve", REDUCE(SC[b], PROD[b].rearrange("p (h x) -> p h x", h=8), AX.X, ALU.add),
                  reads=[("prod", b, 0), ("prod", b, 1)], writes=[("sc", b)])
            for half in range(2):
                P.add("dve", TT(PROD[b][0:1, half * 512:(half + 1) * 512], KB[b][0:1, half * 512:(half + 1) * 512],
                                PS[qb0 + half][0:1, :], ALU.mult), reads=[("kb", b), ("sc", b)],
                      writes=[("prod", b, half), ("ps", qb0 + half)])
            P.add("dve", REDUCE(SCB[b][0:1, :], PROD[b][0:1, :].rearrange("p (h x) -> p h x", h=8), AX.X, ALU.add),
                  reads=[("prod", b, 0), ("prod", b, 1)], writes=[("scb", b)])
            P.add("dve", TT(SC[b], SC[b], BS[:, g, :], ALU.add), reads=[("bs",)], writes=[("sc", b)])
            P.add("dve", TT(SCB[b][0:1, :], SCB[b][0:1, :], B0[0:1, g, :], ALU.add), reads=[("b0",)], writes=[("scb", b)])
            P.add("act", ACTI(PA[b], SC[b], AF.Exp), reads=[("sc", b)], writes=[("pa", b)])
            P.add("act", ACTI(PB[b][0:1, :], SCB[b][0:1, :], AF.Exp), reads=[("scb", b)], writes=[("pb", b)])

        def s_pv(it):
            s, g = sitems[it]
            b = it % 2
            pvb = 6 + b
            items2 = []
            for hh in range(8):
                items2.append((PS[pvb][:, hh:hh + 1], VC[b][:, hh * 128:(hh + 1) * 128], PA[b][:, hh:hh + 1], True, False))
                items2.append((PS[pvb][:, hh:hh + 1], VB[b][0:1, hh * 128:(hh + 1) * 128], PB[b][0:1, hh:hh + 1],
                               False, True))
            items2.append((PS[pvb][:, 8:16], ONESB, PA[b], True, False))
            items2.append((PS[pvb][:, 8:16], ONESB[0:1, :], PB[b][0:1, :], False, True))
            P.add("pe", MM(items2), reads=[("vc", b), ("vb", b), ("pa", b), ("pb", b), ("onesb",)], writes=[("ps", pvb)])
            if g == 0:
                P.add("dve", COPY(OSA[:, s, :], PS[pvb][:, 0:16]), reads=[], writes=[("ps", pvb), ("osa", s)])
            else:
                P.add("dve", TT(OSA[:, s, :], OSA[:, s, :], PS[pvb][:, 0:16], ALU.add), reads=[],
                      writes=[("ps", pvb), ("osa", s)])

        wo_load(0)
        wo_load(1)
        s_loads(0)
        s_loads(1)
        s_scores(0)
        wo_next = [0]

        def wo_some(n):
            for _ in range(n):
                if wo_next[0] < NT:
                    wo_tile(wo_next[0])
                    wo_next[0] += 1

        for it in range(len(sitems)):
            wo_some(2 if it % 3 == 0 else 1)
            if it + 1 < len(sitems):
                s_scores(it + 1)
            s_pv(it)
            if it + 2 < len(sitems):
                s_loads(it + 2)
        wo_some(NT)
        P.add("dve", RECIP(RD, OSA[:, :, 8:16]), reads=[("osa", s) for s in range(NS)], writes=[("rd",)])
        P.add("dve", TT(OTS.rearrange("p h s -> p s h"), OSA[:, :, 0:8], RD, ALU.mult),
              reads=[("rd",)] + [("osa", s) for s in range(NS)], writes=[("ots",)])
        P.add("pe", MM([(PS[half][0:NS, :], OTS[:, hh, :], WO[:, hh, half * 512:(half + 1) * 512], hh == 0, hh == 7)
                        for half in range(2) for hh in range(8)]),
              reads=[("ots",), ("wsl", 0), ("wsl", 1)], writes=[("ps", 0), ("ps", 1)])
        for half in range(2):
            xs = X[0:NS, TS, half * 512:(half + 1) * 512]
            P.add("dve", TT(xs, xs, PS[half][0:NS, :], ALU.add), reads=[("x", TS)],
                  writes=[("ps", half), ("x", TS)])

    def final_norm():
        P.add("sp", DMA(TBL, gvec[9].partition_broadcast(128)), writes=[("tbl",)], dma="tblf")
        for t in list(range(NT)) + [TS]:
            slot = t % 2
            P.add("act", ACTI(YST[slot], X[:, t, :], AF.Square, accum_out=SS[:, t:t + 1]),
                  reads=[("x", t)], writes=[("yst", slot), ("ss", t)])
            P.add("act", ACTI(RS[:, t:t + 1], SS[:, t:t + 1], AF.Sqrt, bias=EPS, scale=1.0 / D),
                  reads=[("ss", t)], writes=[("rs", t)])
            P.add("dve", RECIP(RS[:, t:t + 1], RS[:, t:t + 1]), reads=[("rs", t)], writes=[("rs", t)])
            P.add("dve", STT(YST[slot], X[:, t, :], RS[:, t:t + 1], TBL, ALU.mult, ALU.mult),
                  reads=[("x", t), ("rs", t), ("tbl",)], writes=[("yst", slot)])
            if t == TS:
                P.add("sp", DMA(y_s, YST[slot][0:NS, :]), reads=[("yst", slot)], dma="out_y%d" % slot)
            else:
                P.add("sp", DMA(y_p[t], YST[slot]), reads=[("yst", slot)], dma="out_y%d" % slot)

    stages = ["pool0", "mlp0", "pool1", "mlp1", "kv", "attn2", "mlp2", "attn3", "mlp3"]
    last = stages.index(stop_after) if stop_after else len(stages) - 1

    def run_stage(i):
        name = stages[i]
        if name == "pool0":
            pool_layer(0)
        elif name == "mlp0":
            mlp_phase(0, True)
        elif name == "pool1":
            pool_layer(1)
        elif name == "mlp1":
            mlp_phase(1, False)
        elif name == "kv":
            kv_phase()
        elif name == "attn2":
            attn_layer(2)
        elif name == "mlp2":
            mlp_phase(2, False)
        elif name == "attn3":
            attn_layer(3)
        elif name == "mlp3":
            mlp_phase(3, False)

    for i in range(last + 1):
        run_stage(i)
        P.barrier()
    dump_dbg()
    final_norm()
    P.emit(nc, es)
    es.close()
    return nc, P


def make_in_maps(inputs):
    f = lambda a: np.ascontiguousarray(np.asarray(a, dtype=np.float32))
    x_prompt = f(inputs["x_prompt"]); x_sample = f(inputs["x_sample"]); state_pool = f(inputs["state_pool"])
    caches = [f(inputs["cache_kv_w128"]), f(inputs["cache_kv_w512"]), f(inputs["cache_kv_w2048"])]
    rel_bias = f(inputs["rel_bias"])
    gvec = np.concatenate([f(inputs["norm_mix"]), f(inputs["norm_mlp"]), f(inputs["norm_kv"])[None],
                           f(inputs["norm_final"])[None], f(inputs["pool_scale"])], 0)
    shared = {
        "gvec": gvec, "pool_w": f(inputs["pool_w"]), "mlp_in": f(inputs["mlp_in"]), "mlp_out": f(inputs["mlp_out"]),
        "w_kv": f(inputs["w_kv"]), "w_q": f(inputs["w_q"]), "w_o": f(inputs["w_o"]),
        "c_ident": np.eye(128, dtype=np.float32),
    }
    sel = np.zeros((NS, 15, 4, NS), np.float32)
    for g, w in enumerate((2, 4, 8, 16)):
        for s in range(NS):
            sel[s, 15 - (w - 1):, g, s] = 1.0 / w
    shared["c_sel"] = sel.reshape(60, 16)
    selh = np.zeros((NS, 4, NS), np.float32)
    for g, w in enumerate((2, 4, 8, 16)):
        for s_ in range(NS):
            selh[s_, g, s_] = 1.0 / w - 1.0
    shared["c_selh"] = selh.reshape(NS, 16)
    j, valid = toeplitz_index()
    bf = np.zeros((24, 128, 256), np.float32)
    for g, (w, d) in enumerate(BR):
        bk = t5_bucket_np(j * d)
        for h in range(8):
            bf[g * 8 + h] = rel_bias[bk, g * 8 + h]
    nx, sm = bf[:, :, 0:128], bf[:, :, 128:256]
    vn, vs = valid[:, 0:128] > 0, valid[:, 128:256] > 0
    NEG = np.float32(-30000.0)
    nxm = np.where(vn[None], nx, NEG)
    smm = np.where(vs[None], sm, NEG)
    dead = np.full_like(nxm, NEG)
    tabs = []
    for has_halo in (False, True):
        hn = nxm if has_halo else dead
        t01 = np.concatenate([hn, smm, nxm, smm, nxm, smm], 2)
        t2 = np.concatenate([hn, smm, hn, smm, dead, dead], 2)
        tabs.append(np.ascontiguousarray(np.concatenate([t01[0:16], t2[16:24]], 0)))
    bS = np.zeros((3, 128, 8), np.float32)
    b0 = np.zeros((3, 8), np.float32)
    for g, (w, d) in enumerate(BR):
        bk = t5_bucket_np((128 - np.arange(128)) * d)
        bS[g] = rel_bias[bk, g * 8:(g + 1) * 8]
        b0[g] = rel_bias[0, g * 8:(g + 1) * 8]
    shared["c_biasS"] = bS
    shared["c_bias0"] = b0
    in_maps = []
    for c in range(8):
        b, half = c // 2, c % 2
        xin = np.zeros((18, 128, D), np.float32)
        xin[0:NT] = x_prompt[b, half * 2048:(half + 1) * 2048].reshape(NT, 128, D)
        xin[TS, 0:NS] = x_sample[NS * c:NS * (c + 1), 0, :]
        if half == 1:
            xin[TH] = x_prompt[b, 1920:2048]
        ap = np.zeros((2, 4, 2, 128, 128), np.float32)
        tt_src = np.arange(128)[:, None]
        tt_dst = np.arange(128)[None, :]
        for g, w in enumerate((2, 4, 8, 16)):
            for first in range(2):
                if first == 1 and half == 0:
                    cnt = np.minimum(np.arange(128) + 1, w).astype(np.float32)[None, :]
                else:
                    cnt = np.full((1, 128), float(w), np.float32)
                dist_same = tt_dst - tt_src
                dist_prev = tt_dst + 128 - tt_src
                ap[first, g, 1] = ((dist_same >= 0) & (dist_same < w)) / cnt - (dist_same == 0)
                ap[first, g, 0] = ((dist_prev >= 0) & (dist_prev < w)) / cnt
        m = dict(shared)
        m["xin"] = xin
        m["state"] = np.ascontiguousarray(state_pool[:, NS * c:NS * (c + 1)].reshape(2, 60, D))
        for g in range(3):
            m["cache%d" % g] = np.ascontiguousarray(caches[g][NS * c:NS * (c + 1)].reshape(NS, -1, 2048))
        m["c_apool"] = ap
        m["c_bias3"] = tabs[half]
        in_maps.append(m)
    return in_maps


_CACHE = {}


def kernel(**inputs):
    if "nc" not in _CACHE:
        _CACHE["nc"] = build_program()[0]
    nc = _CACHE["nc"]
    in_maps = make_in_maps(inputs)
    res = run_bass_kernel_spmd(nc, in_maps, core_ids=list(range(8)))
    R = res.results
    y_prompt = np.zeros((4, 4096, D), np.float32)
    y_sample = np.zeros((32, 1, D), np.float32)
    pool_p = np.zeros((2, 4, 15, D), np.float32)
    pool_s = np.zeros((2, 32, 15, D), np.float32)
    kvp = [np.zeros((4, w, 2, 8, 128), np.float32) for (w, d) in BR]
    kvs = [np.zeros((32, 1, 2, 8, 128), np.float32) for _ in BR]
    for c in range(8):
        b, half = c // 2, c % 2
        r = R[c]
        y_prompt[b, half * 2048:(half + 1) * 2048] = r["y_p"].reshape(2048, D)
        y_sample[NS * c:NS * (c + 1), 0] = r["y_s"]
        pool_s[:, NS * c:NS * (c + 1)] = r["pool_s"]
        for g in range(3):
            kvs[g][NS * c:NS * (c + 1), 0] = r["kvs%d" % g].reshape(NS, 2, 8, 128)
        if half == 1:
            pool_p[:, b] = r["pool_p"]
            for g, (w, d) in enumerate(BR):
                kvp[g][b] = r["kvp%d" % g].reshape(w, 2, 8, 128)
    return (y_prompt, y_sample, pool_p, pool_s, kvp[0], kvs[0], kvp[1], kvs[1], kvp[2], kvs[2])
```

```python
import math
from contextlib import ExitStack

import numpy as np
import concourse.bass as bass
import concourse.mybir as mybir
from concourse.bass_utils import run_bass_kernel_spmd

F32 = mybir.dt.float32
BF16 = mybir.dt.bfloat16
AF = mybir.ActivationFunctionType
ALU = mybir.AluOpType
AX = mybir.AxisListType

D = 1024
NT = 16
TS = 16
TH = 17
NS = 4
HID = 4096
EPS = 1e-6
BR = ((128, 1), (512, 4), (2048, 16))
CH = 16
C0 = CH + 128
CS = C0 + 2048
CW = CS + NS
SCALE = 128 ** -0.5


def TT(out, a, b, op):
    return lambda e: e.tensor_tensor(out, a, b, op)


def STT(out, a, s, b, op0, op1):
    return lambda e: e.scalar_tensor_tensor(out, a, s, b, op0, op1)


def TSC(out, a, s1, op0):
    return lambda e: e.tensor_scalar(out, a, s1, None, op0)


def TSC2(out, a, s1, s2, op0, op1):
    return lambda e: e.tensor_scalar(out, a, s1, s2, op0, op1)


def ACTI(out, in_, func, **kw):
    return lambda e: e.activation(out, in_, func, **kw)


def DMA(out, in_, **kw):
    return lambda e: e.dma_start(out=out, in_=in_, **kw)


def MM(items):
    def f(e):
        ins = None
        for (o, l, r, st, sp) in items:
            ins = e.matmul(o, l, r, start=st, stop=sp)
        return ins
    return f


def MMX(items):
    def f(e):
        ins = None
        for (o, l, r, st, sp) in items:
            ins = e.matmul(o, l, r, start=st, stop=sp, skip_group_check=True)
        return ins
    return f


def TRS(items, ident):
    def f(e):
        ins = None
        for (o, i) in items:
            ins = e.transpose(o, i, ident)
        return ins
    return f


def RECIP(out, in_):
    return lambda e: e.reciprocal(out, in_)


def MEMSET(ap, v):
    return lambda e: e.memset(ap, v)


def COPY(out, in_):
    return lambda e: e.tensor_copy(out, in_)


def REDUCE(out, in_, axis, op):
    return lambda e: e.tensor_reduce(out, in_, axis, op)


def CC(groups, src, dst):
    return lambda e: e.collective_compute("AllGather", ALU.bypass, replica_groups=groups, ins=[src], outs=[dst])


class Op:
    __slots__ = ("eng", "fn", "deps", "need", "is_dma", "semkey", "sig", "idx", "inc")


class Prog:
    ENGS = ("pe", "act", "dve", "pool", "sp")

    def __init__(self):
        self.ops = []
        self.lastw = {}
        self.readers = {}
        self.pending_barrier = {e: [] for e in self.ENGS}
        self.last_on = {}

    def add(self, eng, fn, reads=(), writes=(), dma=None, cc=False):
        op = Op()
        op.eng = eng
        op.fn = fn
        op.is_dma = dma is not None
        op.semkey = dma
        op.need = op.is_dma
        op.sig = None
        op.inc = 1 if (cc or dma is None) else 16
        deps = set()
        lw = self.lastw
        rd = self.readers
        for k in reads:
            w = lw.get(k)
            if w is not None:
                deps.add(w)
        for k in writes:
            w = lw.get(k)
            if w is not None:
                deps.add(w)
            r = rd.get(k)
            if r:
                deps.update(r)
        if eng == "pe" and not op.is_dma:
            deps = {d for d in deps if d.is_dma or d.eng != "pe"}
        pb = self.pending_barrier[eng]
        if pb:
            deps.update(pb)
            self.pending_barrier[eng] = []
        for d in deps:
            d.need = True
        op.deps = deps
        for k in reads:
            rd.setdefault(k, []).append(op)
        for k in writes:
            lw[k] = op
            rd[k] = []
        op.idx = len(self.ops)
        self.ops.append(op)
        self.last_on[(eng, op.is_dma, dma)] = op
        return op

    def barrier(self):
        lasts = list(self.last_on.values())
        for o in lasts:
            o.need = True
        for e in self.ENGS:
            self.pending_barrier[e] = list(lasts)
        self.lastw = {}
        self.readers = {}

    def emit(self, nc, es):
        SEM_ROT = 12000
        DMA_ROT = 700
        comp_count = {e: 0 for e in self.ENGS}
        sem_pool = {}

        def get_sem(name):
            s = sem_pool.get(name)
            if s is None:
                s = es.enter_context(nc.semaphore(name))
                sem_pool[name] = s
            return s

        totals = {}
        for op in self.ops:
            if op.is_dma:
                totals[op.semkey] = totals.get(op.semkey, 0) + 1
        dma_cnt = {}
        for op in self.ops:
            if op.is_dma:
                c = dma_cnt.get(op.semkey, 0) + 1
                dma_cnt[op.semkey] = c
                if op.semkey.startswith("@"):
                    assert totals[op.semkey] <= DMA_ROT
                    op.sig = (get_sem("dg_" + op.semkey[1:]), op.inc * totals[op.semkey])
                else:
                    gen = (c - 1) // DMA_ROT
                    op.sig = (get_sem("d_%s_%d" % (op.semkey, gen)), op.inc * (c - gen * DMA_ROT))
            elif op.need:
                c = comp_count[op.eng] + 1
                comp_count[op.eng] = c
                gen = (c - 1) // SEM_ROT
                op.sig = (get_sem("c_%s_%d" % (op.eng, gen)), c - gen * SEM_ROT)
        self.n_sems = len(sem_pool)
        final_dma = {}
        for op in self.ops:
            if op.is_dma:
                k = id(op.sig[0])
                if k not in final_dma or final_dma[k][1] < op.sig[1]:
                    final_dma[k] = op.sig
        per_eng = {e: [] for e in self.ENGS}
        for op in self.ops:
            per_eng[op.eng].append(op)
        self.n_waits = 0
        block = es.enter_context(nc.Block())

        def run(engname, e):
            waited = {}
            for op in per_eng[engname]:
                need = {}
                for d in op.deps:
                    s, v = d.sig
                    k = id(s)
                    if waited.get(k, 0) >= v:
                        continue
                    if k not in need or need[k][1] < v:
                        need[k] = (s, v)
                for k, (s, v) in need.items():
                    e.wait_ge(s, v)
                    waited[k] = v
                    self.n_waits += 1
                ins = op.fn(e)
                if op.sig is not None:
                    if op.is_dma and op.inc == 1:
                        ins.then_inc(op.sig[0])
                    else:
                        ins.then_inc(op.sig[0], op.inc)
            if engname == "sp":
                for k, (s, v) in final_dma.items():
                    if waited.get(k, 0) < v:
                        e.wait_ge(s, v)

        @block.tensor
        def _(e):
            run("pe", e)

        @block.scalar
        def _(e):
            run("act", e)

        @block.vector
        def _(e):
            run("dve", e)

        @block.gpsimd
        def _(e):
            run("pool", e)

        @block.sync
        def _(e):
            run("sp", e)


def t5_bucket_np(dist):
    dist = np.asarray(dist, np.int64)
    max_exact = 16
    df = np.maximum(dist, 1).astype(np.float32)
    large = max_exact + (np.log(df / np.float32(max_exact)) / np.float32(math.log(2048 / max_exact))
                         * np.float32(32 - max_exact)).astype(np.int32)
    large = np.minimum(large, 31)
    return np.where(dist < max_exact, dist, large).astype(np.int64)


def toeplitz_index():
    k = np.arange(128)[:, None]
    c = np.arange(256)[None, :]
    j = np.where(c < 128, 128 + c - k, c - 128 - k)
    valid = np.where(c < 128, k >= c, (c - 128) >= k)
    j = np.where(valid, j, 0)
    return j, valid.astype(np.float32)


def build_program(stop_after=None, debug=False):
    nc = bass.Bass("TRN2", target_bir_lowering=False)
    P = Prog()

    def din(name, shape, dt=F32):
        return nc.dram_tensor(name, list(shape), dt, kind="ExternalInput").ap()

    def dout(name, shape, dt=F32):
        return nc.dram_tensor(name, list(shape), dt, kind="ExternalOutput").ap()

    def dscr(name, shape, dt=BF16):
        return nc.dram_tensor(name, list(shape), dt).ap()

    xin = din("xin", [18, 128, D])
    state = din("state", [2, 60, D])
    cache = [din("cache0", [NS, 128, 2048]), din("cache1", [NS, 512, 2048]), din("cache2", [NS, 2048, 2048])]
    gvec = din("gvec", [12, D])
    pool_w = din("pool_w", [2, 4, 256, 256])
    mlp_in = din("mlp_in", [4, D, HID])
    mlp_out = din("mlp_out", [4, HID, D])
    w_kv = din("w_kv", [D, 6144])
    w_q = din("w_q", [2, D, 3072])
    w_o = din("w_o", [2, D, D])
    c_ident = din("c_ident", [128, 128])
    c_apool = din("c_apool", [2, 4, 2, 128, 128])
    c_selh = din("c_selh", [NS, 16])
    c_sel = din("c_sel", [60, 16])
    c_bias3 = din("c_bias3", [24, 128, 768])
    c_biasS = din("c_biasS", [3, 128, 8])
    c_bias0 = din("c_bias0", [3, 8])
    y_p = dout("y_p", [NT, 128, D])
    y_s = dout("y_s", [NS, D])
    pool_p = dout("pool_p", [2, 15, D])
    pool_s = dout("pool_s", [2, NS, 15, D])
    kvp = [dout("kvp0", [128, 2048]), dout("kvp1", [512, 2048]), dout("kvp2", [2048, 2048])]
    kvs = [dout("kvs%d" % g, [NS, 2048]) for g in range(3)]
    dbg = dout("dbg", [18, 128, D]) if debug else None
    kT_d = dscr("kT_d", [24, 128, 2048])
    v_d = dscr("v_d", [24, 128, 2048])
    exp_src0 = dscr("exp_src0", [1024, 256])
    exp_dst0 = dscr("exp_dst0", [2048, 256])
    exp_src1 = dscr("exp_src1", [1024, 1024])
    exp_dst1 = dscr("exp_dst1", [2048, 1024])
    exp_src2 = [dscr("exp_src2_%d" % i, [256, 4096]) for i in range(4)]
    exp_dst2 = [dscr("exp_dst2_%d" % i, [512, 4096]) for i in range(4)]
    ot_d = dscr("ot_d", [8, 128, 2048])

    es = ExitStack()
    ARENA_BYTES = 212000
    arena = es.enter_context(nc.sbuf_tensor("arena", [128, ARENA_BYTES // 2], BF16))
    pos = [0]

    def alloc(nbytes):
        off = (pos[0] + 63) // 64 * 64
        pos[0] = off + nbytes
        assert pos[0] <= ARENA_BYTES, ("arena overflow", pos[0])
        return off

    def vbf(off, n):
        return arena[:, off // 2: off // 2 + n]

    def vf32(off, n):
        return arena[:, off // 2: off // 2 + 2 * n].bitcast(F32)

    X = vf32(alloc(18 * D * 4), 18 * D).rearrange("p (t d) -> p t d", t=18)
    HT = vbf(alloc(8 * CW * 2), 8 * CW).rearrange("p (k c) -> p k c", k=8)
    WALL = vbf(alloc(4 * 8192), 4 * 4096)
    WSL = [WALL[:, i * 4096:(i + 1) * 4096] for i in range(4)]
    IDENT = vbf(alloc(256), 128)
    ONESB = vbf(alloc(256), 128)
    ONES32 = vf32(alloc(512), 128)
    GCOL = vf32(alloc(12 * 8 * 4), 96).rearrange("p (v k) -> p v k", v=12)
    SS = vf32(alloc(32 * 4), 32)
    RS = vf32(alloc(32 * 4), 32)
    FLAG = vf32(alloc(64), 2)
    TBL = vf32(alloc(4096), 1024)
    HTOKALL = vbf(alloc(4096), 2048)
    HTOK = [HTOKALL[:, 0:1024], HTOKALL[:, 1024:2048]]
    YALL = vf32(alloc(8192), 2048)
    YST = [YALL[:, 0:1024], YALL[:, 1024:2048]]
    PH0 = alloc(0)

    def phase_alloc():
        st = [PH0]

        def a(nbytes):
            off = (st[0] + 63) // 64 * 64
            st[0] = off + nbytes
            assert st[0] <= ARENA_BYTES, ("phase overflow", st[0] - PH0, ARENA_BYTES - PH0)
            return off
        return a

    PSALL = es.enter_context(nc.psum_tensor("psall", [128, 4096], F32))
    PS = [PSALL[:, i * 512:(i + 1) * 512] for i in range(8)]

    def psb(b):
        return PS[b][:, :].bitcast(BF16)

    P.add("pool", DMA(IDENT, c_ident), writes=[("ident",)], dma="ident")
    P.add("sp", DMA(GCOL, gvec.rearrange("v (k p) -> p v k", p=128), allow_slow_non_contiguous=True),
          writes=[("gcol",)], dma="@setup2")
    P.add("dve", MEMSET(ONESB, 1.0), writes=[("onesb",)])
    P.add("dve", MEMSET(ONES32, 1.0), writes=[("ones32",)])
    P.add("dve", MEMSET(HT[:, :, 0:CH], 0.0), writes=[("hT", "Z")])
    def load_x(tiles, extra_reads=()):
        for t in tiles:
            P.add("sp", DMA(X[:, t, :], xin[t]), reads=list(extra_reads), writes=[("x", t)], dma="xin%d" % t)

    load_x([TH, 0, 1])

    def tile_cols(t):
        if t == TH:
            return CH, 128
        if t == TS:
            return CS, NS
        return C0 + 128 * t, 128

    def hkey(t):
        return ("hT", t)

    tr_ctr = [0]

    def norm_a(t, gidx, h32_tbl=None, h32_slot=0):
        buf = tr_ctr[0] % 2
        tr_ctr[0] += 1
        htok = HTOK[buf]
        bank = 6 + buf
        c, n = tile_cols(t)
        P.add("act", ACTI(htok, X[:, t, :], AF.Square, accum_out=SS[:, t:t + 1]),
              reads=[("x", t)], writes=[("htok", buf), ("ss", t)])
        P.add("act", ACTI(RS[:, t:t + 1], SS[:, t:t + 1], AF.Sqrt, bias=EPS, scale=1.0 / D),
              reads=[("ss", t)], writes=[("rs", t)])
        P.add("dve", RECIP(RS[:, t:t + 1], RS[:, t:t + 1]), reads=[("rs", t)], writes=[("rs", t)])
        P.add("dve", TSC(htok, X[:, t, :], RS[:, t:t + 1], ALU.mult),
              reads=[("x", t), ("rs", t)], writes=[("htok", buf)])
        if h32_tbl is not None:
            P.add("dve", STT(YST[h32_slot], X[:, t, :], RS[:, t:t + 1], h32_tbl, ALU.mult, ALU.mult),
                  reads=[("x", t), ("rs", t), ("gtbl",)], writes=[("yst", h32_slot)])
        pv = psb(bank)
        P.add("pe", TRS([(pv[:, k * 128:(k + 1) * 128], htok[:, k * 128:(k + 1) * 128]) for k in range(8)], IDENT),
              reads=[("htok", buf), ("ident",)], writes=[("ps", bank)])

        def evac():
            P.add("dve", TT(HT[:, :, c:c + n], pv.rearrange("p (k n) -> p k n", k=8)[:, :, 0:n],
                            GCOL[:, gidx, :].unsqueeze(2).broadcast_to([128, 8, n]), ALU.mult),
                  reads=[("gcol",)], writes=[("ps", bank), hkey(t)])
        return evac

    def norm_tiles(tiles, gidx, hooks=None):
        pend = None
        for t in tiles:
            hk = hooks.get(t) if hooks else None
            ev = norm_a(t, gidx, *(hk[0] if hk else ()))
            if hk:
                hk[1]()
            if pend is not None:
                pend()
            pend = ev
        if pend is not None:
            pend()

    def norm_tile(t, gidx):
        norm_a(t, gidx)()

    def load_w(slot_ap, src_ap, keys, semkey):
        P.add("pool", DMA(slot_ap, src_ap), writes=keys, dma=semkey)

    def mlp_layer(l, batches):
        pa = phase_alloc()
        HIDT = [vbf(pa(4096), 2048).rearrange("p (c n) -> p c n", c=4) for _ in range(2)]
        SQ = [vf32(pa(2048), 512) for _ in range(2)]
        NHB = 8
        steps = [(hb, bi) for hb in range(NHB) for bi in range(len(batches))]

        def wviews(hb):
            par = hb % 2
            win = WSL[2 * par].rearrange("p (k n) -> p k n", k=8)
            wout = WSL[2 * par + 1].rearrange("p (c n) -> p c n", c=4)
            return par, win, wout

        def issue_load(hb):
            par, win, wout = wviews(hb)
            load_w(win, mlp_in[l, :, hb * 512:(hb + 1) * 512].rearrange("(k p) n -> p k n", p=128),
                   [("wsl", 2 * par)], "w%d" % (2 * par))
            load_w(wout, mlp_out[l, hb * 512:(hb + 1) * 512, :].rearrange("(c p) n -> p c n", p=128),
                   [("wsl", 2 * par + 1)], "w%d" % (2 * par + 1))

        ctr = {"ps": 0, "out": 0}

        def emit_in(i):
            hb, bi = steps[i]
            par, win, wout = wviews(hb)
            tiles, c, n = batches[bi]
            hbuf = i % 2
            hid = HIDT[hbuf]
            for m in range(4):
                bank = ctr["ps"] % 2
                ctr["ps"] += 1
                P.add("pe", MM([(PS[bank][:, 0:n], win[:, k, m * 128:(m + 1) * 128], HT[:, k, c:c + n], k == 0, k == 7)
                                for k in range(8)]),
                      reads=[("wsl", 2 * par)] + [hkey(t) for t in tiles], writes=[("ps", bank)])
                sq = SQ[bank]
                P.add("act", ACTI(sq[:, 0:n], PS[bank][:, 0:n], AF.Square), reads=[("ps", bank)], writes=[("sq", bank)])
                P.add("dve", STT(hid[:, m, 0:n], PS[bank][:, 0:n], 0.0, sq[:, 0:n], ALU.is_gt, ALU.mult),
                      reads=[("ps", bank), ("sq", bank)], writes=[("hid", hbuf, m)])

        def emit_out(i):
            hb, bi = steps[i]
            par, win, wout = wviews(hb)
            tiles, c, n = batches[bi]
            hbuf = i % 2
            hid = HIDT[hbuf]
            for ti, t in enumerate(tiles):
                pb = 2 + 2 * (ctr["out"] % 2)
                ctr["out"] += 1
                nr = NS if t == TS else 128
                o = 0 if t == TS else ti * 128
                P.add("pe", MM([(PS[pb + half][0:nr, :], hid[:, hc, o:o + nr], wout[:, hc, half * 512:(half + 1) * 512],
                                 hc == 0, hc == 3) for hc in range(4) for half in range(2)]),
                      reads=[("wsl", 2 * par + 1)] + [("hid", hbuf, m) for m in range(4)],
                      writes=[("ps", pb), ("ps", pb + 1)])
                for half in range(2):
                    xs = X[0:nr, t, half * 512:(half + 1) * 512]
                    P.add("dve", TT(xs, xs, PS[pb + half][0:nr, :], ALU.add),
                          reads=[("ps", pb + half), ("x", t)], writes=[("x", t)])

        issue_load(0)
        for i in range(len(steps)):
            hb, bi = steps[i]
            emit_in(i)
            if i >= 1:
                emit_out(i - 1)
            if bi == 0 and hb + 1 < NHB:
                issue_load(hb + 1)
        emit_out(len(steps) - 1)

    def pool_layer(l):
        pa = phase_alloc()
        POOLED = [vbf(pa(8 * 128 * 2), 8 * 128).rearrange("p (k n) -> p k n", k=8) for _ in range(2)]
        AG = vbf(pa(8 * 128 * 2), 8 * 128).rearrange("p (g s n) -> p g s n", g=4, s=2)
        A0 = vbf(pa(8 * 128 * 2), 8 * 128).rearrange("p (g s n) -> p g s n", g=4, s=2)
        STT_ = vf32(pa(D * 4), D)
        SEL = vf32(pa(16 * 4), 16)
        SELH = vf32(pa(16 * 4), 16)
        WP = vbf(pa(8 * 256 * 2), 8 * 256).rearrange("p (g n) -> p g n", g=8)
        WPS = vbf(pa(8 * 256 * 2), 8 * 256).rearrange("p (g n) -> p g n", g=8)
        GTB = vf32(pa(4096), 1024)
        PS_S = vbf(pa(8 * NS * 2), 8 * NS).rearrange("p (k n) -> p k n", k=8)
        HT3 = [vbf(pa(2048), 1024) for _ in range(3)]
        gk = "@pl%d" % l
        P.add("sp", DMA(STT_[0:60, :], state[l]), writes=[("stt",)], dma=gk)
        P.add("sp", DMA(SEL[0:60, :], c_sel), writes=[("sel",)], dma=gk)
        P.add("sp", DMA(SELH[0:NS, :], c_selh), writes=[("selh",)], dma=gk)
        P.add("sp", DMA(TBL, gvec[10 + l].partition_broadcast(128)), writes=[("tbl",)], dma=gk)
        P.add("sp", DMA(GTB, gvec[l].partition_broadcast(128)), writes=[("gtbl",)], dma=gk)
        P.add("pool", DMA(WP, pool_w[l].rearrange("g (kk p) n -> p (g kk) n", p=128)), writes=[("wp",)], dma="wp")
        P.add("pool", DMA(AG, c_apool[0].rearrange("g s p n -> p g s n")), writes=[("ag",)], dma="ag")
        P.add("pool", DMA(A0, c_apool[1].rearrange("g s p n -> p g s n")), writes=[("a0",)], dma="a0")
        P.add("sp", DMA(pool_s[l, :, 0:14, :], state[l].rearrange("(s r) d -> s r d", r=15)[:, 1:15, :]),
              dma="out_psa%d" % l)
        if l == 0:
            load_x(list(range(2, NT)) + [TS], extra_reads=[("wp",), ("ag",), ("a0",), ("tbl",), ("gtbl",)])

        def fold_scale():
            for kk in range(2):
                P.add("dve", TT(WPS.rearrange("p (g k) n -> p g k n", k=2)[:, :, kk, :],
                                WP.rearrange("p (g k) n -> p g k n", k=2)[:, :, kk, :],
                                TBL.rearrange("p (g n) -> p g n", g=4), ALU.mult),
                      reads=[("wp",), ("tbl",)], writes=[("wps", kk)])
        tiles = [TH] + list(range(NT))
        pend = []
        pctr = [0]
        for i, t in enumerate(tiles):
            if i == 1:
                fold_scale()
            buf = i % 3
            htok = HT3[buf]
            hprev = HT3[(i - 1) % 3]
            P.add("act", ACTI(htok, X[:, t, :], AF.Square, accum_out=SS[:, t:t + 1]),
                  reads=[("x", t)], writes=[("htok", buf), ("ss", t)])
            P.add("act", ACTI(RS[:, t:t + 1], SS[:, t:t + 1], AF.Sqrt, bias=EPS, scale=1.0 / D),
                  reads=[("ss", t)], writes=[("rs", t)])
            P.add("dve", RECIP(RS[:, t:t + 1], RS[:, t:t + 1]), reads=[("rs", t)], writes=[("rs", t)])
            P.add("dve", TSC(htok, X[:, t, :], RS[:, t:t + 1], ALU.mult),
                  reads=[("x", t), ("rs", t)], writes=[("htok", buf)])
            if t == NT - 1:
                P.add("dve", STT(YST[0], X[:, t, :], RS[:, t:t + 1], GTB, ALU.mult, ALU.mult),
                      reads=[("x", t), ("rs", t), ("gtbl",)], writes=[("yst", 0)])
                P.add("sp", DMA(pool_p[l], YST[0][113:128, :]), reads=[("yst", 0)], dma="out_pp%d" % l)
            Am = A0 if t == 0 else AG
            pb = 2 + 2 * (i % 2)
            mm = []
            for c in range(8):
                g = c // 2
                o = PS[pb + c // 4][:, (c % 4) * 128:(c % 4) * 128 + 128]
                has_prev = (i > 0)
                mm.append((o, htok[:, c * 128:(c + 1) * 128], Am[:, g, 1, :], True, not has_prev))
                if has_prev:
                    mm.append((o, hprev[:, c * 128:(c + 1) * 128], Am[:, g, 0, :], False, True))
            P.add("pe", MM(mm), reads=[("htok", buf), ("htok", (i - 1) % 3), ("ag",), ("a0",)],
                  writes=[("ps", pb), ("ps", pb + 1)])

            def tail(t=t, pb=pb, i=i):
                pooled = POOLED[i % 2]
                for half in range(2):
                    P.add("dve", TT(pooled[:, 4 * half:4 * half + 4, :],
                                    PS[pb + half][:, :].rearrange("p (k n) -> p k n", k=4),
                                    GCOL[:, l, 4 * half:4 * half + 4].unsqueeze(2).broadcast_to([128, 4, 128]), ALU.mult),
                          reads=[("gcol",)], writes=[("ps", pb + half), ("pooled", i % 2, half)])
                ob = 6 if pctr[0] % 2 == 0 else 0
                pctr[0] += 1
                P.add("pe", MM([(PS[ob + g // 2][:, (g % 2) * 256:(g % 2) * 256 + 256],
                                 pooled[:, 2 * g + kk, :], WPS[:, 2 * g + kk, :], kk == 0, kk == 1)
                                for g in range(4) for kk in range(2)]),
                      reads=[("pooled", i % 2, 0), ("pooled", i % 2, 1), ("wps", 0), ("wps", 1)],
                      writes=[("ps", ob), ("ps", ob + 1)])
                for half in range(2):
                    xs = X[:, t, half * 512:(half + 1) * 512]
                    P.add("dve", TT(xs, xs, PS[ob + half][:, :], ALU.add), reads=[("x", t)],
                          writes=[("ps", ob + half), ("x", t)])
            pend.append(tail)
            if len(pend) > 1:
                pend.pop(0)()
        while pend:
            pend.pop(0)()
        t = TS
        P.add("act", ACTI(HTOK[0], X[:, t, :], AF.Square, accum_out=SS[:, t:t + 1]),
              reads=[("x", t)], writes=[("htok", 0), ("ss", t)])
        P.add("act", ACTI(RS[:, t:t + 1], SS[:, t:t + 1], AF.Sqrt, bias=EPS, scale=1.0 / D),
              reads=[("ss", t)], writes=[("rs", t)])
        P.add("dve", RECIP(RS[:, t:t + 1], RS[:, t:t + 1]), reads=[("rs", t)], writes=[("rs", t)])
        P.add("dve", STT(YST[1], X[:, t, :], RS[:, t:t + 1], GTB, ALU.mult, ALU.mult),
              reads=[("x", t), ("rs", t), ("gtbl",)], writes=[("yst", 1)])
        P.add("sp", DMA(pool_s[l, :, 14, :], YST[1][0:NS, :]), reads=[("yst", 1)], dma="out_psb%d" % l)
        mm = []
        for c in range(8):
            g = c // 2
            o = PS[4][:, c * NS:(c + 1) * NS]
            mm.append((o, STT_[0:60, c * 128:(c + 1) * 128], SEL[0:60, g * NS:(g + 1) * NS], True, False))
            mm.append((o, YST[1][0:NS, c * 128:(c + 1) * 128], SELH[0:NS, g * NS:(g + 1) * NS], False, True))
        P.add("pe", MM(mm), reads=[("stt",), ("sel",), ("selh",), ("yst", 1)], writes=[("ps", 4)])
        P.add("dve", COPY(PS_S, PS[4][:, 0:8 * NS].rearrange("p (k n) -> p k n", k=8)), reads=[],
              writes=[("ps", 4), ("pss",)])
        P.add("pe", MM([(PS[2 + g // 2][0:NS, (g % 2) * 256:(g % 2) * 256 + 256], PS_S[:, 2 * g + kk, :],
                         WPS[:, 2 * g + kk, :], kk == 0, kk == 1) for g in range(4) for kk in range(2)]),
              reads=[("pss",), ("wps", 0), ("wps", 1)], writes=[("ps", 2), ("ps", 3)])
        for half in range(2):
            xs = X[0:NS, TS, half * 512:(half + 1) * 512]
            P.add("dve", TT(xs, xs, PS[2 + half][0:NS, :], ALU.add), reads=[("x", TS)],
                  writes=[("ps", 2 + half), ("x", TS)])

    def mlp_phase(l, with_halo):
        tiles = ([TH] if with_halo else []) + list(range(NT)) + [TS]
        norm_tiles(tiles, 4 + l)
        batches = []
        if with_halo:
            batches.append(([TH], CH, 128))
        for b in range(4):
            batches.append((list(range(4 * b, 4 * b + 4)), C0 + 512 * b, 512))
        batches.append(([TS], CS, NS))
        mlp_layer(l, batches)

    def dump_dbg():
        if dbg is not None:
            for t in range(18):
                P.add("sp", DMA(dbg[t], X[:, t, :]), reads=[("x", t)], dma="dbg")

    def colset(k, g, nb):
        d = BR[g][1]
        if d == 1:
            return HT[:, k, C0 + nb * 512:C0 + (nb + 1) * 512]
        if d == 4:
            return HT[:, k, C0 + nb:C0 + nb + 4 * 511 + 1:4]
        return HT[:, k, C0:C0 + 2048].rearrange("p (u r) -> p r u", r=16)[:, 4 * nb:4 * nb + 4, :]

    def chunkset(k, g, ci):
        d = BR[g][1]
        if d == 1:
            return HT[:, k, C0 + ci * 128:C0 + (ci + 1) * 128]
        if d == 4:
            r, jb = ci // 4, ci % 4
            s0 = C0 + 512 * jb + r
            return HT[:, k, s0:s0 + 4 * 127 + 1:4]
        s0 = C0 + ci
        return HT[:, k, s0:s0 + 16 * 127 + 1:16]

    def halo_chunk0(g):
        return (0, 1, 5)[g]

    GROUPS = [[0, 1], [2, 3], [4, 5], [6, 7]]

    def halo_src(g, h, kv_i):
        if g == 0:
            base = exp_dst0[h * 128:(h + 1) * 128, :]
        elif g == 1:
            base = exp_dst1[h * 128:(h + 1) * 128, :]
        else:
            base = exp_dst2[h // 2][(h % 2) * 128:(h % 2 + 1) * 128, :]
        return base.rearrange("p (c x) -> p c x", x=256)[:, :, kv_i * 128:(kv_i + 1) * 128]

    def kv_phase():
        pa = phase_alloc()
        KTS = [vbf(pa(4096), 2048) for _ in range(2)]
        VST = [vbf(pa(2048), 1024).rearrange("p (h x) -> p h x", h=8) for _ in range(2)]
        norm_tiles(list(range(NT)) + [TS], 8)
        ectr = {"kts": 0, "vst": 0, "ps": 0, "pt": 0, "yst": 0}

        def exp_view(g, h):
            if g == 0:
                t = exp_src0[h * 128:(h + 1) * 128, :]
            elif g == 1:
                t = exp_src1[h * 128:(h + 1) * 128, :]
            else:
                t = exp_src2[h // 2][(h % 2) * 128:(h % 2 + 1) * 128, :]
            return t.rearrange("p (c x) -> p c x", x=256)

        for g in (2, 1, 0):
            d = BR[g][1]
            R = d
            NB = 16 // R
            wK = WALL[:, 0:8192].rearrange("p (k n) -> p k n", k=8)
            wV = WALL[:, 8192:16384].rearrange("p (k n) -> p k n", k=8)
            load_w(wK, w_kv[:, g * 2048:g * 2048 + 1024].rearrange("(k p) n -> p k n", p=128),
                   [("wsl", 0), ("wsl", 1)], "w0")
            load_w(wV, w_kv[:, g * 2048 + 1024:g * 2048 + 2048].rearrange("(k p) n -> p k n", p=128),
                   [("wsl", 2), ("wsl", 3)], "w2")
            for h in range(8):
                kb = ectr["kts"] % 2
                ectr["kts"] += 1
                kts = KTS[kb]
                for nb in range(4):
                    bank = ectr["ps"] % 2
                    ectr["ps"] += 1
                    P.add("pe", MM([(PS[bank][:, :], wK[:, k, h * 128:(h + 1) * 128],
                                     HT[:, k, C0 + nb * 512:C0 + (nb + 1) * 512], k == 0, k == 7) for k in range(8)]),
                          reads=[("wsl", 0), ("wsl", 1)] + [hkey(t) for t in range(4 * nb, 4 * nb + 4)],
                          writes=[("ps", bank)])
                    uw = 512 // d
                    P.add("act", ACTI(kts.rearrange("p (r u) -> p r u", r=d)[:, :, nb * uw:(nb + 1) * uw],
                                      PS[bank][:, :].rearrange("p (u r) -> p r u", r=d), AF.Copy),
                          reads=[], writes=[("ps", bank), ("kts", kb, nb)])
                rk = [("kts", kb, nb) for nb in range(4)]
                P.add("sp", DMA(kT_d[g * 8 + h], kts), reads=rk, writes=[("kT_d", g * 8 + h)], dma="kts%d" % kb)
                src = kts.rearrange("p (r u) -> p r u", r=R)[:, :, (NB - 1) * 128:NB * 128]
                P.add("sp", DMA(exp_view(g, h)[:, :, 0:128], src), reads=rk, writes=[("exps", g, h, "k")],
                      dma="kte%d" % kb)
            for ci in range(16):
                pb = 2 + 2 * (ectr["pt"] % 2)
                ectr["pt"] += 1
                vb = ectr["vst"] % 2
                ectr["vst"] += 1
                P.add("pe", MM([(PS[pb + half][:, :], chunkset(k, g, ci), wV[:, k, half * 512:(half + 1) * 512],
                                 k == 0, k == 7) for half in range(2) for k in range(8)]),
                      reads=[("wsl", 2), ("wsl", 3)] + [hkey(t) for t in range(NT)],
                      writes=[("ps", pb), ("ps", pb + 1)])
                for half in range(2):
                    P.add("dve", COPY(VST[vb][:, 4 * half:4 * half + 4, :],
                                      PS[pb + half][:, :].rearrange("p (h x) -> p h x", h=4)),
                          reads=[], writes=[("ps", pb + half), ("vst", vb, half)])
                rk = [("vst", vb, 0), ("vst", vb, 1)]
                P.add("sp", DMA(v_d[g * 8:(g + 1) * 8].rearrange("h p (c x) -> p h c x", x=128)[:, :, ci, :], VST[vb]),
                      reads=rk, writes=[("v_d", g, ci)], dma="vst%d" % vb)
                r, jb = ci // NB, ci % NB
                if jb == NB - 1:
                    if g < 2:
                        srcv = (exp_src0 if g == 0 else exp_src1).rearrange("(h p) (c x) -> p h c x", p=128, x=256)
                        P.add("sp", DMA(srcv[:, :, r, 128:256], VST[vb]), reads=rk, writes=[("exps", g, "v", r)],
                              dma="vse%d" % vb)
                    else:
                        for i in range(4):
                            srcv = exp_src2[i].rearrange("(h p) (c x) -> p h c x", p=128, x=256)
                            P.add("sp", DMA(srcv[:, :, r, 128:256], VST[vb][:, 2 * i:2 * i + 2, :]), reads=rk,
                                  writes=[("exps", g, "v", r, i)], dma="vse%d_%d" % (vb, i))
            if g < 2:
                rk = [("exps", g, h, "k") for h in range(8)] + [("exps", g, "v", r) for r in range(R)]
                P.add("pool", CC(GROUPS, (exp_src0 if g == 0 else exp_src1).opt(), (exp_dst0 if g == 0 else exp_dst1).opt()),
                      reads=rk, writes=[("expd", g, h) for h in range(8)], dma="cc%d" % g, cc=True)
            else:
                for i in range(4):
                    rk = [("exps", g, h, "k") for h in (2 * i, 2 * i + 1)] + [("exps", g, "v", r, i) for r in range(R)]
                    P.add("pool", CC(GROUPS, exp_src2[i].opt(), exp_dst2[i].opt()),
                          reads=rk, writes=[("expd", g, h) for h in (2 * i, 2 * i + 1)], dma="cc2_%d" % i, cc=True)
            nt_out = (1, 4, 16)[g]
            for t in list(range(NT - nt_out, NT)) + [TS]:
                c, n = tile_cols(t)
                for kv_i, wsel in enumerate((wK, wV)):
                    pb = 2 + 2 * (ectr["pt"] % 2)
                    ectr["pt"] += 1
                    slot = ectr["yst"] % 2
                    ectr["yst"] += 1
                    P.add("pe", MM([(PS[pb + half][0:n, :], HT[:, k, c:c + n], wsel[:, k, half * 512:(half + 1) * 512],
                                     k == 0, k == 7) for half in range(2) for k in range(8)]),
                          reads=[("wsl", 2 * kv_i), ("wsl", 2 * kv_i + 1), hkey(t)],
                          writes=[("ps", pb), ("ps", pb + 1)])
                    for half in range(2):
                        P.add("act", ACTI(YST[slot][0:n, half * 512:(half + 1) * 512], PS[pb + half][0:n, :], AF.Copy),
                              reads=[], writes=[("ps", pb + half), ("yst", slot, half)])
                    if t == TS:
                        dst = kvs[g][:, kv_i * 1024:(kv_i + 1) * 1024]
                        wk = [("kvs", g)]
                    else:
                        r0 = (t - (NT - nt_out)) * 128
                        dst = kvp[g][r0:r0 + 128, kv_i * 1024:(kv_i + 1) * 1024]
                        wk = []
                    P.add("sp", DMA(dst, YST[slot][0:n, :]), reads=[("yst", slot, 0), ("yst", slot, 1)], writes=wk,
                          dma="out_kv%d" % slot)

    def attn_layer(l):
        lb = l - 2
        pa = phase_alloc()
        KTG, VVG = [], []
        for g in range(3):
            R = BR[g][1]
            NB = 16 // R
            n = R * (NB + 1) * 128
            KTG.append(vbf(pa(n * 2), n).rearrange("p (r c x) -> p r c x", r=R, x=128))
            VVG.append(vbf(pa(n * 2), n).rearrange("p (r c x) -> p r c x", r=R, x=128))
        OACC = vf32(pa(2 * 2048 * 4), 2 * 2048).rearrange("p (a n) -> p a n", a=2)
        QT = [YALL[:, 0:1024].bitcast(BF16), YALL[:, 1024:2048].bitcast(BF16)]
        WQ = [WALL[:, 8192 + i * 1024:8192 + (i + 1) * 1024].rearrange("p (k n) -> p k n", k=8) for i in range(2)]
        o3 = 8192 + 2048
        TBS = [WALL[:, o3 + i * 768:o3 + (i + 1) * 768] for i in range(2)]
        o3 += 2 * 768
        PPB = [WALL[:, o3 + i * 1024:o3 + (i + 1) * 1024] for i in range(3)]
        o3 += 3 * 1024
        QTS = WALL[:, o3:o3 + 96].rearrange("p (a s) -> p a s", s=NS)
        o3 += 96
        assert o3 <= 16384
        OT2 = [TBL.bitcast(BF16), TBL.bitcast(BF16)]
        DTMP = [HTOKALL.bitcast(F32)[:, 0:512], HTOKALL.bitcast(F32)[:, 512:1024]]
        WO = WALL[:, 0:8192].rearrange("p (h n) -> p h n", h=8)
        load_w(WO, w_o[lb].rearrange("(h p) n -> p h n", p=128), [("wsl", 0), ("wsl", 1)], "w0")
        ctr = {"unit": 0, "ps": 0, "pp": 0, "ud": 0, "sring": 0}
        items = [(h, g) for h in range(8) for g in range(3)]
        DLAG = 1
        deferred = []

        def pop_deferred(maxlen):
            while len(deferred) > maxlen:
                deferred.pop(0)()

        def issue_loads(idx):
            h, g = items[idx]
            gh = g * 8 + h
            R = BR[g][1]
            NB = 16 // R
            par = idx % 2
            P.add("pool", DMA(WQ[par], w_q[lb][:, g * 1024 + h * 128:g * 1024 + (h + 1) * 128]
                              .rearrange("(k p) n -> p k n", p=128)), writes=[("wq", par)], dma="wq%d" % par)
            P.add("pool", DMA(TBS[par], c_bias3[gh]), writes=[("tbs", par)], dma="tbs%d" % par)
            P.add("sp", DMA(KTG[g][:, :, 1:NB + 1, :], kT_d[gh].rearrange("p (r c x) -> p r c x", r=R, x=128)),
                  reads=[("kT_d", gh)], writes=[("kt", g, "own")], dma="ktko%d" % g)
            P.add("sp", DMA(KTG[g][:, :, 0, :], halo_src(g, h, 0)), reads=[("expd", g, h)],
                  writes=[("kt", g, "h")], dma="ktkh%d" % g)
            P.add("sp", DMA(VVG[g][:, :, 1:NB + 1, :], v_d[gh].rearrange("p (r c x) -> p r c x", r=R, x=128)),
                  reads=[("v_d", g, ci) for ci in range(16)], writes=[("vv", g, "own")], dma="ktvo%d" % g)
            P.add("sp", DMA(VVG[g][:, :, 0, :], halo_src(g, h, 1)), reads=[("expd", g, h)],
                  writes=[("vv", g, "h")], dma="ktvh%d" % g)

        def qproj_part(idx, nb):
            h, g = items[idx]
            gh = g * 8 + h
            d = BR[g][1]
            par = idx % 2
            wq = WQ[par]
            qt = QT[par]
            qt3 = qt.rearrange("p (r u) -> p r u", r=d)
            bank = 0
            if nb < 4:
                P.add("pe", MM([(PS[bank][:, :], wq[:, k, :], HT[:, k, C0 + nb * 512:C0 + (nb + 1) * 512], k == 0, k == 7)
                                for k in range(8)]),
                      reads=[("wq", par)] + [hkey(t) for t in range(4 * nb, 4 * nb + 4)], writes=[("ps", bank)])
                uw = 512 // d
                P.add("act", ACTI(qt3[:, :, nb * uw:(nb + 1) * uw], PS[bank][:, :].rearrange("p (u r) -> p r u", r=d),
                                  AF.Copy, scale=SCALE), reads=[], writes=[("ps", bank), ("qt", par, nb)])
            else:
                P.add("pe", MM([(PS[bank][:, 0:NS], wq[:, k, :], HT[:, k, CS:CS + NS], k == 0, k == 7) for k in range(8)]),
                      reads=[("wq", par), hkey(TS)], writes=[("ps", bank)])
                P.add("act", ACTI(QTS[:, gh, :], PS[bank][:, 0:NS], AF.Copy, scale=SCALE),
                      reads=[], writes=[("ps", bank), ("qts", gh)])

        def qproj(idx):
            for nb in range(5):
                qproj_part(idx, nb)

        def make_pv(g, pv_fn, den_fn, oa, pi):
            def f():
                ui = ctr["ud"] % 2
                ctr["ud"] += 1
                ub, db = 4 + 2 * ui, 5 + 2 * ui
                U, DN = PS[ub], PS[db]
                UDv = PSALL[:, ub * 512:(ub + 2) * 512].rearrange("p (a n) -> p a n", a=2)
                P.add("pe", MMX(pv_fn(U) + den_fn(DN)),
                      reads=[("vv", g, "own"), ("vv", g, "h"), ("pp", pi, 0), ("pp", pi, 1), ("onesb",)],
                      writes=[("ps", ub), ("ps", db)])
                if g == 0:
                    P.add("dve", COPY(oa, UDv), reads=[], writes=[("ps", ub), ("ps", db), ("oacc", "u"), ("oacc", "d")])
                else:
                    uv = UDv if g == 1 else UDv.rearrange("p a (m i) -> p a m i", m=4)
                    P.add("dve", TT(oa, oa, uv, ALU.add), reads=[],
                          writes=[("ps", ub), ("ps", db), ("oacc", "u"), ("oacc", "d")])
            return f

        def make_finish(h):
            def f():
                den = OACC[:, 1, :]
                ot = OT2[h % 2]
                P.add("act", ACTI(den, den, AF.Ln), reads=[], writes=[("oacc", "d")])
                P.add("act", ACTI(den, den, AF.Exp, scale=-1.0), reads=[], writes=[("oacc", "d")])
                P.add("dve", TT(ot, OACC[:, 0, :], den, ALU.mult), reads=[], writes=[("ot", 0), ("oacc", "u"), ("oacc", "d")])
                P.add("sp", DMA(ot_d[h], ot), reads=[("ot", 0)], writes=[("ot_d", h)], dma="otd0")
            return f

        issue_loads(0)
        norm_tiles(list(range(NT)) + [TS], l)
        qproj(0)
        for idx, (h, g) in enumerate(items):
            if idx + 1 < len(items):
                issue_loads(idx + 1)
            d = BR[g][1]
            par = idx % 2
            qt = QT[par]
            tbs = TBS[par]
            K4 = KTG[g]
            V4 = VVG[g]
            tb4h = tbs[:, 0:512]
            tb4 = tbs[:, 256:768]
            nunits = 4 if g < 2 else 4
            for u in range(nunits):
                sa = 1 + ctr["sring"] % 3
                sbb = 1 + (ctr["sring"] + 1) % 3
                ctr["sring"] += 2
                ctr["unit"] += 1
                A, B = PS[sa], PS[sbb]
                qk = []
                pv = []
                if g < 2:
                    if g == 0:
                        r, j0 = 0, 4 * u
                        NBg = 16
                    else:
                        r, j0 = u, 0
                        NBg = 4
                    qc = lambda j, n=1: qt[:, (r * NBg + j) * 128:(r * NBg + j + n) * 128]
                    tba = tb4h if j0 == 0 else tb4
                    tbb = tb4
                    qk = [(A, IDENT, tba, True, False),
                          (A[:, 0:128], K4[:, r, j0, :], qc(j0), False, False),
                          (A[:, 128:384], K4[:, r, j0 + 1, :], qc(j0, 2), False, False),
                          (A[:, 384:512], K4[:, r, j0 + 2, :], qc(j0 + 1), False, True),
                          (B, IDENT, tbb, True, False),
                          (B[:, 0:128], K4[:, r, j0 + 2, :], qc(j0 + 2), False, False),
                          (B[:, 128:384], K4[:, r, j0 + 3, :], qc(j0 + 2, 2), False, False),
                          (B[:, 384:512], K4[:, r, j0 + 4, :], qc(j0 + 3), False, True)]
                    if g == 0:
                        oa = OACC[:, :, 512 * u:512 * (u + 1)]
                    else:
                        oa = OACC[:, :, r:r + 4 * 511 + 1:4]
                else:
                    r0 = 4 * u
                    tba = tb4h
                    tbb = tb4h
                    for m in range(4):
                        bank = A if m < 2 else B
                        c = (m % 2) * 256
                        q = qt[:, (r0 + m) * 128:(r0 + m + 1) * 128]
                        if m % 2 == 0:
                            qk.append((bank, IDENT, tb4h, True, False))
                        qk.append((bank[:, c:c + 128], K4[:, r0 + m, 0, :], q, False, False))
                        qk.append((bank[:, c + 128:c + 256], K4[:, r0 + m, 1, :], q, False, m % 2 == 1))
                    oa = OACC[:, :, r0:r0 + 16 * 127 + 4].rearrange("p a (i m) -> p a m i", m=16)[:, :, 0:4, :] \
                        if False else None
                P.add("pe", MMX(qk), reads=[("kt", g, "own"), ("kt", g, "h"), ("tbs", par), ("ident",)] +
                      [("qt", par, nb) for nb in range(4)], writes=[("ps", sa), ("ps", sbb)])
                pi = ctr["pp"] % 3
                ctr["pp"] += 1
                pp = PPB[pi]
                P.add("act", ACTI(pp[:, 0:512], A, AF.Exp), reads=[], writes=[("ps", sa), ("pp", pi, 0)])
                P.add("act", ACTI(pp[:, 512:1024], B, AF.Exp), reads=[], writes=[("ps", sbb), ("pp", pi, 1)])
                sl = lambda s_, n=1, pp=pp: pp[:, s_ * 128:(s_ + n) * 128]
                if g < 2:
                    def pv_fn(U, r=r, j0=j0, sl=sl, V4=V4):
                        return [(U[:, 0:128], V4[:, r, j0, :], sl(0), True, False),
                                (U[:, 0:256], V4[:, r, j0 + 1, :], sl(1, 2), False, False),
                                (U[:, 128:384], V4[:, r, j0 + 2, :], sl(3, 2), False, False),
                                (U[:, 256:512], V4[:, r, j0 + 3, :], sl(5, 2), False, False),
                                (U[:, 384:512], V4[:, r, j0 + 4, :], sl(7), False, True)]
                else:
                    def pv_fn(U, r0=r0, sl=sl, V4=V4):
                        out = []
                        for m in range(4):
                            out.append((U[:, m * 128:(m + 1) * 128], V4[:, r0 + m, 0, :], sl(2 * m), m == 0, False))
                            out.append((U[:, m * 128:(m + 1) * 128], V4[:, r0 + m, 1, :], sl(2 * m + 1), False, m == 3))
                        return out
                    oa = OACC[:, :, :].rearrange("p a (i r) -> p a r i", r=16)[:, :, r0:r0 + 4, :]
                ppv = pp.rearrange("p (b s x) -> p s b x", s=2, x=128)

                def den_fn(DN, ppv=ppv):
                    return [(DN[:, :], ONESB, ppv[:, 0, :, :], True, False), (DN[:, :], ONESB, ppv[:, 1, :, :], False, True)]
                pv, den = pv_fn, den_fn
                if idx + 1 < len(items):
                    qproj_part(idx + 1, u)
                    if u == 3:
                        qproj_part(idx + 1, 4)
                deferred.append(make_pv(g, pv, den, oa, pi))
                pop_deferred(DLAG)
            if g == 2:
                deferred.append(make_finish(h))
        pop_deferred(0)
        P.barrier()
        OTT = [YALL[:, 0:512].bitcast(BF16).rearrange("p (h c) -> p h c", h=8),
               YALL[:, 512:1024].bitcast(BF16).rearrange("p (h c) -> p h c", h=8)]

        def wo_load(t):
            ob = t % 2
            P.add("sp", DMA(OTT[ob], ot_d[:, :, t * 128:(t + 1) * 128].rearrange("h p c -> p h c")),
                  writes=[("ott", ob)], dma="ott%d" % ob)

        def wo_tile(t):
            ob = t % 2
            pb = 0
            P.add("pe", MM([(PS[pb + half][:, :], OTT[ob][:, hh, :], WO[:, hh, half * 512:(half + 1) * 512], hh == 0, hh == 7)
                            for half in range(2) for hh in range(8)]),
                  reads=[("ott", ob), ("wsl", 0), ("wsl", 1)], writes=[("ps", pb), ("ps", pb + 1)])
            for half in range(2):
                xs = X[:, t, half * 512:(half + 1) * 512]
                P.add("dve", TT(xs, xs, PS[pb + half][:, :], ALU.add), reads=[("x", t)],
                      writes=[("ps", pb + half), ("x", t)])
            if t + 2 < NT:
                wo_load(t + 2)
        pa2 = phase_alloc()
        KC = [vf32(pa2(4096), 1024) for _ in range(2)]
        KB = [vf32(pa2(4096), 1024) for _ in range(2)]
        PROD = [vf32(pa2(4096), 1024) for _ in range(2)]
        VC = [vbf(pa2(2048), 1024) for _ in range(2)]
        VB = [vbf(pa2(2048), 1024) for _ in range(2)]
        BS = vf32(pa2(3 * 8 * 4), 24).rearrange("p (g h) -> p g h", g=3)
        B0 = vf32(pa2(3 * 8 * 4), 24).rearrange("p (g h) -> p g h", g=3)
        SC = [vf32(pa2(32), 8) for _ in range(2)]
        SCB = [vf32(pa2(32), 8) for _ in range(2)]
        PA = [vbf(pa2(64), 8) for _ in range(2)]
        PB = [vbf(pa2(64), 8) for _ in range(2)]
        OSA = vf32(pa2(NS * 16 * 4), NS * 16).rearrange("p (s x) -> p s x", s=NS)
        RD = vf32(pa2(NS * 8 * 4), NS * 8).rearrange("p (s x) -> p s x", s=NS)
        OTS = vbf(pa2(8 * NS * 2), 8 * NS).rearrange("p (h s) -> p h s", h=8)
        P.add("sp", DMA(BS, c_biasS.rearrange("g p h -> p g h")), writes=[("bs",)], dma="@sb%d" % l)
        P.add("sp", DMA(B0[0:1, :, :], c_bias0.rearrange("(o g) h -> o g h", o=1)), writes=[("b0",)], dma="@sb%d" % l)
        sitems = [(s, g) for s in range(NS) for g in range(3)]

        def s_loads(it):
            s, g = sitems[it]
            b = it % 2
            d = BR[g][1]
            P.add("sp", DMA(KC[b], cache[g][s, 0:127 * d + 1:d, 0:1024]), writes=[("kc", b)], dma="sc_kc%d" % b)
            P.add("pool", DMA(VC[b], cache[g][s, 0:127 * d + 1:d, 1024:2048]), writes=[("vc", b)], dma="sc_vc%d" % b)
            P.add("sp", DMA(KB[b][0:1, :], kvs[g][s:s + 1, 0:1024]), reads=[("kvs", g)], writes=[("kb", b)],
                  dma="sc_kb%d" % b)
            P.add("pool", DMA(VB[b][0:1, :], kvs[g][s:s + 1, 1024:2048]), reads=[("kvs", g)], writes=[("vb", b)],
                  dma="sc_vb%d" % b)

        def s_scores(it):
            s, g = sitems[it]
            b = it % 2
            qb0 = 2 if b == 0 else 4
            P.add("pe", MM([(PS[qb0 + hh // 4][:, (hh % 4) * 128:(hh % 4) * 128 + 128],
                             QTS[:, g * 8 + hh, s:s + 1].to_broadcast([128, 128]), IDENT, True, True)
                            for hh in range(8)]),
                  reads=[("ident",)], writes=[("ps", qb0), ("ps", qb0 + 1)])
            for half in range(2):
                P.add("dve", TT(PROD[b][:, half * 512:(half + 1) * 512], KC[b][:, half * 512:(half + 1) * 512],
                                PS[qb0 + half][:, :], ALU.mult), reads=[("kc", b), ("ps", qb0 + half)],
                      writes=[("prod", b, half)])
            P.add("dve", REDUCE(SC[b], PROD[b].rearrange("p (h x) -> p h x", h=8), AX.X, ALU.add),
                  reads=[("prod", b, 0), ("prod", b, 1)], writes=[("sc", b)])
            for half in range(2):
                P.add("dve", TT(PROD[b][0:1, half * 512:(half + 1) * 512], KB[b][0:1, half * 512:(half + 1) * 512],
                                PS[qb0 + half][0:1, :], ALU.mult), reads=[("kb", b), ("sc", b)],
                      writes=[("prod", b, half), ("ps", qb0 + half)])
            P.add("dve", REDUCE(SCB[b][0:1, :], PROD[b][0:1, :].rearrange("p (h x) -> p h x", h=8), AX.X, ALU.add),
                  reads=[("prod", b, 0), ("prod", b, 1)], writes=[("scb", b)])
            P.add("dve", TT(SC[b], SC[b], BS[:, g, :], ALU.add), reads=[("bs",)], writes=[("sc", b)])
            P.add("dve", TT(SCB[b][0:1, :], SCB[b][0:1, :], B0[0:1, g, :], ALU.add), reads=[("b0",)], writes=[("scb", b)])
            P.add("act", ACTI(PA[b], SC[b], AF.Exp), reads=[("sc", b)], writes=[("pa", b)])
            P.add("act", ACTI(PB[b][0:1, :], SCB[b][0:1, :], AF.Exp), reads=[("scb", b)], writes=[("pb", b)])

        def s_pv(it):
            s, g = sitems[it]
            b = it % 2
            pvb = 6 + b
            items2 = []
            for hh in range(8):
                items2.append((PS[pvb][:, hh:hh + 1], VC[b][:, hh * 128:(hh + 1) * 128], PA[b][:, hh:hh + 1], True, False))
                items2.append((PS[pvb][:, hh:hh + 1], VB[b][0:1, hh * 128:(hh + 1) * 128], PB[b][0:1, hh:hh + 1],
                               False, True))
            items2.append((PS[pvb][:, 8:16], ONESB, PA[b], True, False))
            items2.append((PS[pvb][:, 8:16], ONESB[0:1, :], PB[b][0:1, :], False, True))
            P.add("pe", MM(items2), reads=[("vc", b), ("vb", b), ("pa", b), ("pb", b), ("onesb",)], writes=[("ps", pvb)])
            if g == 0:
                P.add("dve", COPY(OSA[:, s, :], PS[pvb][:, 0:16]), reads=[], writes=[("ps", pvb), ("osa", s)])
            else:
                P.add("dve", TT(OSA[:, s, :], OSA[:, s, :], PS[pvb][:, 0:16], ALU.add), reads=[],
                      writes=[("ps", pvb), ("osa", s)])

        wo_load(0)
        wo_load(1)
        s_loads(0)
        s_loads(1)
        s_scores(0)
        wo_next = [0]

        def wo_some(n):
            for _ in range(n):
                if wo_next[0] < NT:
                    wo_tile(wo_next[0])
                    wo_next[0] += 1

        for it in range(len(sitems)):
            wo_some(2 if it % 3 == 0 else 1)
            if it + 1 < len(sitems):
                s_scores(it + 1)
            s_pv(it)
            if it + 2 < len(sitems):
                s_loads(it + 2)
        wo_some(NT)
        P.add("dve", RECIP(RD, OSA[:, :, 8:16]), reads=[("osa", s) for s in range(NS)], writes=[("rd",)])
        P.add("dve", TT(OTS.rearrange("p h s -> p s h"), OSA[:, :, 0:8], RD, ALU.mult),
              reads=[("rd",)] + [("osa", s) for s in range(NS)], writes=[("ots",)])
        P.add("pe", MM([(PS[half][0:NS, :], OTS[:, hh, :], WO[:, hh, half * 512:(half + 1) * 512], hh == 0, hh == 7)
                        for half in range(2) for hh in range(8)]),
              reads=[("ots",), ("wsl", 0), ("wsl", 1)], writes=[("ps", 0), ("ps", 1)])
        for half in range(2):
            xs = X[0:NS, TS, half * 512:(half + 1) * 512]
            P.add("dve", TT(xs, xs, PS[half][0:NS, :], ALU.add), reads=[("x", TS)],
                  writes=[("ps", half), ("x", TS)])

    def final_norm():
        P.add("sp", DMA(TBL, gvec[9].partition_broadcast(128)), writes=[("tbl",)], dma="tblf")
        for t in list(range(NT)) + [TS]:
            slot = t % 2
            P.add("act", ACTI(YST[slot], X[:, t, :], AF.Square, accum_out=SS[:, t:t + 1]),
                  reads=[("x", t)], writes=[("yst", slot), ("ss", t)])
            P.add("act", ACTI(RS[:, t:t + 1], SS[:, t:t + 1], AF.Sqrt, bias=EPS, scale=1.0 / D),
                  reads=[("ss", t)], writes=[("rs", t)])
            P.add("dve", RECIP(RS[:, t:t + 1], RS[:, t:t + 1]), reads=[("rs", t)], writes=[("rs", t)])
            P.add("dve", STT(YST[slot], X[:, t, :], RS[:, t:t + 1], TBL, ALU.mult, ALU.mult),
                  reads=[("x", t), ("rs", t), ("tbl",)], writes=[("yst", slot)])
            if t == TS:
                P.add("sp", DMA(y_s, YST[slot][0:NS, :]), reads=[("yst", slot)], dma="out_y%d" % slot)
            else:
                P.add("sp", DMA(y_p[t], YST[slot]), reads=[("yst", slot)], dma="out_y%d" % slot)

    stages = ["pool0", "mlp0", "pool1", "mlp1", "kv", "attn2", "mlp2", "attn3", "mlp3"]
    last = stages.index(stop_after) if stop_after else len(stages) - 1

    def run_stage(i):
        name = stages[i]
        if name == "pool0":
            pool_layer(0)
        elif name == "mlp0":
            mlp_phase(0, True)
        elif name == "pool1":
            pool_layer(1)
        elif name == "mlp1":
            mlp_phase(1, False)
        elif name == "kv":
            kv_phase()
        elif name == "attn2":
            attn_layer(2)
        elif name == "mlp2":
            mlp_phase(2, False)
        elif name == "attn3":
            attn_layer(3)
        elif name == "mlp3":
            mlp_phase(3, False)

    for i in range(last + 1):
        run_stage(i)
        P.barrier()
    dump_dbg()
    final_norm()
    P.emit(nc, es)
    es.close()
    return nc, P


def make_in_maps(inputs):
    f = lambda a: np.ascontiguousarray(np.asarray(a, dtype=np.float32))
    x_prompt = f(inputs["x_prompt"]); x_sample = f(inputs["x_sample"]); state_pool = f(inputs["state_pool"])
    caches = [f(inputs["cache_kv_w128"]), f(inputs["cache_kv_w512"]), f(inputs["cache_kv_w2048"])]
    rel_bias = f(inputs["rel_bias"])
    gvec = np.concatenate([f(inputs["norm_mix"]), f(inputs["norm_mlp"]), f(inputs["norm_kv"])[None],
                           f(inputs["norm_final"])[None], f(inputs["pool_scale"])], 0)
    shared = {
        "gvec": gvec, "pool_w": f(inputs["pool_w"]), "mlp_in": f(inputs["mlp_in"]), "mlp_out": f(inputs["mlp_out"]),
        "w_kv": f(inputs["w_kv"]), "w_q": f(inputs["w_q"]), "w_o": f(inputs["w_o"]),
        "c_ident": np.eye(128, dtype=np.float32),
    }
    sel = np.zeros((NS, 15, 4, NS), np.float32)
    for g, w in enumerate((2, 4, 8, 16)):
        for s in range(NS):
            sel[s, 15 - (w - 1):, g, s] = 1.0 / w
    shared["c_sel"] = sel.reshape(60, 16)
    selh = np.zeros((NS, 4, NS), np.float32)
    for g, w in enumerate((2, 4, 8, 16)):
        for s_ in range(NS):
            selh[s_, g, s_] = 1.0 / w - 1.0
    shared["c_selh"] = selh.reshape(NS, 16)
    j, valid = toeplitz_index()
    bf = np.zeros((24, 128, 256), np.float32)
    for g, (w, d) in enumerate(BR):
        bk = t5_bucket_np(j * d)
        for h in range(8):
            bf[g * 8 + h] = rel_bias[bk, g * 8 + h]
    nx, sm = bf[:, :, 0:128], bf[:, :, 128:256]
    vn, vs = valid[:, 0:128] > 0, valid[:, 128:256] > 0
    NEG = np.float32(-30000.0)
    nxm = np.where(vn[None], nx, NEG)
    smm = np.where(vs[None], sm, NEG)
    dead = np.full_like(nxm, NEG)
    tabs = []
    for has_halo in (False, True):
        hn = nxm if has_halo else dead
        t01 = np.concatenate([hn, smm, nxm, smm, nxm, smm], 2)
        t2 = np.concatenate([hn, smm, hn, smm, dead, dead], 2)
        tabs.append(np.ascontiguousarray(np.concatenate([t01[0:16], t2[16:24]], 0)))
    bS = np.zeros((3, 128, 8), np.float32)
    b0 = np.zeros((3, 8), np.float32)
    for g, (w, d) in enumerate(BR):
        bk = t5_bucket_np((128 - np.arange(128)) * d)
        bS[g] = rel_bias[bk, g * 8:(g + 1) * 8]
        b0[g] = rel_bias[0, g * 8:(g + 1) * 8]
    shared["c_biasS"] = bS
    shared["c_bias0"] = b0
    in_maps = []
    for c in range(8):
        b, half = c // 2, c % 2
        xin = np.zeros((18, 128, D), np.float32)
        xin[0:NT] = x_prompt[b, half * 2048:(half + 1) * 2048].reshape(NT, 128, D)
        xin[TS, 0:NS] = x_sample[NS * c:NS * (c + 1), 0, :]
        if half == 1:
            xin[TH] = x_prompt[b, 1920:2048]
        ap = np.zeros((2, 4, 2, 128, 128), np.float32)
        tt_src = np.arange(128)[:, None]
        tt_dst = np.arange(128)[None, :]
        for g, w in enumerate((2, 4, 8, 16)):
            for first in range(2):
                if first == 1 and half == 0:
                    cnt = np.minimum(np.arange(128) + 1, w).astype(np.float32)[None, :]
                else:
                    cnt = np.full((1, 128), float(w), np.float32)
                dist_same = tt_dst - tt_src
                dist_prev = tt_dst + 128 - tt_src
                ap[first, g, 1] = ((dist_same >= 0) & (dist_same < w)) / cnt - (dist_same == 0)
                ap[first, g, 0] = ((dist_prev >= 0) & (dist_prev < w)) / cnt
        m = dict(shared)
        m["xin"] = xin
        m["state"] = np.ascontiguousarray(state_pool[:, NS * c:NS * (c + 1)].reshape(2, 60, D))
        for g in range(3):
            m["cache%d" % g] = np.ascontiguousarray(caches[g][NS * c:NS * (c + 1)].reshape(NS, -1, 2048))
        m["c_apool"] = ap
        m["c_bias3"] = tabs[half]
        in_maps.append(m)
    return in_maps


_CACHE = {}


def kernel(**inputs):
    if "nc" not in _CACHE:
        _CACHE["nc"] = build_program()[0]
    nc = _CACHE["nc"]
    in_maps = make_in_maps(inputs)
    res = run_bass_kernel_spmd(nc, in_maps, core_ids=list(range(8)))
    R = res.results
    y_prompt = np.zeros((4, 4096, D), np.float32)
    y_sample = np.zeros((32, 1, D), np.float32)
    pool_p = np.zeros((2, 4, 15, D), np.float32)
    pool_s = np.zeros((2, 32, 15, D), np.float32)
    kvp = [np.zeros((4, w, 2, 8, 128), np.float32) for (w, d) in BR]
    kvs = [np.zeros((32, 1, 2, 8, 128), np.float32) for _ in BR]
    for c in range(8):
        b, half = c // 2, c % 2
        r = R[c]
        y_prompt[b, half * 2048:(half + 1) * 2048] = r["y_p"].reshape(2048, D)
        y_sample[NS * c:NS * (c + 1), 0] = r["y_s"]
        pool_s[:, NS * c:NS * (c + 1)] = r["pool_s"]
        for g in range(3):
            kvs[g][NS * c:NS * (c + 1), 0] = r["kvs%d" % g].reshape(NS, 2, 8, 128)
        if half == 1:
            pool_p[:, b] = r["pool_p"]
            for g, (w, d) in enumerate(BR):
                kvp[g][b] = r["kvp%d" % g].reshape(w, 2, 8, 128)
    return (y_prompt, y_sample, pool_p, pool_s, kvp[0], kvs[0], kvp[1], kvs[1], kvp[2], kvs[2])
```
